# Optimizing a Trainium2 kernel written in Bass

```python
import jax, jax.numpy as jnp
from jax import lax
import numpy as np


D_MODEL = 1024
BATCH = 32
SEQ = 2048
DEPTH = 1
DEC_BATCH = 16
DEC_SEQ = 4096
PAST_LEN = 128

HEAD_DIM = 64
ROPE_DIM = HEAD_DIM // 4
ROPE_THETA = 500000.0
EPS = 1e-6
A_Q_HEADS = 8
A_KV_HEADS = 2
A_GROUP = A_Q_HEADS // A_KV_HEADS
A_WINDOW = 128
B_PATTERNS = ((128, 1), (512, 4), (2048, 16))
B_HEADS_PER_GROUP = 4
B_HEADS = B_HEADS_PER_GROUP * len(B_PATTERNS)
A_Q_W = A_Q_HEADS * HEAD_DIM
A_KV_W = A_KV_HEADS * HEAD_DIM
B_W = B_HEADS * HEAD_DIM
B_OUT_W = B_HEADS_PER_GROUP * HEAD_DIM
IN_COLS = A_Q_W + 2 * A_KV_W + 3 * B_W + 2 * D_MODEL
SPLITS = tuple(int(s) for s in np.cumsum([A_Q_W, A_KV_W, A_KV_W, B_W, B_W, B_W, D_MODEL]))
PEER_HEADS = 8
PEER_KEYS = 128
PEER_EXPERTS = PEER_KEYS * PEER_KEYS
PEER_HALF = 128
PEER_TOPK = 16
PEER_CHUNK = 128

kernel_name = "hybrid_gated_swa_dilated_peer_encoder"


def rms_norm(x, g):
    xf = x.astype(jnp.float32)
    y = xf * lax.rsqrt(jnp.mean(xf * xf, axis=-1, keepdims=True) + EPS)
    return (y * g.astype(jnp.float32)).astype(x.dtype)


def rope_partial(t):
    S = t.shape[1]
    half = ROPE_DIM // 2
    inv = ROPE_THETA ** (-jnp.arange(0, ROPE_DIM, 2, dtype=jnp.float32) / ROPE_DIM)
    ang = jnp.arange(S, dtype=jnp.float32)[:, None] * inv[None, :]
    cos = jnp.cos(ang)[None, :, None, :]
    sin = jnp.sin(ang)[None, :, None, :]
    x1 = t[..., :half].astype(jnp.float32)
    x2 = t[..., half:ROPE_DIM].astype(jnp.float32)
    rot = jnp.concatenate([x1 * cos - x2 * sin, x2 * cos + x1 * sin], axis=-1)
    return jnp.concatenate([rot.astype(t.dtype), t[..., ROPE_DIM:]], axis=-1)


def qk_prep(t, g):
    return rope_partial(rms_norm(t, g))


def banded_attention(q, k, v, w, sink=None):
    B, L, KVH, G, Dh = q.shape
    nb = -(-L // w)
    Lp = nb * w
    qp = jnp.pad(q, ((0, 0), (0, Lp - L), (0, 0), (0, 0), (0, 0)))
    pad_k = ((0, 0), (w, Lp - L + w), (0, 0), (0, 0))
    kp = jnp.pad(k, pad_k)
    vp = jnp.pad(v, pad_k)
    kb = jnp.concatenate([kp[:, i * w:i * w + Lp].reshape(B, nb, w, KVH, Dh) for i in range(3)], axis=2)
    vb = jnp.concatenate([vp[:, i * w:i * w + Lp].reshape(B, nb, w, KVH, Dh) for i in range(3)], axis=2)
    qb = qp.reshape(B, nb, w, KVH, G, Dh)
    s = jnp.einsum('bnqkgd,bnpkd->bnkgqp', qb, kb, preferred_element_type=jnp.float32) * (Dh ** -0.5)
    blk = jnp.arange(nb)[:, None, None]
    qpos = blk * w + jnp.arange(w)[None, :, None]
    kpos = (blk - 1) * w + jnp.arange(3 * w)[None, None, :]
    mask = (jnp.abs(kpos - qpos) <= w) & (kpos >= 0) & (kpos < L)
    s = jnp.where(mask[None, :, None, None, :, :], s, -1e30)
    m = jnp.max(s, axis=-1)
    if sink is not None:
        sk = sink.astype(jnp.float32).reshape(KVH, G)[None, None, :, :, None]
        m = jnp.maximum(m, sk)
    p = jnp.exp(s - m[..., None])
    denom = jnp.sum(p, axis=-1)
    if sink is not None:
        denom = denom + jnp.exp(sk - m)
    o = jnp.einsum('bnkgqp,bnpkd->bnkgqd', p, vb.astype(jnp.float32)) / denom[..., None]
    lse = m + jnp.log(denom)
    o = o.transpose(0, 1, 4, 2, 3, 5).reshape(B, Lp, KVH, G, Dh)[:, :L]
    lse = lse.transpose(0, 1, 4, 2, 3).reshape(B, Lp, KVH, G)[:, :L]
    return o, lse


def dilated_attention(q, k, v, window, dilation):
    B, S, H, Dh = q.shape
    L = S // dilation
    hw = window // (2 * dilation)

    def to_sub(t):
        return t.reshape(B, L, dilation, H, Dh).transpose(0, 2, 1, 3, 4).reshape(B * dilation, L, H, Dh)

    o, lse = banded_attention(to_sub(q)[:, :, :, None, :], to_sub(k), to_sub(v), hw)
    o = o[:, :, :, 0].reshape(B, dilation, L, H, Dh).transpose(0, 2, 1, 3, 4).reshape(B, S, H, Dh)
    lse = lse[..., 0].reshape(B, dilation, L, H).transpose(0, 2, 1, 3).reshape(B, S, H)
    return o, lse


def peer(h, w_query, sub_keys_1, sub_keys_2, expert_u, expert_v):
    B, S, D = h.shape
    t = h.reshape(B * S, D)
    T = t.shape[0]
    q = (t @ w_query).reshape(T, PEER_HEADS, 2, PEER_HALF)
    s1 = jnp.einsum('thc,hkc->thk', q[:, :, 0], sub_keys_1, preferred_element_type=jnp.float32)
    s2 = jnp.einsum('thc,hkc->thk', q[:, :, 1], sub_keys_2, preferred_element_type=jnp.float32)
    v1, i1 = lax.top_k(s1, PEER_TOPK)
    v2, i2 = lax.top_k(s2, PEER_TOPK)
    cand = (v1[..., :, None] + v2[..., None, :]).reshape(T, PEER_HEADS, PEER_TOPK * PEER_TOPK)
    top, ci = lax.top_k(cand, PEER_TOPK)
    e1 = jnp.take_along_axis(i1, ci // PEER_TOPK, axis=-1)
    e2 = jnp.take_along_axis(i2, ci % PEER_TOPK, axis=-1)
    idx = (e1 * PEER_KEYS + e2).reshape(T, PEER_HEADS * PEER_TOPK)
    g = jax.nn.softmax(top, axis=-1).reshape(T, PEER_HEADS * PEER_TOPK)
    nc = T // PEER_CHUNK

    def chunk(args):
        tc, ic, gc = args
        u = expert_u[ic]
        a = jax.nn.gelu(jnp.einsum('ted,td->te', u, tc, preferred_element_type=jnp.float32), approximate=False)
        vv = expert_v[ic]
        return jnp.einsum('te,ted->td', (gc * a).astype(tc.dtype), vv)

    out = lax.map(chunk, (t.reshape(nc, PEER_CHUNK, D),
                          idx.reshape(nc, PEER_CHUNK, -1),
                          g.reshape(nc, PEER_CHUNK, -1)))
    return out.reshape(B, S, D).astype(h.dtype)


def encoder_layer(x, norm1, w_in, q_norm_a, k_norm_a, sink_a, q_norm_b, k_norm_b,
                  w_o_a, w_o_b, w_out, norm2, w_query, sub_keys_1, sub_keys_2, expert_u, expert_v):
    B, S, _ = x.shape
    h = rms_norm(x, norm1)
    proj = h @ w_in
    qa, ka, va, qb, kb, vb, ga, gb = jnp.split(proj, SPLITS, axis=-1)
    qa = qk_prep(qa.reshape(B, S, A_Q_HEADS, HEAD_DIM), q_norm_a).reshape(B, S, A_KV_HEADS, A_GROUP, HEAD_DIM)
    ka = qk_prep(ka.reshape(B, S, A_KV_HEADS, HEAD_DIM), k_norm_a)
    va = va.reshape(B, S, A_KV_HEADS, HEAD_DIM)
    oa, _ = banded_attention(qa, ka, va, A_WINDOW, sink_a)
    ya = oa.reshape(B, S, A_Q_W).astype(x.dtype) @ w_o_a
    qb = qk_prep(qb.reshape(B, S, B_HEADS, HEAD_DIM), q_norm_b)
    kb = qk_prep(kb.reshape(B, S, B_HEADS, HEAD_DIM), k_norm_b)
    vb = vb.reshape(B, S, B_HEADS, HEAD_DIM)
    outs, lses = [], []
    for gi, (win, dil) in enumerate(B_PATTERNS):
        hs = slice(gi * B_HEADS_PER_GROUP, (gi + 1) * B_HEADS_PER_GROUP)
        o, l = dilated_attention(qb[:, :, hs], kb[:, :, hs], vb[:, :, hs], win, dil)
        outs.append(o)
        lses.append(l)
    wts = jax.nn.softmax(jnp.stack(lses, axis=0), axis=0)
    ob = jnp.sum(wts[..., None] * jnp.stack(outs, axis=0), axis=0)
    yb = ob.reshape(B, S, B_OUT_W).astype(x.dtype) @ w_o_b
    merged = jax.nn.sigmoid(ga) * ya + jax.nn.sigmoid(gb) * yb
    x = x + merged @ w_out
    x = x + peer(rms_norm(x, norm2), w_query, sub_keys_1, sub_keys_2, expert_u, expert_v)
    return x


def run_trunk(x, norm1, w_in, q_norm_a, k_norm_a, sink_a, q_norm_b, k_norm_b,
              w_o_a, w_o_b, w_out, norm2, w_query, sub_keys_1, sub_keys_2, expert_u, expert_v):
    for l in range(DEPTH):
        x = encoder_layer(x, norm1[l], w_in[l], q_norm_a[l], k_norm_a[l], sink_a[l], q_norm_b[l], k_norm_b[l],
                          w_o_a[l], w_o_b[l], w_out[l], norm2[l], w_query[l], sub_keys_1[l], sub_keys_2[l],
                          expert_u[l], expert_v[l])
    return x


def setup_inputs(seed: int = 0) -> dict:
    key = jax.random.key(seed)
    ks = jax.random.split(key, 20)
    f32 = jnp.float32
    nrm = lambda k, shape, scale: jax.random.normal(k, shape, f32) * scale
    gain = lambda k, shape: 1.0 + 0.02 * jax.random.normal(k, shape, f32)
    return {
        "x_prompt": nrm(ks[0], (BATCH, SEQ, D_MODEL), 1.0),
        "x_sample": nrm(ks[1], (DEC_BATCH, DEC_SEQ, D_MODEL), 1.0),
        "norm1": gain(ks[2], (DEPTH, D_MODEL)),
        "w_in": nrm(ks[3], (DEPTH, D_MODEL, IN_COLS), D_MODEL ** -0.5),
        "q_norm_a": gain(ks[4], (DEPTH, HEAD_DIM)),
        "k_norm_a": gain(ks[5], (DEPTH, HEAD_DIM)),
        "sink_a": nrm(ks[6], (DEPTH, A_Q_HEADS), 0.5),
        "q_norm_b": gain(ks[7], (DEPTH, HEAD_DIM)),
        "k_norm_b": gain(ks[8], (DEPTH, HEAD_DIM)),
        "w_o_a": nrm(ks[9], (DEPTH, A_Q_W, D_MODEL), A_Q_W ** -0.5),
        "w_o_b": nrm(ks[10], (DEPTH, B_OUT_W, D_MODEL), B_OUT_W ** -0.5),
        "w_out": nrm(ks[11], (DEPTH, D_MODEL, D_MODEL), D_MODEL ** -0.5),
        "norm2": gain(ks[12], (DEPTH, D_MODEL)),
        "w_query": nrm(ks[13], (DEPTH, D_MODEL, PEER_HEADS * 2 * PEER_HALF), D_MODEL ** -0.5),
        "sub_keys_1": nrm(ks[14], (DEPTH, PEER_HEADS, PEER_KEYS, PEER_HALF), PEER_HALF ** -0.5),
        "sub_keys_2": nrm(ks[15], (DEPTH, PEER_HEADS, PEER_KEYS, PEER_HALF), PEER_HALF ** -0.5),
        "expert_u": nrm(ks[16], (DEPTH, PEER_EXPERTS, D_MODEL), D_MODEL ** -0.5),
        "expert_v": nrm(ks[17], (DEPTH, PEER_EXPERTS, D_MODEL), (PEER_HEADS * PEER_TOPK) ** -0.5),
    }


def reference(x_prompt, x_sample, norm1, w_in, q_norm_a, k_norm_a, sink_a, q_norm_b, k_norm_b,
              w_o_a, w_o_b, w_out, norm2, w_query, sub_keys_1, sub_keys_2, expert_u, expert_v):
    y_prompt = run_trunk(x_prompt, norm1, w_in, q_norm_a, k_norm_a, sink_a, q_norm_b, k_norm_b,
                         w_o_a, w_o_b, w_out, norm2, w_query, sub_keys_1, sub_keys_2, expert_u, expert_v)
    y_sample = run_trunk(x_sample, norm1, w_in, q_norm_a, k_norm_a, sink_a, q_norm_b, k_norm_b,
                         w_o_a, w_o_b, w_out, norm2, w_query, sub_keys_1, sub_keys_2, expert_u, expert_v)
    return (y_prompt, y_sample)
```

```python
import numpy as np
import ml_dtypes
from contextlib import ExitStack
import concourse.bass as bass
import concourse.mybir as mybir
from concourse.bass_utils import run_bass_kernel_spmd

F32 = mybir.dt.float32
BF16 = mybir.dt.bfloat16
U32 = mybir.dt.uint32
U8 = mybir.dt.uint8
AF = mybir.ActivationFunctionType
ALU = mybir.AluOpType
AX = mybir.AxisListType
ESZ = {F32: 4, BF16: 2, U32: 4, U8: 1}

D_MODEL = 1024
EPS = 1e-6
NCORES = 8
FULL_SEQS = [2048] * 4 + [4096] * 2
GROUPS = [
    dict(name="A0", qcol=0, kcol=512, vcol=640, HK=1, w=128, D=1),
    dict(name="A1", qcol=256, kcol=576, vcol=704, HK=1, w=128, D=1),
    dict(name="B0", qcol=768, kcol=1536, vcol=2304, HK=4, w=64, D=1),
    dict(name="B1", qcol=1024, kcol=1792, vcol=2560, HK=4, w=64, D=4),
    dict(name="B2", qcol=1280, kcol=2048, vcol=2816, HK=4, w=64, D=16),
]
DBG_G = 2
NCH = 128
PT = 256
FILL_PER_CHUNK = 4
GB = 8
UVD = 4


class Buf:
    __slots__ = ("name", "lw", "rd", "rd_dma", "sem", "cum", "const")

    def __init__(self, name, const=False):
        self.name = name
        self.lw = None
        self.rd = {}
        self.rd_dma = []
        self.sem = None
        self.cum = 0
        self.const = const


class Op:
    __slots__ = ("eng", "fn", "deps", "sig", "sem", "ticket", "dma")

    def __init__(self, eng, fn, dma=None):
        self.eng = eng
        self.fn = fn
        self.deps = []
        self.sig = False
        self.sem = None
        self.ticket = 0
        self.dma = dma


class Prog:
    ENGS = ("pe", "act", "dve", "pool", "sp")

    def __init__(self, nc, stack):
        self.nc = nc
        self.stack = stack
        self.ops = {e: [] for e in self.ENGS}
        self.esem = {e: stack.enter_context(nc.semaphore("sem_" + e)) for e in self.ENGS}
        self.tokens = []
        self.nsem = len(self.ENGS)
        self.pending = {}
        self.fence_tok = Buf("fence")

    def _dep(self, op, prod, kind):
        if prod is None or prod is op:
            return
        if prod.dma is None and op.dma is None and prod.eng == op.eng and kind != "raw" and op.eng != "pool":
            return
        op.deps.append(prod)
        prod.sig = True

    def _track(self, op, reads, writes):
        for b in reads:
            self._dep(op, b.lw, "raw")
        for b in writes:
            self._dep(op, b.lw, "waw")
            for r in b.rd.values():
                self._dep(op, r, "war")
            for r in b.rd_dma:
                self._dep(op, r, "war")
        for b in writes:
            b.lw = op
            b.rd = {}
            b.rd_dma = []
        for b in reads:
            if b.const or b in writes:
                continue
            if op.dma is not None:
                b.rd_dma.append(op)
            else:
                b.rd[op.eng] = op

    def add(self, eng, fn, r=(), w=()):
        op = Op(eng, fn)
        p = self.pending.pop(eng, None)
        if p is not None:
            op.deps.append(p)
        self._track(op, r, w)
        self.ops[eng].append(op)
        return op

    def fence(self, pairs):
        lasts = [self.ops[e][-1] for e in self.ENGS if self.ops[e]]
        toks = [t.lw for t in self.tokens if t.lw is not None]
        op = self.dma("sp", pairs, self.fence_tok)
        for l in lasts + toks:
            if l is op:
                continue
            if l.dma is None:
                l.sig = True
            op.deps.append(l)
        self.pending = {e: op for e in self.ENGS}
        return op

    def dma(self, eng, pairs, token, r=(), w=()):
        op = Op(eng, None, dma=pairs)
        p = self.pending.pop(eng, None)
        if p is not None:
            op.deps.append(p)
        if token.sem is None:
            token.sem = self.stack.enter_context(self.nc.semaphore("dsem_" + token.name))
            self.nsem += 1
            self.tokens.append(token)
        if token not in w:
            w = tuple(w) + (token,)
        self._track(op, r, w)
        token.cum += 16 * len(pairs)
        op.sem = token.sem
        op.ticket = token.cum
        self.ops[eng].append(op)
        return op

    def emit(self):
        nc = self.nc
        for e in self.ENGS:
            n = 0
            for op in self.ops[e]:
                if op.dma is None and op.sig:
                    n += 1
                    op.sem = self.esem[e]
                    op.ticket = n
        handles = {"pe": nc.tensor, "act": nc.scalar, "dve": nc.vector, "pool": nc.gpsimd, "sp": nc.sync}

        def run(e, eng):
            waited = {}
            for op in self.ops[e]:
                for p in op.deps:
                    key = id(p.sem)
                    if waited.get(key, 0) >= p.ticket:
                        continue
                    waited[key] = p.ticket
                    eng.wait_ge(p.sem, p.ticket)
                if op.dma is not None:
                    for (o, i) in op.dma:
                        eng.dma_start(out=o, in_=i).then_inc(op.sem, 16)
                else:
                    ins = op.fn(eng)
                    if op.sig:
                        ins.then_inc(op.sem, 1)
            if e == "sp":
                for t in self.tokens:
                    if waited.get(id(t.sem), 0) < t.cum:
                        eng.wait_ge(t.sem, t.cum)

        with nc.Block() as block:
            @block.tensor
            def _(t):
                run("pe", t)

            @block.scalar
            def _(s):
                run("act", s)

            @block.vector
            def _(v):
                run("dve", v)

            @block.gpsimd
            def _(g):
                run("pool", g)

            @block.sync
            def _(sy):
                run("sp", sy)


def I(method, *args, **kw):
    return lambda e: getattr(e, method)(*args, **kw)


class Arena:
    def __init__(self, t, nbytes):
        self.t = t
        self.n = nbytes
        self.top = 0

    def alloc(self, dtype, shape):
        nfree = int(np.prod(shape[1:]))
        nb = nfree * ESZ[dtype]
        off = self.top
        self.top += (nb + 63) // 64 * 64
        assert self.top <= self.n, f"SBUF arena overflow {self.top} > {self.n}"
        ap = self.t[0:shape[0], off:off + nb].bitcast(dtype)
        if len(shape) == 3:
            ap = ap.rearrange("p (a b) -> p a b", b=shape[2])
        elif len(shape) == 4:
            ap = ap.rearrange("p (a b c) -> p a b c", b=shape[2], c=shape[3])
        return ap


class Ring:
    def __init__(self, arena, name, n, dtype, shape):
        self.n = n
        self.v = [arena.alloc(dtype, shape) for _ in range(n)]
        self.b = [Buf(f"{name}{i}") for i in range(n)]
        self.i = 0

    def next(self):
        k = self.i % self.n
        self.i += 1
        return self.v[k], self.b[k]


def bcast_rows(dram_ap_2d, ncols, nparts=128, col0=0):
    return bass.AP(dram_ap_2d.tensor, dram_ap_2d.offset + col0, [[0, nparts], [1, ncols]])


def build_program(seqs, stop_after=4, debug=False):
    TOK = sum(seqs)
    assert TOK % 512 == 0
    nc = bass.Bass("TRN2", target_bir_lowering=False)
    stack = ExitStack()
    with stack:
        def din(name, shape, dt=F32):
            return nc.dram_tensor(name, list(shape), dt, kind="ExternalInput").ap()

        okind = "ExternalOutput" if debug else "Internal"

        def dscr(name, shape, dt):
            return nc.dram_tensor(name, list(shape), dt, kind=okind).ap()

        x_d = din("x", [TOK, 1024])
        norm1_d = din("norm1", [1, 1024])
        w_in_d = din("w_in", [1024, 5120])
        qna_d = din("q_norm_a", [1, 64])
        kna_d = din("k_norm_a", [1, 64])
        sink_d = din("sink_a", [1, 8])
        qnb_d = din("q_norm_b", [1, 64])
        knb_d = din("k_norm_b", [1, 64])
        woa_d = din("w_o_a", [512, 1024])
        wob_d = din("w_o_b", [256, 1024])
        wout_d = din("w_out", [1024, 1024])
        norm2_d = din("norm2", [1, 1024])
        wq_d = din("w_query", [1024, 2048])
        sk1_d = din("sub_keys_1", [8, 128, 128])
        sk2_d = din("sub_keys_2", [8, 128, 128])
        eu_d = din("expert_u", [16384, 1024])
        ev_d = din("expert_v", [16384, 1024])
        rope_d = din("c_rope", [4096, 16])
        mask_d = din("c_mask", [2, 128, 384], BF16)
        identb_d = din("c_identb", [128, 128], BF16)
        identf_d = din("c_identf", [128, 128])
        iota_d = din("c_iota", [128, 128])
        y_d = nc.dram_tensor("y", [TOK, 1024], F32, kind="ExternalOutput").ap()

        XNT = dscr("s_xnt", [8, 128, TOK], BF16)
        NUM = dscr("s_num", [5, TOK, 260], F32)
        X2 = dscr("s_x2", [TOK, 1024], F32)
        XN2T = dscr("s_xn2t", [8, 128, TOK], BF16)
        UV = dscr("s_uv", [NCH, 128, 2048], BF16)
        SC = dscr("s_sc", [TOK, 2048], F32)

        SB_BYTES = 205 * 1024
        arena_t = stack.enter_context(nc.sbuf_tensor("arena", [128, SB_BYTES], U8))
        psb = [stack.enter_context(nc.psum_tensor(f"psb{i}", [128, 512], F32)) for i in range(8)]
        PSB = [Buf(f"psb{i}") for i in range(8)]
        P = Prog(nc, stack)
        A = Arena(arena_t, SB_BYTES)

        def ps_f32(i):
            return psb[i][:, :]

        def ps_bf16(i):
            return psb[i][:, :].bitcast(BF16)

        st_tok = [Buf(f"st{i}") for i in range(8)]
        st_i = [0]

        def store(eng, pairs, r):
            t = st_tok[st_i[0] % len(st_tok)]
            st_i[0] += 1
            return P.dma(eng, pairs, t, r=r)

        dbg_n = [0]

        def dbg(name, ap, b):
            if not debug:
                return
            shp = list(ap.shape)
            dt_ = ap.dtype
            d = nc.dram_tensor("dbg_" + name, shp, dt_, kind="ExternalOutput").ap()
            store("sp", [(d, ap)], r=[b])

        identb = A.alloc(BF16, [128, 128]); identb_b = Buf("identb", const=True)
        identf = A.alloc(F32, [128, 128]); identf_b = Buf("identf", const=True)
        P.dma("sp", [(identb, identb_d)], identb_b, w=[identb_b])
        P.dma("sp", [(identf, identf_d)], identf_b, w=[identf_b])
        epsc = A.alloc(F32, [128, 4]); epsc_b = Buf("epsc", const=True)
        P.add("pool", I("memset", epsc, EPS), w=[epsc_b])
        pers_top = A.top

        FZ = dscr("s_fence", [2, 16], F32)

        def fence():
            P.fence([(FZ[1:2, :], identf_d[0:1, 0:16])])
            A.top = pers_top

        def rms_tile(xt, xt_b, gb, gb_b, xn, xn_b, junk, junk_b, ss, ss_b):
            P.add("act", I("activation", out=junk, in_=xt, func=AF.Square, accum_out=ss[:, 0:1]),
                  r=[xt_b], w=[junk_b, ss_b])
            P.add("act", I("activation", out=ss[:, 1:2], in_=ss[:, 0:1], func=AF.Ln, scale=1.0 / 1024, bias=epsc[:, 0:1]),
                  r=[ss_b, epsc_b], w=[ss_b])
            P.add("act", I("activation", out=ss[:, 2:3], in_=ss[:, 1:2], func=AF.Exp, scale=-0.5), r=[ss_b], w=[ss_b])
            P.add("dve", I("scalar_tensor_tensor", out=xn, in0=xt, scalar=ss[:, 2:3], in1=gb,
                                                          op0=ALU.mult, op1=ALU.mult), r=[xt_b, ss_b, gb_b], w=[xn_b])

        def transpose8(src, src_b, dst_fn, dst_b, bank, evac_eng):
            pv = ps_bf16(bank)
            for do in range(8):
                P.add("pe", I("transpose", out=pv[:, do * 128:(do + 1) * 128],
                                                         in_=src[:, do * 128:(do + 1) * 128], identity=identb),
                      r=[src_b, identb_b], w=[PSB[bank]])
            pv3 = pv[:, 0:1024].rearrange("p (a b) -> p a b", b=128)
            if evac_eng == "act":
                P.add("act", I("copy", out=dst_fn, in_=pv3), r=[PSB[bank]], w=[dst_b])
            else:
                P.add("dve", I("tensor_copy", out=dst_fn, in_=pv3), r=[PSB[bank]], w=[dst_b])

        def uv_prepass_chunk(c, uvr, ubr):
            eu3 = eu_d.rearrange("(a b) d -> a b d", b=128)
            ev3 = ev_d.rearrange("(a b) d -> a b d", b=128)
            ub, ub_b = ubr.next()
            P.dma("pool", [(ub, eu3[:, c, :])], ub_b, w=[ub_b])
            uv, uv_b = uvr.next()
            P.dma("pool", [(uv[:, 1024:2048], ev3[:, c, :])], uv_b, w=[uv_b])
            bank = 6 + c % 2
            pv = ps_bf16(bank)
            for do in range(8):
                P.add("pe", I("transpose", out=pv[:, do * 128:(do + 1) * 128], in_=ub[:, do * 128:(do + 1) * 128],
                               identity=identb), r=[ub_b, identb_b], w=[PSB[bank]])
            P.add("dve", I("tensor_copy", out=uv[:, 0:1024], in_=pv[:, 0:1024]), r=[PSB[bank], uv_b], w=[uv_b])
            store("sp", [(UV[c], uv)], r=[uv_b])

        def phase1():
            uvr1 = Ring(A, "uv1", 3, BF16, [128, 2048])
            ubr1 = Ring(A, "ub1", 2, BF16, [128, 1024])
            uvc = [0]
            g1 = A.alloc(F32, [128, 1024]); g1_b = Buf("g1", const=True)
            P.dma("sp", [(g1, bcast_rows(norm1_d, 1024))], g1_b, w=[g1_b])
            xr = Ring(A, "p1x", 3, F32, [128, 1024])
            xnr = Ring(A, "p1xn", 2, BF16, [128, 1024])
            jr = Ring(A, "p1j", 1, BF16, [128, 1024])
            ssr = Ring(A, "p1ss", 4, F32, [128, 4])
            gr = Ring(A, "p1g", 2, BF16, [128, 8, 512])
            nt = TOK // 128
            loads = {}

            def load(i):
                xt, xt_b = xr.next()
                P.dma("sp", [(xt, x_d[i * 128:(i + 1) * 128, :])], xt_b, w=[xt_b])
                loads[i] = (xt, xt_b)

            load(0)
            if nt > 1:
                load(1)
            grp = None
            for i in range(nt):
                if i + 2 < nt:
                    load(i + 2)
                xt, xt_b = loads.pop(i)
                xn, xn_b = xnr.next()
                junk, junk_b = jr.next()
                ss, ss_b = ssr.next()
                rms_tile(xt, xt_b, g1, g1_b, xn, xn_b, junk, junk_b, ss, ss_b)
                j = i % 4
                if j == 0:
                    grp = gr.next()
                transpose8(xn, xn_b, grp[0][:, :, j * 128:(j + 1) * 128], grp[1], i % 2, "act")
                if j == 3:
                    t0 = (i - 3) * 128
                    store("sp", [(XNT[:, :, t0:t0 + 512].rearrange("o i t -> i o t"), grp[0])], r=[grp[1]])
                while uvc[0] < NCH and uvc[0] * nt < (i + 1) * NCH:
                    uv_prepass_chunk(uvc[0], uvr1, ubr1)
                    uvc[0] += 1

        def qk_norm_rope(ps_ap, ps_b, H, g_ap, g_b, tab, tab_b, out_bf, out_b, work):
            sq, sq_b, st, st_b, xn, xn_b, tmp, tmp_b = work
            H64 = H * 64
            P.add("act", I("activation", out=sq[:, 0:H64], in_=ps_ap, func=AF.Square), r=[ps_b], w=[sq_b])
            yield
            sq3 = sq[:, 0:H64].rearrange("p (h d) -> p h d", d=64)
            P.add("dve", I("tensor_reduce", out=st[:, 0:H], in_=sq3, axis=AX.X, op=ALU.add), r=[sq_b], w=[st_b])
            yield
            P.add("act", I("activation", out=st[:, 4:4 + H], in_=st[:, 0:H], func=AF.Ln, scale=1.0 / 64, bias=epsc[:, 0:1]),
                  r=[st_b, epsc_b], w=[st_b])
            yield
            P.add("act", I("activation", out=st[:, 8:8 + H], in_=st[:, 4:4 + H], func=AF.Exp, scale=-0.5), r=[st_b], w=[st_b])
            yield
            ps3 = ps_ap.rearrange("p (h d) -> p h d", d=64)
            xn3 = xn[:, 0:H64].rearrange("p (h d) -> p h d", d=64)
            rs_bc = st[:, 8:8 + H].unsqueeze(2).to_broadcast([128, H, 64])
            P.add("dve", I("tensor_tensor", out=xn3, in0=ps3, in1=rs_bc, op=ALU.mult), r=[ps_b, st_b], w=[xn_b])
            yield
            g_bc = g_ap.unsqueeze(1).to_broadcast([128, H, 64])
            P.add("dve", I("tensor_tensor", out=xn3, in0=xn3, in1=g_bc, op=ALU.mult), r=[xn_b, g_b], w=[xn_b])
            yield
            P.add("act", I("copy", out=out_bf, in_=xn3), r=[xn_b], w=[out_b])
            yield
            cos_bc = tab[:, 0:8].unsqueeze(1).to_broadcast([128, H, 8])
            sin_bc = tab[:, 8:16].unsqueeze(1).to_broadcast([128, H, 8])
            x1 = xn3[:, :, 0:8]
            x2 = xn3[:, :, 8:16]
            t3 = tmp[:, 0:4 * H * 8].rearrange("p (k h d) -> p k h d", k=4, d=8)
            P.add("dve", I("tensor_tensor", out=t3[:, 0], in0=x1, in1=cos_bc, op=ALU.mult), r=[xn_b, tab_b], w=[tmp_b])
            yield
            P.add("dve", I("tensor_tensor", out=t3[:, 1], in0=x2, in1=sin_bc, op=ALU.mult), r=[xn_b, tab_b], w=[tmp_b])
            yield
            P.add("pool", I("tensor_tensor", out=t3[:, 2], in0=x2, in1=cos_bc, op=ALU.mult), r=[xn_b, tab_b], w=[tmp_b])
            yield
            P.add("pool", I("tensor_tensor", out=t3[:, 3], in0=x1, in1=sin_bc, op=ALU.mult), r=[xn_b, tab_b], w=[tmp_b])
            yield
            P.add("dve", I("tensor_tensor", out=out_bf[:, :, 0:8], in0=t3[:, 0], in1=t3[:, 1], op=ALU.subtract),
                  r=[tmp_b, out_b], w=[out_b])
            yield
            P.add("dve", I("tensor_tensor", out=out_bf[:, :, 8:16], in0=t3[:, 2], in1=t3[:, 3], op=ALU.add),
                  r=[tmp_b, out_b], w=[out_b])
            yield

        def phase2():
            wqkv = A.alloc(BF16, [128, 8, 3072]); wqkv_b = Buf("wqkv", const=True)
            w3 = w_in_d.rearrange("(o i) c -> i o c", i=128)
            P.dma("pool", [(wqkv[:, do, :], w3[:, do, 0:3072]) for do in range(8)], wqkv_b, w=[wqkv_b])
            gains = A.alloc(F32, [128, 4, 64]); gains_b = Buf("gains", const=True)
            P.dma("sp", [(gains[:, 0, :], bcast_rows(qna_d, 64)), (gains[:, 1, :], bcast_rows(kna_d, 64)),
                         (gains[:, 2, :], bcast_rows(qnb_d, 64)), (gains[:, 3, :], bcast_rows(knb_d, 64))],
                  gains_b, w=[gains_b])
            P.add("act", I("mul", out=gains[:, 0, :], in_=gains[:, 0, :], mul=0.125), r=[gains_b], w=[gains_b])
            P.add("act", I("mul", out=gains[:, 2, :], in_=gains[:, 2, :], mul=0.125), r=[gains_b], w=[gains_b])
            masks = A.alloc(BF16, [128, 2, 384]); masks_b = Buf("masks", const=True)
            P.dma("sp", [(masks[:, 0, :], mask_d[0]), (masks[:, 1, :], mask_d[1])], masks_b, w=[masks_b])
            SMAX = max(seqs)
            xnT = A.alloc(BF16, [128, 8, SMAX]); xnT_b = Buf("xnT")
            NTMAX = SMAX // 128
            KT = A.alloc(BF16, [128, 2, SMAX])
            KT_b = [Buf(f"KT{a}") for a in range(NTMAX)]
            VA = A.alloc(BF16, [128, NTMAX, 4, 65])
            VA_b = [Buf(f"VA{a}") for a in range(NTMAX)]
            VA1_b = Buf("VAones")
            P.add("pool", I("memset", VA[:, :, :, 64:65], 1.0), w=[VA1_b] + VA_b)
            tabr = Ring(A, "tab", 6, F32, [128, 16])
            sqr = Ring(A, "sq", 4, F32, [128, 256])
            str_ = Ring(A, "st", 4, F32, [128, 12])
            xnr = Ring(A, "xnq", 4, F32, [128, 256])
            tmpr = Ring(A, "tmpq", 4, F32, [128, 128])
            kbr = Ring(A, "kb", 4, BF16, [128, 4, 64])
            qbr = Ring(A, "qb", 4, BF16, [128, 4, 64])
            QTr = Ring(A, "QT", 4, BF16, [128, 2, 128])
            ptr = Ring(A, "pt", 6, BF16, [128, 384])
            numr = Ring(A, "num", 4, F32, [128, 260])
            SBANK = [dict(proj=0, tr=2, sc=3, num=5), dict(proj=1, tr=7, sc=4, num=6)]

            def work():
                sq, sq_b = sqr.next(); st, st_b = str_.next(); xn, xn_b = xnr.next(); tmp, tmp_b = tmpr.next()
                return (sq, sq_b, st, st_b, xn, xn_b, tmp, tmp_b)

            def yield_each(n0):
                return sum(len(v) for v in P.ops.values())

            def load_tab(r, D, a):
                tab, tab_b = tabr.next()
                st0 = r + D * 128 * a
                P.dma("sp", [(tab, rope_d[st0:st0 + D * 127 + 1:D, :])], tab_b, w=[tab_b])
                return tab, tab_b

            def kv_task(g, r, a, slot, gk, sidx):
                D, HK = g["D"], g["HK"]
                isA = HK == 1
                bk = SBANK[sidx]
                tab, tab_b = load_tab(r, D, a)
                bank = bk["proj"]
                kv = ps_f32(bank)
                nk = HK * 64
                st0 = r + D * 128 * a
                ts = slice(st0, st0 + D * 127 + 1, D)
                for do in range(8):
                    P.add("pe", I("matmul", kv[:, 0:nk], xnT[:, do, ts], wqkv[:, do, g["kcol"]:g["kcol"] + nk],
                                   start=(do == 0), stop=(do == 7)), r=[xnT_b, wqkv_b], w=[PSB[bank]])
                yield
                for do in range(8):
                    P.add("pe", I("matmul", kv[:, 256:256 + nk], xnT[:, do, ts], wqkv[:, do, g["vcol"]:g["vcol"] + nk],
                                   start=(do == 0), stop=(do == 7)), r=[xnT_b, wqkv_b], w=[PSB[bank]])
                yield
                kb, kb_b = kbr.next()
                yield from qk_norm_rope(kv[:, 0:nk], PSB[bank], HK, gk, gains_b, tab, tab_b, kb[:, 0:HK, :], kb_b, work())
                if isA:
                    P.add("dve", I("tensor_copy", out=kb[:, 1, :], in_=kb[:, 0, :]), r=[kb_b], w=[kb_b])
                P.add("act", I("copy", out=VA[:, slot, 0:HK, 0:64], in_=kv[:, 256:256 + nk].rearrange("p (h d) -> p h d", d=64)),
                      r=[PSB[bank]], w=[VA_b[slot]])
                yield
                npair = 1 if isA else 2
                tb_ = bk["tr"]
                pv = ps_bf16(tb_)
                for p_ in range(npair):
                    P.add("pe", I("transpose", out=pv[:, p_ * 128:(p_ + 1) * 128],
                                   in_=kb[:, 2 * p_:2 * p_ + 2, :].rearrange("p h d -> p (h d)"), identity=identb),
                          r=[kb_b, identb_b], w=[PSB[tb_]])
                yield
                P.add("dve", I("tensor_copy", out=KT[:, 0:npair, slot * 128:(slot + 1) * 128],
                               in_=pv[:, 0:npair * 128].rearrange("p (a b) -> p a b", b=128)), r=[PSB[tb_]], w=[KT_b[slot]])
                yield

            def q_task(g, gi, s0, r, a, nt, slots, gq, mk, sidx):
                D, HK = g["D"], g["HK"]
                isA = HK == 1
                bk = SBANK[sidx]
                tab, tab_b = load_tab(r, D, a)
                bank = bk["proj"]
                qp = ps_f32(bank)
                st0 = r + D * 128 * a
                ts = slice(st0, st0 + D * 127 + 1, D)
                for do in range(8):
                    P.add("pe", I("matmul", qp[:, 0:256], xnT[:, do, ts], wqkv[:, do, g["qcol"]:g["qcol"] + 256],
                                   start=(do == 0), stop=(do == 7)), r=[xnT_b, wqkv_b], w=[PSB[bank]])
                yield
                qb, qb_b = qbr.next()
                yield from qk_norm_rope(qp[:, 0:256], PSB[bank], 4, gq, gains_b, tab, tab_b, qb, qb_b, work())
                tb_ = bk["tr"]
                pv = ps_bf16(tb_)
                for p_ in range(2):
                    P.add("pe", I("transpose", out=pv[:, p_ * 128:(p_ + 1) * 128],
                                   in_=qb[:, 2 * p_:2 * p_ + 2, :].rearrange("p h d -> p (h d)"), identity=identb),
                          r=[qb_b, identb_b], w=[PSB[tb_]])
                yield
                QT, QT_b = QTr.next()
                P.add("dve", I("tensor_copy", out=QT, in_=pv[:, 0:256].rearrange("p (a b) -> p a b", b=128)), r=[PSB[tb_]], w=[QT_b])
                yield
                blocks = [b for b in (a - 1, a, a + 1) if 0 <= b < nt]
                nb = len(blocks)
                moff = (blocks[0] - (a - 1)) * 128
                nbank = bk["num"]
                nps = ps_f32(nbank)
                sbank = bk["sc"]
                sps = ps_f32(sbank)
                for h in range(4):
                    kh = 0 if isA else h
                    bp = (h % 2) * 64
                    pair = h // 2
                    kpair = 0 if isA else kh // 2
                    for bi, b in enumerate(blocks):
                        sl_ = slots[b]
                        P.add("pe", I("matmul", sps[:, bi * 128:(bi + 1) * 128], KT[bp:bp + 64, kpair, sl_ * 128:(sl_ + 1) * 128],
                                       QT[bp:bp + 64, pair, :], start=True, stop=True), r=[KT_b[sl_], QT_b], w=[PSB[sbank]])
                    yield
                    pt, pt_b = ptr.next()
                    P.add("act", I("activation", out=pt[:, 0:nb * 128], in_=sps[:, 0:nb * 128], func=AF.Exp), r=[PSB[sbank]], w=[pt_b])
                    yield
                    P.add("dve", I("tensor_tensor", out=pt[:, 0:nb * 128], in0=pt[:, 0:nb * 128], in1=mk[:, moff:moff + nb * 128],
                                   op=ALU.mult), r=[pt_b, masks_b], w=[pt_b])
                    yield
                    for bi, b in enumerate(blocks):
                        sl_ = slots[b]
                        P.add("pe", I("matmul", nps[:, h * 65:(h + 1) * 65], pt[:, bi * 128:(bi + 1) * 128], VA[:, sl_, kh, :],
                                       start=(bi == 0), stop=(bi == nb - 1)), r=[pt_b, VA_b[sl_], VA1_b], w=[PSB[nbank]])
                    yield
                num, num_b = numr.next()
                P.add("act", I("copy", out=num, in_=nps[:, 0:260]), r=[PSB[nbank]], w=[num_b])
                yield
                st1 = s0 + r + D * 128 * a
                store("sp", [(NUM[gi, st1:st1 + D * 127 + 1:D, :], num)], r=[num_b])
                yield

            def load_task(S, s0, sidx):
                P.dma("sp", [(xnT[:, do, 0:S], XNT[do, :, s0:s0 + S]) for do in range(8)], xnT_b, w=[xnT_b])
                yield

            tasks = []
            s0 = 0
            base = 0
            for S in seqs:
                tasks.append(lambda sidx, S=S, s0=s0: load_task(S, s0, sidx))
                for gi, g in enumerate(GROUPS):
                    D, HK = g["D"], g["HK"]
                    nt = S // D // 128
                    isA = HK == 1
                    gq = gains[:, 0 if isA else 2, :]
                    gk = gains[:, 1 if isA else 3, :]
                    mk = masks[:, 0 if isA else 1, :]
                    for r in range(D):
                        slots = [(base + a) % NTMAX for a in range(nt)]
                        base += nt
                        for a in range(nt):
                            tasks.append(lambda sidx, g=g, r=r, a=a, sl=slots[a], gk=gk: kv_task(g, r, a, sl, gk, sidx))
                        for a in range(nt):
                            tasks.append(lambda sidx, g=g, gi=gi, s0=s0, r=r, a=a, nt=nt, slots=slots, gq=gq, mk=mk:
                                         q_task(g, gi, s0, r, a, nt, slots, gq, mk, sidx))
                s0 += S
            active = {}
            it = iter(tasks)
            done = False
            while True:
                while not done and len(active) < 2:
                    t = next(it, None)
                    if t is None:
                        done = True
                        break
                    sidx = 0 if 0 not in active else 1
                    active[sidx] = t(sidx)
                if not active:
                    break
                for sidx in list(active.keys()):
                    try:
                        next(active[sidx])
                    except StopIteration:
                        del active[sidx]

        def phase3():
            def wload(name, d_ap, nchunk, c0, ncol):
                t = A.alloc(BF16, [128, nchunk, ncol]); b = Buf(name, const=True)
                v = d_ap.rearrange("(o i) c -> i o c", i=128)
                P.dma("pool", [(t[:, o, :], v[:, o, c0:c0 + ncol]) for o in range(nchunk)], b, w=[b])
                return t, b
            woa, woa_b = wload("woa", woa_d, 4, 0, 1024)
            wob, wob_b = wload("wob", wob_d, 2, 0, 1024)
            wg, wg_b = wload("wg", w_in_d, 8, 3072, 2048)
            wout, wout_b = wload("wout", wout_d, 8, 0, 1024)
            wq, wq_b = wload("wq", wq_d, 8, 0, 2048)
            g2 = A.alloc(F32, [128, 1024]); g2_b = Buf("g2", const=True)
            P.dma("sp", [(g2, bcast_rows(norm2_d, 1024))], g2_b, w=[g2_b])
            esink = A.alloc(F32, [128, 8]); esink_b = Buf("esink", const=True)
            P.dma("sp", [(esink, bcast_rows(sink_d, 8))], esink_b, w=[esink_b])
            P.add("act", I("activation", out=esink, in_=esink, func=AF.Exp), r=[esink_b], w=[esink_b])
            n5r = Ring(A, "n5", 2, F32, [128, 5, 260])
            tBr = Ring(A, "tB", 2, F32, [128, 260])
            rAr = Ring(A, "rA", 2, F32, [128, 16])
            Or = Ring(A, "O", 2, BF16, [128, 768])
            oTr = Ring(A, "oT", 1, BF16, [128, 6, 512])
            xTr = Ring(A, "xT3", 1, BF16, [128, 8, 512])
            sgr = Ring(A, "sg", 2, F32, [128, 512])
            mr = Ring(A, "m12", 2, F32, [128, 512])
            mTr = Ring(A, "mT", 1, BF16, [128, 8, 512])
            xr = Ring(A, "x3", 1, F32, [128, 1024])
            x2r = Ring(A, "x23", 2, F32, [128, 1024])
            xn2r = Ring(A, "xn23", 1, BF16, [128, 1024])
            skT = A.alloc(F32, [128, 16, 128]); skT_b = Buf("skT", const=True)
            qTs = A.alloc(F32, [128, 16, 128]); qTs_b = Buf("qTs")
            scs = A.alloc(F32, [128, 16, 128]); scs_b = Buf("scs")
            P.dma("sp", [(qTs[:, 2 * h, :], sk1_d[h]) for h in range(8)] + [(qTs[:, 2 * h + 1, :], sk2_d[h]) for h in range(8)],
                  qTs_b, w=[qTs_b])
            for rnd in range(4):
                bank = 4 + rnd % 2
                for j in range(4):
                    g_ = rnd * 4 + j
                    P.add("pe", I("transpose", out=ps_f32(bank)[:, j * 128:(j + 1) * 128], in_=qTs[:, g_, :], identity=identf),
                          r=[qTs_b, identf_b], w=[PSB[bank]])
                P.add("act", I("copy", out=skT[:, rnd * 4:(rnd + 1) * 4, :],
                                in_=ps_f32(bank).rearrange("p (a b) -> p a b", b=128)), r=[PSB[bank]], w=[skT_b])
            ev3 = [0]

            def evac3(out, in_, r, w):
                ev3[0] += 1
                if ev3[0] % 2:
                    P.add("act", I("copy", out=out, in_=in_), r=r, w=w)
                else:
                    P.add("dve", I("tensor_copy", out=out, in_=in_), r=r, w=w)
            jr = Ring(A, "j3", 1, BF16, [128, 1024])
            ssr = Ring(A, "ss3", 4, F32, [128, 4])
            gr = Ring(A, "g3", 2, BF16, [128, 8, 512])
            for T0 in range(0, TOK, 512):
                oT, oT_b = oTr.next()
                for j in range(4):
                    tok = T0 + 128 * j
                    n5, n5_b = n5r.next()
                    P.dma("sp", [(n5[:, g_, :], NUM[g_, tok:tok + 128, :]) for g_ in range(5)], n5_b, w=[n5_b])
                    tB, tB_b = tBr.next()
                    rA, rA_b = rAr.next()
                    O, O_b = Or.next()
                    P.add("dve", I("tensor_tensor", out=tB, in0=n5[:, 2, :], in1=n5[:, 3, :], op=ALU.add), r=[n5_b], w=[tB_b])
                    P.add("dve", I("tensor_tensor", out=tB, in0=tB, in1=n5[:, 4, :], op=ALU.add), r=[n5_b, tB_b], w=[tB_b])
                    A8 = n5[:, 0:2, :].rearrange("p g (h e) -> p (g h) e", e=65)
                    B4 = tB.rearrange("p (h e) -> p h e", e=65)
                    P.add("dve", I("tensor_tensor", out=rA[:, 0:8], in0=A8[:, :, 64], in1=esink, op=ALU.add),
                          r=[n5_b, esink_b], w=[rA_b])
                    P.add("dve", I("tensor_copy", out=rA[:, 8:12], in_=B4[:, :, 64]), r=[tB_b], w=[rA_b])
                    P.add("dve", I("reciprocal", out=rA[:, 0:12], in_=rA[:, 0:12]), r=[rA_b], w=[rA_b])
                    P.add("dve", I("tensor_tensor", out=O[:, 0:512].rearrange("p (h d) -> p h d", d=64), in0=A8[:, :, 0:64],
                                   in1=rA[:, 0:8].unsqueeze(2).to_broadcast([128, 8, 64]), op=ALU.mult), r=[n5_b, rA_b], w=[O_b])
                    P.add("dve", I("tensor_tensor", out=O[:, 512:768].rearrange("p (h d) -> p h d", d=64), in0=B4[:, :, 0:64],
                                   in1=rA[:, 8:12].unsqueeze(2).to_broadcast([128, 4, 64]), op=ALU.mult), r=[tB_b, rA_b, O_b], w=[O_b])
                    if debug and T0 == 0 and j == 0:
                        dbg("O", O, O_b)
                    bank = 0
                    pv = ps_bf16(bank)
                    for fc in range(6):
                        P.add("pe", I("transpose", out=pv[:, fc * 128:(fc + 1) * 128], in_=O[:, fc * 128:(fc + 1) * 128],
                                       identity=identb), r=[O_b, identb_b], w=[PSB[bank]])
                    P.add("act", I("copy", out=oT[:, :, j * 128:(j + 1) * 128],
                                    in_=pv[:, 0:768].rearrange("p (a b) -> p a b", b=128)), r=[PSB[bank]], w=[oT_b])
                xT, xT_b = xTr.next()
                P.dma("sp", [(xT[:, do, :], XNT[do, :, T0:T0 + 512]) for do in range(8)], xT_b, w=[xT_b])
                mT, mT_b = mTr.next()
                for dc in range(8):
                    bs = (dc % 2) * 4
                    dcs = slice(dc * 128, (dc + 1) * 128)
                    ya, yb, ga, gb = ps_f32(bs), ps_f32(bs + 1), ps_f32(bs + 2), ps_f32(bs + 3)
                    for fc in range(4):
                        P.add("pe", I("matmul", ya, woa[:, fc, dcs], oT[:, fc, :], start=(fc == 0), stop=(fc == 3)),
                              r=[woa_b, oT_b], w=[PSB[bs]])
                    for fc in range(2):
                        P.add("pe", I("matmul", yb, wob[:, fc, dcs], oT[:, 4 + fc, :], start=(fc == 0), stop=(fc == 1)),
                              r=[wob_b, oT_b], w=[PSB[bs + 1]])
                    for do in range(8):
                        P.add("pe", I("matmul", ga, wg[:, do, dcs], xT[:, do, :], start=(do == 0), stop=(do == 7)),
                              r=[wg_b, xT_b], w=[PSB[bs + 2]])
                    for do in range(8):
                        P.add("pe", I("matmul", gb, wg[:, do, 1024 + dc * 128:1024 + (dc + 1) * 128], xT[:, do, :],
                                       start=(do == 0), stop=(do == 7)), r=[wg_b, xT_b], w=[PSB[bs + 3]])
                    sga, sga_b = sgr.next()
                    sgb, sgb_b = sgr.next()
                    P.add("act", I("activation", out=sga, in_=ga, func=AF.Sigmoid), r=[PSB[bs + 2]], w=[sga_b])
                    P.add("act", I("activation", out=sgb, in_=gb, func=AF.Sigmoid), r=[PSB[bs + 3]], w=[sgb_b])
                    m1, m1_b = mr.next()
                    m2, m2_b = mr.next()
                    P.add("dve", I("tensor_tensor", out=m1, in0=ya, in1=sga, op=ALU.mult), r=[PSB[bs], sga_b], w=[m1_b])
                    P.add("dve", I("tensor_tensor", out=m2, in0=yb, in1=sgb, op=ALU.mult), r=[PSB[bs + 1], sgb_b], w=[m2_b])
                    P.add("pool", I("tensor_tensor", out=mT[:, dc, :], in0=m1, in1=m2, op=ALU.add), r=[m1_b, m2_b], w=[mT_b])
                grp = gr.next()
                for j in range(4):
                    tok = T0 + 128 * j
                    xt, xt_b = xr.next()
                    P.dma("sp", [(xt, x_d[tok:tok + 128, :])], xt_b, w=[xt_b])
                    x2t, x2t_b = x2r.next()
                    for hd in range(2):
                        bank = 1 + hd
                        ops_ = ps_f32(bank)
                        for dc in range(8):
                            P.add("pe", I("matmul", ops_, mT[:, dc, j * 128:(j + 1) * 128], wout[:, dc, hd * 512:(hd + 1) * 512],
                                           start=(dc == 0), stop=(dc == 7)), r=[mT_b, wout_b], w=[PSB[bank]])
                        P.add("dve", I("tensor_tensor", out=x2t[:, hd * 512:(hd + 1) * 512], in0=ops_,
                                       in1=xt[:, hd * 512:(hd + 1) * 512], op=ALU.add), r=[PSB[bank], xt_b], w=[x2t_b])
                    store("sp", [(X2[tok:tok + 128, :], x2t)], r=[x2t_b])
                    xn2, xn2_b = xn2r.next()
                    junk, junk_b = jr.next()
                    ss, ss_b = ssr.next()
                    rms_tile(x2t, x2t_b, g2, g2_b, xn2, xn2_b, junk, junk_b, ss, ss_b)
                    transpose8(xn2, xn2_b, grp[0][:, :, j * 128:(j + 1) * 128], grp[1], 3, "act")
                    for qc in range(16):
                        bank = 4 + qc % 2
                        qps = ps_f32(bank)[:, 0:128]
                        for do in range(8):
                            P.add("pe", I("matmul", qps, wq[:, do, qc * 128:(qc + 1) * 128], grp[0][:, do, j * 128:(j + 1) * 128],
                                           start=(do == 0), stop=(do == 7)), r=[wq_b, grp[1]], w=[PSB[bank]])
                        evac3(qTs[:, qc, :], qps, [PSB[bank]], [qTs_b])
                    for rnd in range(4):
                        bank = 6 + rnd % 2
                        sps = ps_f32(bank)
                        for jj in range(4):
                            g_ = rnd * 4 + jj
                            P.add("pe", I("matmul", sps[:, jj * 128:(jj + 1) * 128], qTs[:, g_, :], skT[:, g_, :], start=True, stop=True),
                                  r=[qTs_b, skT_b], w=[PSB[bank]])
                        evac3(scs[:, rnd * 4:(rnd + 1) * 4, :], sps.rearrange("p (a b) -> p a b", b=128), [PSB[bank]], [scs_b])
                    store("sp", [(SC[tok:tok + 128, :], scs.rearrange("p a b -> p (a b)"))], r=[scs_b])
                store("sp", [(XN2T[:, :, T0:T0 + 512].rearrange("o i t -> i o t"), grp[0])], r=[grp[1]])


        def phase4():
            PSBH = [Buf(f"psbh{i}") for i in range(4)]
            iota = A.alloc(F32, [128, 128]); iota_b = Buf("iota", const=True)
            P.dma("sp", [(iota, iota_d)], iota_b, w=[iota_b])
            Gr = Ring(A, "G", 2, BF16, [128, PT, 128])
            uvr = Ring(A, "uv", UVD, BF16, [128, 2048])
            xqr = Ring(A, "xq", 2, BF16, [128, 8, PT])
            sc = A.alloc(F32, [128, 16, 128]); sc_b = Buf("sc")
            sc2 = A.alloc(F32, [128, 16, 128]); sc2_b = Buf("sc2")
            cand = sc.rearrange("p a b -> p (a b)").rearrange("p (h c) -> p h c", c=256)
            cand2 = sc2.rearrange("p a b -> p (a b)").rearrange("p (h c) -> p h c", c=256)
            oh = sc.rearrange("p a b -> p (a b)").rearrange("p (h s i) -> p h s i", s=16, i=16)
            cand4 = oh
            vt = A.alloc(F32, [128, 16, 16]); vt_b = Buf("vt")
            ix = A.alloc(U32, [128, 16, 16]); ix_b = Buf("ix")
            ixf = A.alloc(F32, [128, 16, 16]); ixf_b = Buf("ixf")
            top = A.alloc(F32, [128, 8, 16]); top_b = Buf("top")
            ci = A.alloc(U32, [128, 8, 16]); ci_b = Buf("ci")
            hl = A.alloc(U32, [128, 2, 128]); hl_b = Buf("hl")
            hlf = A.alloc(F32, [128, 2, 128]); hlf_b = Buf("hlf")
            ex = A.alloc(F32, [128, 8, 16]); ex_b = Buf("ex")
            sm = A.alloc(F32, [128, 8]); sm_b = Buf("sm")
            e12w = A.alloc(F32, [128, 3, 128]); e12w_b = Buf("e12w")
            eTr = Ring(A, "eT", 2, F32, [128, 3, PT])
            A1r = Ring(A, "A1", 3, BF16, [128, GB, 128])
            B1r = Ring(A, "B1", 3, BF16, [128, GB, 128])
            gelr = Ring(A, "gel", 3, BF16, [128, PT])
            atr = Ring(A, "at", 3, BF16, [128, PT])
            x2r = Ring(A, "x24", 1, F32, [128, 1024])
            ev_i = [0]

            def evac(out, in_, r, w):
                ev_i[0] += 1
                if ev_i[0] % 2:
                    P.add("act", I("copy", out=out, in_=in_), r=r, w=w)
                else:
                    P.add("dve", I("tensor_copy", out=out, in_=in_), r=r, w=w)

            def prep(T0, u, xq, xq_b, eT, eT_b):
                us = slice(u * 128, (u + 1) * 128)
                tok = T0 + u * 128
                P.dma("sp", [(sc.rearrange("p a b -> p (a b)"), SC[tok:tok + 128, :])], sc_b, w=[sc_b])
                yield
                for g_ in range(16):
                    P.add("dve", I("max", out=vt[:, g_, 0:8], in_=sc[:, g_, :]), r=[sc_b], w=[vt_b])
                    P.add("dve", I("max_index", out=ix[:, g_, 0:8], in_max=vt[:, g_, 0:8], in_values=sc[:, g_, :]),
                          r=[sc_b, vt_b], w=[ix_b])
                    P.add("dve", I("match_replace", out=sc2[:, g_, :], in_to_replace=vt[:, g_, 0:8], in_values=sc[:, g_, :],
                                   imm_value=-1e30), r=[sc_b, vt_b], w=[sc2_b])
                    P.add("dve", I("max", out=vt[:, g_, 8:16], in_=sc2[:, g_, :]), r=[sc2_b], w=[vt_b])
                    P.add("dve", I("max_index", out=ix[:, g_, 8:16], in_max=vt[:, g_, 8:16], in_values=sc2[:, g_, :]),
                          r=[sc2_b, vt_b], w=[ix_b])
                    yield
                v4 = vt.rearrange("p (h two) s -> p h two s", two=2)
                P.add("dve", I("tensor_tensor", out=cand4, in0=v4[:, :, 0, :].unsqueeze(3).to_broadcast([128, 8, 16, 16]),
                               in1=v4[:, :, 1, :].unsqueeze(2).to_broadcast([128, 8, 16, 16]), op=ALU.add), r=[vt_b], w=[sc_b])
                for h in range(8):
                    P.add("dve", I("max", out=top[:, h, 0:8], in_=cand[:, h, :]), r=[sc_b], w=[top_b])
                    P.add("dve", I("max_index", out=ci[:, h, 0:8], in_max=top[:, h, 0:8], in_values=cand[:, h, :]),
                          r=[sc_b, top_b], w=[ci_b])
                    P.add("dve", I("match_replace", out=cand2[:, h, :], in_to_replace=top[:, h, 0:8], in_values=cand[:, h, :],
                                   imm_value=-1e30), r=[sc_b, top_b], w=[sc2_b])
                    P.add("dve", I("max", out=top[:, h, 8:16], in_=cand2[:, h, :]), r=[sc2_b], w=[top_b])
                    P.add("dve", I("max_index", out=ci[:, h, 8:16], in_max=top[:, h, 8:16], in_values=cand2[:, h, :]),
                          r=[sc2_b, top_b], w=[ci_b])
                    yield
                P.add("dve", I("tensor_tensor", out=ex, in0=top, in1=top[:, :, 0:1].to_broadcast([128, 8, 16]), op=ALU.subtract),
                      r=[top_b], w=[ex_b])
                P.add("act", I("activation", out=ex, in_=ex, func=AF.Exp), r=[ex_b], w=[ex_b])
                P.add("dve", I("tensor_reduce", out=sm, in_=ex, axis=AX.X, op=ALU.add), r=[ex_b], w=[sm_b])
                P.add("dve", I("reciprocal", out=sm, in_=sm), r=[sm_b], w=[sm_b])
                P.add("dve", I("tensor_tensor", out=e12w[:, 2, :].rearrange("p (h s) -> p h s", s=16), in0=ex,
                               in1=sm.unsqueeze(2).to_broadcast([128, 8, 16]), op=ALU.mult), r=[ex_b, sm_b], w=[e12w_b])
                cif = ci.rearrange("p h s -> p (h s)")
                P.add("dve", I("tensor_single_scalar", out=hl[:, 0, :], in_=cif, scalar=4, op=ALU.logical_shift_right),
                      r=[ci_b], w=[hl_b])
                P.add("dve", I("tensor_single_scalar", out=hl[:, 1, :], in_=cif, scalar=15, op=ALU.bitwise_and),
                      r=[ci_b], w=[hl_b])
                P.add("dve", I("tensor_copy", out=hlf, in_=hl), r=[hl_b], w=[hlf_b])
                P.add("dve", I("tensor_copy", out=ixf, in_=ix), r=[ix_b], w=[ixf_b])
                yield
                ixf4 = ixf.rearrange("p (h two) s -> p h two s", two=2)
                io16 = iota[:, 0:16].unsqueeze(1).unsqueeze(1).to_broadcast([128, 8, 16, 16])
                for k in range(2):
                    sel = hlf[:, k, :].rearrange("p (h s) -> p h s", s=16).unsqueeze(3).to_broadcast([128, 8, 16, 16])
                    P.add("dve", I("tensor_tensor", out=oh, in0=sel, in1=io16, op=ALU.is_equal), r=[hlf_b, iota_b], w=[sc_b])
                    P.add("dve", I("tensor_tensor", out=oh, in0=oh, in1=ixf4[:, :, k, :].unsqueeze(2).to_broadcast([128, 8, 16, 16]),
                                   op=ALU.mult), r=[sc_b, ixf_b], w=[sc_b])
                    P.add("dve", I("tensor_reduce", out=e12w[:, k, :].rearrange("p (h s) -> p h s", s=16), in_=oh, axis=AX.X,
                                   op=ALU.add), r=[sc_b], w=[e12w_b])
                    yield
                if debug and T0 == 0 and u == 0:
                    dbg("e12w", e12w, e12w_b)
                    dbg("top", top, top_b)
                bank = 7
                for k in range(3):
                    P.add("pe", I("transpose", out=ps_f32(bank)[:, k * 128:(k + 1) * 128], in_=e12w[:, k, :], identity=identf),
                          r=[e12w_b, identf_b], w=[PSB[bank]])
                P.add("act", I("copy", out=eT[:, :, us], in_=ps_f32(bank)[:, 0:384].rearrange("p (a b) -> p a b", b=128)),
                      r=[PSB[bank]], w=[eT_b])
                yield

            gb_i = [0]
            iota_bf = A.alloc(BF16, [128, 128]); iota_bf_b = Buf("iota_bf", const=True)
            P.add("dve", I("tensor_copy", out=iota_bf, in_=iota), r=[iota_b], w=[iota_bf_b])

            def gbuild(eT, eT_b, G_all, G_b):
                nb_ = PT // GB
                stageA = {}

                def sA(bt):
                    t0 = bt * GB
                    A1, A1_b = A1r.next()
                    B1, B1_b = B1r.next()
                    for t in range(GB):
                        tt = t0 + t
                        P.add("dve", I("tensor_scalar", out=A1[:, t, :], in0=iota_bf, scalar1=eT[:, 0, tt:tt + 1], scalar2=None,
                                       op0=ALU.is_equal), r=[iota_bf_b, eT_b], w=[A1_b])
                        P.add("dve", I("tensor_scalar", out=B1[:, t, :], in0=iota_bf, scalar1=eT[:, 1, tt:tt + 1],
                                       scalar2=eT[:, 2, tt:tt + 1], op0=ALU.is_equal, op1=ALU.mult), r=[iota_bf_b, eT_b], w=[B1_b])
                        yield
                    stageA[bt] = (A1, A1_b, B1, B1_b)

                def sB(bt):
                    t0 = bt * GB
                    A1, A1_b, B1, B1_b = stageA.pop(bt)
                    for q4 in range(GB // 4):
                        bank = 7
                        for tt in range(4):
                            t = q4 * 4 + tt
                            P.add("pe", I("matmul", ps_f32(bank)[:, tt * 128:(tt + 1) * 128], A1[:, t, :], B1[:, t, :],
                                           start=True, stop=True), r=[A1_b, B1_b], w=[PSB[bank]])
                        tb = t0 + q4 * 4
                        P.add("act", I("copy", out=G_all[:, tb:tb + 4, :], in_=ps_f32(bank).rearrange("p (t k) -> p t k", k=128)),
                              r=[PSB[bank]], w=[G_b])
                        yield

                yield from sA(0)
                yield from sA(1)
                for bt in range(nb_):
                    if bt + 2 < nb_:
                        yield from sA(bt + 2)
                    yield from sB(bt)

            def dense(T0, xq, xq_b, G_all, G_b, filler):
                nfill = [0]
                uvs = {}

                def load(c):
                    uv, uv_b = uvr.next()
                    P.dma("sp", [(uv, UV[c])], uv_b, w=[uv_b])
                    uvs[c] = (uv, uv_b)

                ats = {}

                def H(c):
                    uv, uv_b = uvs[c]
                    bank = 4 + c % 3
                    hps = ps_f32(bank)[:, 0:PT]
                    for do in range(8):
                        P.add("pe", I("matmul", hps, uv[:, do * 128:(do + 1) * 128], xq[:, do, :], start=(do == 0), stop=(do == 7)),
                              r=[uv_b, xq_b], w=[PSB[bank]])
                    gel, gel_b = gelr.next()
                    at, at_b = atr.next()
                    P.add("act", I("activation", out=gel, in_=hps, func=AF.Gelu), r=[PSB[bank]], w=[gel_b])
                    P.add("dve", I("tensor_tensor", out=at, in0=gel, in1=G_all[:, :, c], op=ALU.mult),
                          r=[gel_b, G_b], w=[at_b])
                    ats[c] = (at, at_b)

                def V(c):
                    uv, uv_b = uvs.pop(c)
                    at, at_b = ats.pop(c)
                    for u in range(2):
                        for hd in range(2):
                            bk = u * 2 + hd
                            P.add("pe", I("matmul", ps_f32(bk), at[:, u * 128:(u + 1) * 128], uv[:, 1024 + hd * 512:1024 + (hd + 1) * 512],
                                           start=(c == 0), stop=(c == NCH - 1)), r=[at_b, uv_b], w=[PSB[bk]])

                for c in range(UVD):
                    load(c)
                H(0); H(1)
                for c in range(NCH):
                    if c + 2 < NCH:
                        H(c + 2)
                    V(c)
                    if c + UVD < NCH:
                        load(c + UVD)
                    if filler is not None and c >= 2:
                        for _ in range(FILL_PER_CHUNK):
                            if next(filler, "end") == "end":
                                filler = None
                                break
                if filler is not None:
                    for _ in filler:
                        pass
                for u in range(2):
                    tok = T0 + u * 128
                    x2t, x2t_b = x2r.next()
                    P.dma("sp", [(x2t, X2[tok:tok + 128, :])], x2t_b, w=[x2t_b])
                    for hd in range(2):
                        bk = u * 2 + hd
                        P.add("dve", I("tensor_tensor", out=x2t[:, hd * 512:(hd + 1) * 512], in0=ps_f32(bk),
                                       in1=x2t[:, hd * 512:(hd + 1) * 512], op=ALU.add), r=[PSB[bk], x2t_b], w=[x2t_b])
                    store("sp", [(y_d[tok:tok + 128, :], x2t)], r=[x2t_b])

            def prep_tile(T0):
                xq, xq_b = xqr.next()
                eT, eT_b = eTr.next()
                P.dma("sp", [(xq[:, do, :], XN2T[do, :, T0:T0 + PT]) for do in range(8)], xq_b, w=[xq_b])
                st = dict(xq=xq, xq_b=xq_b, eT=eT, eT_b=eT_b)

                def gen():
                    for u in range(2):
                        yield from prep(T0, u, xq, xq_b, eT, eT_b)
                return st, gen()

            tiles = list(range(0, TOK, PT))

            def tile_gen(T0):
                st, g = prep_tile(T0)
                G_all, G_b = Gr.next()
                st["G"] = G_all
                st["G_b"] = G_b

                def gen():
                    yield from g
                    yield from gbuild(st["eT"], st["eT_b"], G_all, G_b)
                return st, gen()

            st, g0 = tile_gen(tiles[0])
            for _ in g0:
                pass
            for i, T0 in enumerate(tiles):
                if debug and i == 0:
                    dbg("eT", st["eT"], st["eT_b"])
                    dbg("G", st["G"][:, 0:8, :], st["G_b"])
                if i + 1 < len(tiles):
                    st2, g2 = tile_gen(tiles[i + 1])
                else:
                    st2, g2 = None, None
                dense(T0, st["xq"], st["xq_b"], st["G"], st["G_b"], g2)
                st = st2

        phase1()
        if stop_after >= 2:
            fence()
            phase2()
        if stop_after >= 3:
            fence()
            phase3()
        if stop_after >= 4:
            fence()
            phase4()
        P.emit()
    global LAST_PROG
    LAST_PROG = P
    return nc


def host_consts():
    half = 8
    inv = 500000.0 ** (-np.arange(0, 16, 2, dtype=np.float32) / 16)
    ang = np.arange(4096, dtype=np.float32)[:, None] * inv[None, :].astype(np.float32)
    rope = np.concatenate([np.cos(ang), np.sin(ang)], axis=1).astype(np.float32)
    j = np.arange(128)[:, None]
    i = np.arange(128)[None, :]
    m128 = np.concatenate([(j >= i), np.ones((128, 128), bool), (j <= i)], axis=1)
    m64 = np.concatenate([(j - i >= 64), (np.abs(i - j) <= 64), (i - j >= 64)], axis=1)
    masks = np.stack([m128, m64]).astype(np.float32).astype(ml_dtypes.bfloat16)
    return dict(
        c_rope=rope, c_mask=masks,
        c_identb=np.eye(128, dtype=np.float32).astype(ml_dtypes.bfloat16),
        c_identf=np.eye(128, dtype=np.float32),
        c_iota=np.tile(np.arange(128, dtype=np.float32)[None, :], (128, 1)),
    )


_PROG_CACHE = {}


def kernel(x_prompt, x_sample, norm1, w_in, q_norm_a, k_norm_a, sink_a, q_norm_b, k_norm_b,
           w_o_a, w_o_b, w_out, norm2, w_query, sub_keys_1, sub_keys_2, expert_u, expert_v):
    f = lambda a: np.ascontiguousarray(np.asarray(a, dtype=np.float32))
    x_prompt = f(x_prompt)
    x_sample = f(x_sample)
    shared = dict(
        norm1=f(norm1)[0:1], w_in=f(w_in)[0], q_norm_a=f(q_norm_a)[0:1], k_norm_a=f(k_norm_a)[0:1],
        sink_a=f(sink_a)[0:1], q_norm_b=f(q_norm_b)[0:1], k_norm_b=f(k_norm_b)[0:1],
        w_o_a=f(w_o_a)[0], w_o_b=f(w_o_b)[0], w_out=f(w_out)[0], norm2=f(norm2)[0:1],
        w_query=f(w_query)[0], sub_keys_1=f(sub_keys_1)[0], sub_keys_2=f(sub_keys_2)[0],
        expert_u=f(expert_u)[0], expert_v=f(expert_v)[0],
    )
    shared.update(host_consts())
    nc = build_program(FULL_SEQS)
    in_maps = []
    for c in range(NCORES):
        xp = x_prompt[4 * c:4 * c + 4].reshape(8192, 1024)
        xs = x_sample[2 * c:2 * c + 2].reshape(8192, 1024)
        m = dict(shared)
        m["x"] = np.concatenate([xp, xs], axis=0)
        in_maps.append(m)
    res = run_bass_kernel_spmd(nc, in_maps, core_ids=list(range(NCORES)))
    yp = np.empty((32, 2048, 1024), np.float32)
    ys = np.empty((16, 4096, 1024), np.float32)
    for c in range(NCORES):
        y = np.asarray(res.results[c]["y"], dtype=np.float32)
        yp[4 * c:4 * c + 4] = y[0:8192].reshape(4, 2048, 1024)
        ys[2 * c:2 * c + 2] = y[8192:16384].reshape(2, 4096, 1024)
    return (yp, ys)
```

```python
import numpy as np
import ml_dtypes
from contextlib import ExitStack
import concourse.bass as bass
import concourse.mybir as mybir
from concourse.bass_utils import run_bass_kernel_spmd

F32 = mybir.dt.float32
BF16 = mybir.dt.bfloat16
U32 = mybir.dt.uint32
U8 = mybir.dt.uint8
AF = mybir.ActivationFunctionType
ALU = mybir.AluOpType
AX = mybir.AxisListType
ESZ = {F32: 4, BF16: 2, U32: 4, U8: 1}

D_MODEL = 1024
EPS = 1e-6
NCORES = 8
FULL_SEQS = [2048] * 4 + [4096] * 2
GROUPS = [
    dict(name="A0", qcol=0, kcol=512, vcol=640, HK=1, w=128, D=1),
    dict(name="A1", qcol=256, kcol=576, vcol=704, HK=1, w=128, D=1),
    dict(name="B0", qcol=768, kcol=1536, vcol=2304, HK=4, w=64, D=1),
    dict(name="B1", qcol=1024, kcol=1792, vcol=2560, HK=4, w=64, D=4),
    dict(name="B2", qcol=1280, kcol=2048, vcol=2816, HK=4, w=64, D=16),
]
DBG_G = 2
NCH = 128
PT = 256
FILL_PER_CHUNK = 3
GB = 8
UVD = 4


class Buf:
    __slots__ = ("name", "lw", "rd", "rd_dma", "sem", "cum", "const")

    def __init__(self, name, const=False):
        self.name = name
        self.lw = None
        self.rd = {}
        self.rd_dma = []
        self.sem = None
        self.cum = 0
        self.const = const


class Op:
    __slots__ = ("eng", "fn", "deps", "sig", "sem", "ticket", "dma")

    def __init__(self, eng, fn, dma=None):
        self.eng = eng
        self.fn = fn
        self.deps = []
        self.sig = False
        self.sem = None
        self.ticket = 0
        self.dma = dma


class Prog:
    ENGS = ("pe", "act", "dve", "pool", "sp")

    def __init__(self, nc, stack):
        self.nc = nc
        self.stack = stack
        self.ops = {e: [] for e in self.ENGS}
        self.esem = {e: stack.enter_context(nc.semaphore("sem_" + e)) for e in self.ENGS}
        self.tokens = []
        self.nsem = len(self.ENGS)
        self.pending = {}
        self.fence_tok = Buf("fence")

    def _dep(self, op, prod, kind):
        if prod is None or prod is op:
            return
        if prod.dma is None and op.dma is None and prod.eng == op.eng and kind != "raw" and op.eng != "pool":
            return
        op.deps.append(prod)
        prod.sig = True

    def _track(self, op, reads, writes):
        for b in reads:
            self._dep(op, b.lw, "raw")
        for b in writes:
            self._dep(op, b.lw, "waw")
            for r in b.rd.values():
                self._dep(op, r, "war")
            for r in b.rd_dma:
                self._dep(op, r, "war")
        for b in writes:
            b.lw = op
            b.rd = {}
            b.rd_dma = []
        for b in reads:
            if b.const or b in writes:
                continue
            if op.dma is not None:
                b.rd_dma.append(op)
            else:
                b.rd[op.eng] = op

    def add(self, eng, fn, r=(), w=()):
        op = Op(eng, fn)
        p = self.pending.pop(eng, None)
        if p is not None:
            op.deps.append(p)
        self._track(op, r, w)
        self.ops[eng].append(op)
        return op

    def fence(self, pairs):
        lasts = [self.ops[e][-1] for e in self.ENGS if self.ops[e]]
        toks = [t.lw for t in self.tokens if t.lw is not None]
        op = self.dma("sp", pairs, self.fence_tok)
        for l in lasts + toks:
            if l is op:
                continue
            if l.dma is None:
                l.sig = True
            op.deps.append(l)
        self.pending = {e: op for e in self.ENGS}
        return op

    def dma(self, eng, pairs, token, r=(), w=()):
        op = Op(eng, None, dma=pairs)
        p = self.pending.pop(eng, None)
        if p is not None:
            op.deps.append(p)
        if token.sem is None:
            token.sem = self.stack.enter_context(self.nc.semaphore("dsem_" + token.name))
            self.nsem += 1
            self.tokens.append(token)
        if token not in w:
            w = tuple(w) + (token,)
        self._track(op, r, w)
        token.cum += 16 * len(pairs)
        op.sem = token.sem
        op.ticket = token.cum
        self.ops[eng].append(op)
        return op

    def emit(self):
        nc = self.nc
        for e in self.ENGS:
            n = 0
            for op in self.ops[e]:
                if op.dma is None and op.sig:
                    n += 1
                    op.sem = self.esem[e]
                    op.ticket = n
        handles = {"pe": nc.tensor, "act": nc.scalar, "dve": nc.vector, "pool": nc.gpsimd, "sp": nc.sync}

        def run(e, eng):
            waited = {}
            for op in self.ops[e]:
                for p in op.deps:
                    key = id(p.sem)
                    if waited.get(key, 0) >= p.ticket:
                        continue
                    waited[key] = p.ticket
                    eng.wait_ge(p.sem, p.ticket)
                if op.dma is not None:
                    for (o, i) in op.dma:
                        eng.dma_start(out=o, in_=i).then_inc(op.sem, 16)
                else:
                    ins = op.fn(eng)
                    if op.sig:
                        ins.then_inc(op.sem, 1)
            if e == "sp":
                for t in self.tokens:
                    if waited.get(id(t.sem), 0) < t.cum:
                        eng.wait_ge(t.sem, t.cum)

        with nc.Block() as block:
            @block.tensor
            def _(t):
                run("pe", t)

            @block.scalar
            def _(s):
                run("act", s)

            @block.vector
            def _(v):
                run("dve", v)

            @block.gpsimd
            def _(g):
                run("pool", g)

            @block.sync
            def _(sy):
                run("sp", sy)


def I(method, *args, **kw):
    return lambda e: getattr(e, method)(*args, **kw)


class Arena:
    def __init__(self, t, nbytes):
        self.t = t
        self.n = nbytes
        self.top = 0

    def alloc(self, dtype, shape):
        nfree = int(np.prod(shape[1:]))
        nb = nfree * ESZ[dtype]
        off = self.top
        self.top += (nb + 63) // 64 * 64
        assert self.top <= self.n, f"SBUF arena overflow {self.top} > {self.n}"
        ap = self.t[0:shape[0], off:off + nb].bitcast(dtype)
        if len(shape) == 3:
            ap = ap.rearrange("p (a b) -> p a b", b=shape[2])
        elif len(shape) == 4:
            ap = ap.rearrange("p (a b c) -> p a b c", b=shape[2], c=shape[3])
        return ap


class Ring:
    def __init__(self, arena, name, n, dtype, shape):
        self.n = n
        self.v = [arena.alloc(dtype, shape) for _ in range(n)]
        self.b = [Buf(f"{name}{i}") for i in range(n)]
        self.i = 0

    def next(self):
        k = self.i % self.n
        self.i += 1
        return self.v[k], self.b[k]


def bcast_rows(dram_ap_2d, ncols, nparts=128, col0=0):
    return bass.AP(dram_ap_2d.tensor, dram_ap_2d.offset + col0, [[0, nparts], [1, ncols]])


def build_program(seqs, stop_after=4, debug=False):
    TOK = sum(seqs)
    assert TOK % 512 == 0
    nc = bass.Bass("TRN2", target_bir_lowering=False)
    stack = ExitStack()
    with stack:
        def din(name, shape, dt=F32):
            return nc.dram_tensor(name, list(shape), dt, kind="ExternalInput").ap()

        okind = "ExternalOutput" if debug else "Internal"

        def dscr(name, shape, dt):
            return nc.dram_tensor(name, list(shape), dt, kind=okind).ap()

        x_d = din("x", [TOK, 1024])
        norm1_d = din("norm1", [1, 1024])
        w_in_d = din("w_in", [1024, 5120])
        qna_d = din("q_norm_a", [1, 64])
        kna_d = din("k_norm_a", [1, 64])
        sink_d = din("sink_a", [1, 8])
        qnb_d = din("q_norm_b", [1, 64])
        knb_d = din("k_norm_b", [1, 64])
        woa_d = din("w_o_a", [512, 1024])
        wob_d = din("w_o_b", [256, 1024])
        wout_d = din("w_out", [1024, 1024])
        norm2_d = din("norm2", [1, 1024])
        wq_d = din("w_query", [1024, 2048])
        sk1_d = din("sub_keys_1", [8, 128, 128])
        sk2_d = din("sub_keys_2", [8, 128, 128])
        eu_d = din("expert_u", [16384, 1024])
        ev_d = din("expert_v", [16384, 1024])
        rope_d = din("c_rope", [4096, 16])
        mask_d = din("c_mask", [2, 128, 384], BF16)
        identb_d = din("c_identb", [128, 128], BF16)
        identf_d = din("c_identf", [128, 128])
        iota_d = din("c_iota", [128, 128])
        y_d = nc.dram_tensor("y", [TOK, 1024], F32, kind="ExternalOutput").ap()

        XNT = dscr("s_xnt", [8, 128, TOK], BF16)
        NUM = dscr("s_num", [5, TOK, 260], F32)
        X2 = dscr("s_x2", [TOK, 1024], F32)
        XN2T = dscr("s_xn2t", [8, 128, TOK], BF16)
        UV = dscr("s_uv", [NCH, 128, 2048], BF16)
        SC = dscr("s_sc", [TOK, 2048], F32)

        SB_BYTES = 205 * 1024
        arena_t = stack.enter_context(nc.sbuf_tensor("arena", [128, SB_BYTES], U8))
        psb = [stack.enter_context(nc.psum_tensor(f"psb{i}", [128, 512], F32)) for i in range(8)]
        PSB = [Buf(f"psb{i}") for i in range(8)]
        P = Prog(nc, stack)
        A = Arena(arena_t, SB_BYTES)

        def ps_f32(i):
            return psb[i][:, :]

        def ps_bf16(i):
            return psb[i][:, :].bitcast(BF16)

        st_tok = [Buf(f"st{i}") for i in range(8)]
        st_i = [0]

        def store(eng, pairs, r):
            t = st_tok[st_i[0] % len(st_tok)]
            st_i[0] += 1
            return P.dma(eng, pairs, t, r=r)

        dbg_n = [0]

        def dbg(name, ap, b):
            if not debug:
                return
            shp = list(ap.shape)
            dt_ = ap.dtype
            d = nc.dram_tensor("dbg_" + name, shp, dt_, kind="ExternalOutput").ap()
            store("sp", [(d, ap)], r=[b])

        identb = A.alloc(BF16, [128, 128]); identb_b = Buf("identb", const=True)
        identf = A.alloc(F32, [128, 128]); identf_b = Buf("identf", const=True)
        P.dma("sp", [(identb, identb_d)], identb_b, w=[identb_b])
        P.dma("sp", [(identf, identf_d)], identf_b, w=[identf_b])
        epsc = A.alloc(F32, [128, 4]); epsc_b = Buf("epsc", const=True)
        P.add("pool", I("memset", epsc, EPS), w=[epsc_b])
        pers_top = A.top

        FZ = dscr("s_fence", [2, 16], F32)

        def fence():
            P.fence([(FZ[1:2, :], identf_d[0:1, 0:16])])
            A.top = pers_top

        def rms_tile(xt, xt_b, gb, gb_b, xn, xn_b, junk, junk_b, ss, ss_b):
            P.add("act", I("activation", out=junk, in_=xt, func=AF.Square, accum_out=ss[:, 0:1]),
                  r=[xt_b], w=[junk_b, ss_b])
            P.add("act", I("activation", out=ss[:, 1:2], in_=ss[:, 0:1], func=AF.Ln, scale=1.0 / 1024, bias=epsc[:, 0:1]),
                  r=[ss_b, epsc_b], w=[ss_b])
            P.add("act", I("activation", out=ss[:, 2:3], in_=ss[:, 1:2], func=AF.Exp, scale=-0.5), r=[ss_b], w=[ss_b])
            P.add("dve", I("scalar_tensor_tensor", out=xn, in0=xt, scalar=ss[:, 2:3], in1=gb,
                                                          op0=ALU.mult, op1=ALU.mult), r=[xt_b, ss_b, gb_b], w=[xn_b])

        def transpose8(src, src_b, dst_fn, dst_b, bank, evac_eng):
            pv = ps_bf16(bank)
            for do in range(8):
                P.add("pe", I("transpose", out=pv[:, do * 128:(do + 1) * 128],
                                                         in_=src[:, do * 128:(do + 1) * 128], identity=identb),
                      r=[src_b, identb_b], w=[PSB[bank]])
            pv3 = pv[:, 0:1024].rearrange("p (a b) -> p a b", b=128)
            if evac_eng == "act":
                P.add("act", I("copy", out=dst_fn, in_=pv3), r=[PSB[bank]], w=[dst_b])
            else:
                P.add("dve", I("tensor_copy", out=dst_fn, in_=pv3), r=[PSB[bank]], w=[dst_b])

        def uv_prepass_chunk(c, uvr, ubr):
            eu3 = eu_d.rearrange("(a b) d -> a b d", b=128)
            ev3 = ev_d.rearrange("(a b) d -> a b d", b=128)
            ub, ub_b = ubr.next()
            P.dma("pool", [(ub, eu3[:, c, :])], ub_b, w=[ub_b])
            uv, uv_b = uvr.next()
            P.dma("pool", [(uv[:, 1024:2048], ev3[:, c, :])], uv_b, w=[uv_b])
            bank = 6 + c % 2
            pv = ps_bf16(bank)
            for do in range(8):
                P.add("pe", I("transpose", out=pv[:, do * 128:(do + 1) * 128], in_=ub[:, do * 128:(do + 1) * 128],
                               identity=identb), r=[ub_b, identb_b], w=[PSB[bank]])
            P.add("dve", I("tensor_copy", out=uv[:, 0:1024], in_=pv[:, 0:1024]), r=[PSB[bank], uv_b], w=[uv_b])
            store("sp", [(UV[c], uv)], r=[uv_b])

        def phase1():
            uvr1 = Ring(A, "uv1", 3, BF16, [128, 2048])
            ubr1 = Ring(A, "ub1", 2, BF16, [128, 1024])
            uvc = [0]
            g1 = A.alloc(F32, [128, 1024]); g1_b = Buf("g1", const=True)
            P.dma("sp", [(g1, bcast_rows(norm1_d, 1024))], g1_b, w=[g1_b])
            xr = Ring(A, "p1x", 3, F32, [128, 1024])
            xnr = Ring(A, "p1xn", 2, BF16, [128, 1024])
            jr = Ring(A, "p1j", 1, BF16, [128, 1024])
            ssr = Ring(A, "p1ss", 4, F32, [128, 4])
            gr = Ring(A, "p1g", 2, BF16, [128, 8, 512])
            nt = TOK // 128
            loads = {}

            def load(i):
                xt, xt_b = xr.next()
                P.dma("sp", [(xt, x_d[i * 128:(i + 1) * 128, :])], xt_b, w=[xt_b])
                loads[i] = (xt, xt_b)

            load(0)
            if nt > 1:
                load(1)
            grp = None
            for i in range(nt):
                if i + 2 < nt:
                    load(i + 2)
                xt, xt_b = loads.pop(i)
                xn, xn_b = xnr.next()
                junk, junk_b = jr.next()
                ss, ss_b = ssr.next()
                rms_tile(xt, xt_b, g1, g1_b, xn, xn_b, junk, junk_b, ss, ss_b)
                j = i % 4
                if j == 0:
                    grp = gr.next()
                transpose8(xn, xn_b, grp[0][:, :, j * 128:(j + 1) * 128], grp[1], i % 2, "act")
                if j == 3:
                    t0 = (i - 3) * 128
                    store("sp", [(XNT[:, :, t0:t0 + 512].rearrange("o i t -> i o t"), grp[0])], r=[grp[1]])
                while uvc[0] < NCH and uvc[0] * nt < (i + 1) * NCH:
                    uv_prepass_chunk(uvc[0], uvr1, ubr1)
                    uvc[0] += 1

        def qk_norm_rope(ps_ap, ps_b, H, g_ap, g_b, tab, tab_b, out_bf, out_b, work):
            sq, sq_b, st, st_b, xn, xn_b, tmp, tmp_b = work
            H64 = H * 64
            P.add("act", I("activation", out=sq[:, 0:H64], in_=ps_ap, func=AF.Square), r=[ps_b], w=[sq_b])
            yield
            sq3 = sq[:, 0:H64].rearrange("p (h d) -> p h d", d=64)
            P.add("dve", I("tensor_reduce", out=st[:, 0:H], in_=sq3, axis=AX.X, op=ALU.add), r=[sq_b], w=[st_b])
            yield
            P.add("act", I("activation", out=st[:, 4:4 + H], in_=st[:, 0:H], func=AF.Ln, scale=1.0 / 64, bias=epsc[:, 0:1]),
                  r=[st_b, epsc_b], w=[st_b])
            yield
            P.add("act", I("activation", out=st[:, 8:8 + H], in_=st[:, 4:4 + H], func=AF.Exp, scale=-0.5), r=[st_b], w=[st_b])
            yield
            ps3 = ps_ap.rearrange("p (h d) -> p h d", d=64)
            xn3 = xn[:, 0:H64].rearrange("p (h d) -> p h d", d=64)
            rs_bc = st[:, 8:8 + H].unsqueeze(2).to_broadcast([128, H, 64])
            P.add("dve", I("tensor_tensor", out=xn3, in0=ps3, in1=rs_bc, op=ALU.mult), r=[ps_b, st_b], w=[xn_b])
            yield
            g_bc = g_ap.unsqueeze(1).to_broadcast([128, H, 64])
            P.add("dve", I("tensor_tensor", out=xn3, in0=xn3, in1=g_bc, op=ALU.mult), r=[xn_b, g_b], w=[xn_b])
            yield
            P.add("act", I("copy", out=out_bf, in_=xn3), r=[xn_b], w=[out_b])
            yield
            cos_bc = tab[:, 0:8].unsqueeze(1).to_broadcast([128, H, 8])
            sin_bc = tab[:, 8:16].unsqueeze(1).to_broadcast([128, H, 8])
            x1 = xn3[:, :, 0:8]
            x2 = xn3[:, :, 8:16]
            t3 = tmp[:, 0:4 * H * 8].rearrange("p (k h d) -> p k h d", k=4, d=8)
            P.add("dve", I("tensor_tensor", out=t3[:, 0], in0=x1, in1=cos_bc, op=ALU.mult), r=[xn_b, tab_b], w=[tmp_b])
            yield
            P.add("dve", I("tensor_tensor", out=t3[:, 1], in0=x2, in1=sin_bc, op=ALU.mult), r=[xn_b, tab_b], w=[tmp_b])
            yield
            P.add("pool", I("tensor_tensor", out=t3[:, 2], in0=x2, in1=cos_bc, op=ALU.mult), r=[xn_b, tab_b], w=[tmp_b])
            yield
            P.add("pool", I("tensor_tensor", out=t3[:, 3], in0=x1, in1=sin_bc, op=ALU.mult), r=[xn_b, tab_b], w=[tmp_b])
            yield
            P.add("dve", I("tensor_tensor", out=out_bf[:, :, 0:8], in0=t3[:, 0], in1=t3[:, 1], op=ALU.subtract),
                  r=[tmp_b, out_b], w=[out_b])
            yield
            P.add("dve", I("tensor_tensor", out=out_bf[:, :, 8:16], in0=t3[:, 2], in1=t3[:, 3], op=ALU.add),
                  r=[tmp_b, out_b], w=[out_b])
            yield

        def phase2():
            wqkv = A.alloc(BF16, [128, 8, 3072]); wqkv_b = Buf("wqkv", const=True)
            w3 = w_in_d.rearrange("(o i) c -> i o c", i=128)
            P.dma("pool", [(wqkv[:, do, :], w3[:, do, 0:3072]) for do in range(8)], wqkv_b, w=[wqkv_b])
            gains = A.alloc(F32, [128, 4, 64]); gains_b = Buf("gains", const=True)
            P.dma("sp", [(gains[:, 0, :], bcast_rows(qna_d, 64)), (gains[:, 1, :], bcast_rows(kna_d, 64)),
                         (gains[:, 2, :], bcast_rows(qnb_d, 64)), (gains[:, 3, :], bcast_rows(knb_d, 64))],
                  gains_b, w=[gains_b])
            P.add("act", I("mul", out=gains[:, 0, :], in_=gains[:, 0, :], mul=0.125), r=[gains_b], w=[gains_b])
            P.add("act", I("mul", out=gains[:, 2, :], in_=gains[:, 2, :], mul=0.125), r=[gains_b], w=[gains_b])
            masks = A.alloc(BF16, [128, 2, 384]); masks_b = Buf("masks", const=True)
            P.dma("sp", [(masks[:, 0, :], mask_d[0]), (masks[:, 1, :], mask_d[1])], masks_b, w=[masks_b])
            SMAX = max(seqs)
            xnT = A.alloc(BF16, [128, 8, SMAX]); xnT_b = Buf("xnT")
            NTMAX = SMAX // 128
            KT = A.alloc(BF16, [128, 2, SMAX])
            KT_b = [Buf(f"KT{a}") for a in range(NTMAX)]
            VA = A.alloc(BF16, [128, NTMAX, 4, 65])
            VA_b = [Buf(f"VA{a}") for a in range(NTMAX)]
            VA1_b = Buf("VAones")
            P.add("pool", I("memset", VA[:, :, :, 64:65], 1.0), w=[VA1_b] + VA_b)
            tabr = Ring(A, "tab", 6, F32, [128, 16])
            sqr = Ring(A, "sq", 4, F32, [128, 256])
            str_ = Ring(A, "st", 4, F32, [128, 12])
            xnr = Ring(A, "xnq", 4, F32, [128, 256])
            tmpr = Ring(A, "tmpq", 4, F32, [128, 128])
            kbr = Ring(A, "kb", 4, BF16, [128, 4, 64])
            qbr = Ring(A, "qb", 4, BF16, [128, 4, 64])
            QTr = Ring(A, "QT", 4, BF16, [128, 2, 128])
            ptr = Ring(A, "pt", 6, BF16, [128, 384])
            numr = Ring(A, "num", 4, F32, [128, 260])
            SBANK = [dict(proj=0, tr=2, sc=3, num=5), dict(proj=1, tr=7, sc=4, num=6)]

            def work():
                sq, sq_b = sqr.next(); st, st_b = str_.next(); xn, xn_b = xnr.next(); tmp, tmp_b = tmpr.next()
                return (sq, sq_b, st, st_b, xn, xn_b, tmp, tmp_b)

            def yield_each(n0):
                return sum(len(v) for v in P.ops.values())

            def load_tab(r, D, a):
                tab, tab_b = tabr.next()
                st0 = r + D * 128 * a
                P.dma("sp", [(tab, rope_d[st0:st0 + D * 127 + 1:D, :])], tab_b, w=[tab_b])
                return tab, tab_b

            def kv_task(g, r, a, slot, gk, sidx):
                D, HK = g["D"], g["HK"]
                isA = HK == 1
                bk = SBANK[sidx]
                tab, tab_b = load_tab(r, D, a)
                bank = bk["proj"]
                kv = ps_f32(bank)
                nk = HK * 64
                st0 = r + D * 128 * a
                ts = slice(st0, st0 + D * 127 + 1, D)
                for do in range(8):
                    P.add("pe", I("matmul", kv[:, 0:nk], xnT[:, do, ts], wqkv[:, do, g["kcol"]:g["kcol"] + nk],
                                   start=(do == 0), stop=(do == 7)), r=[xnT_b, wqkv_b], w=[PSB[bank]])
                yield
                for do in range(8):
                    P.add("pe", I("matmul", kv[:, 256:256 + nk], xnT[:, do, ts], wqkv[:, do, g["vcol"]:g["vcol"] + nk],
                                   start=(do == 0), stop=(do == 7)), r=[xnT_b, wqkv_b], w=[PSB[bank]])
                yield
                kb, kb_b = kbr.next()
                yield from qk_norm_rope(kv[:, 0:nk], PSB[bank], HK, gk, gains_b, tab, tab_b, kb[:, 0:HK, :], kb_b, work())
                if isA:
                    P.add("dve", I("tensor_copy", out=kb[:, 1, :], in_=kb[:, 0, :]), r=[kb_b], w=[kb_b])
                P.add("act", I("copy", out=VA[:, slot, 0:HK, 0:64], in_=kv[:, 256:256 + nk].rearrange("p (h d) -> p h d", d=64)),
                      r=[PSB[bank]], w=[VA_b[slot]])
                yield
                npair = 1 if isA else 2
                tb_ = bk["tr"]
                pv = ps_bf16(tb_)
                for p_ in range(npair):
                    P.add("pe", I("transpose", out=pv[:, p_ * 128:(p_ + 1) * 128],
                                   in_=kb[:, 2 * p_:2 * p_ + 2, :].rearrange("p h d -> p (h d)"), identity=identb),
                          r=[kb_b, identb_b], w=[PSB[tb_]])
                yield
                P.add("dve", I("tensor_copy", out=KT[:, 0:npair, slot * 128:(slot + 1) * 128],
                               in_=pv[:, 0:npair * 128].rearrange("p (a b) -> p a b", b=128)), r=[PSB[tb_]], w=[KT_b[slot]])
                yield

            def q_task(g, gi, s0, r, a, nt, slots, gq, mk, sidx):
                D, HK = g["D"], g["HK"]
                isA = HK == 1
                bk = SBANK[sidx]
                tab, tab_b = load_tab(r, D, a)
                bank = bk["proj"]
                qp = ps_f32(bank)
                st0 = r + D * 128 * a
                ts = slice(st0, st0 + D * 127 + 1, D)
                for do in range(8):
                    P.add("pe", I("matmul", qp[:, 0:256], xnT[:, do, ts], wqkv[:, do, g["qcol"]:g["qcol"] + 256],
                                   start=(do == 0), stop=(do == 7)), r=[xnT_b, wqkv_b], w=[PSB[bank]])
                yield
                qb, qb_b = qbr.next()
                yield from qk_norm_rope(qp[:, 0:256], PSB[bank], 4, gq, gains_b, tab, tab_b, qb, qb_b, work())
                tb_ = bk["tr"]
                pv = ps_bf16(tb_)
                for p_ in range(2):
                    P.add("pe", I("transpose", out=pv[:, p_ * 128:(p_ + 1) * 128],
                                   in_=qb[:, 2 * p_:2 * p_ + 2, :].rearrange("p h d -> p (h d)"), identity=identb),
                          r=[qb_b, identb_b], w=[PSB[tb_]])
                yield
                QT, QT_b = QTr.next()
                P.add("dve", I("tensor_copy", out=QT, in_=pv[:, 0:256].rearrange("p (a b) -> p a b", b=128)), r=[PSB[tb_]], w=[QT_b])
                yield
                blocks = [b for b in (a - 1, a, a + 1) if 0 <= b < nt]
                nb = len(blocks)
                moff = (blocks[0] - (a - 1)) * 128
                nbank = bk["num"]
                nps = ps_f32(nbank)
                sbank = bk["sc"]
                sps = ps_f32(sbank)
                for h in range(4):
                    kh = 0 if isA else h
                    bp = (h % 2) * 64
                    pair = h // 2
                    kpair = 0 if isA else kh // 2
                    for bi, b in enumerate(blocks):
                        sl_ = slots[b]
                        P.add("pe", I("matmul", sps[:, bi * 128:(bi + 1) * 128], KT[bp:bp + 64, kpair, sl_ * 128:(sl_ + 1) * 128],
                                       QT[bp:bp + 64, pair, :], start=True, stop=True), r=[KT_b[sl_], QT_b], w=[PSB[sbank]])
                    yield
                    pt, pt_b = ptr.next()
                    P.add("act", I("activation", out=pt[:, 0:nb * 128], in_=sps[:, 0:nb * 128], func=AF.Exp), r=[PSB[sbank]], w=[pt_b])
                    yield
                    P.add("dve", I("tensor_tensor", out=pt[:, 0:nb * 128], in0=pt[:, 0:nb * 128], in1=mk[:, moff:moff + nb * 128],
                                   op=ALU.mult), r=[pt_b, masks_b], w=[pt_b])
                    yield
                    for bi, b in enumerate(blocks):
                        sl_ = slots[b]
                        P.add("pe", I("matmul", nps[:, h * 65:(h + 1) * 65], pt[:, bi * 128:(bi + 1) * 128], VA[:, sl_, kh, :],
                                       start=(bi == 0), stop=(bi == nb - 1)), r=[pt_b, VA_b[sl_], VA1_b], w=[PSB[nbank]])
                    yield
                num, num_b = numr.next()
                P.add("act", I("copy", out=num, in_=nps[:, 0:260]), r=[PSB[nbank]], w=[num_b])
                yield
                st1 = s0 + r + D * 128 * a
                store("sp", [(NUM[gi, st1:st1 + D * 127 + 1:D, :], num)], r=[num_b])
                yield

            def load_task(S, s0, sidx):
                P.dma("sp", [(xnT[:, do, 0:S], XNT[do, :, s0:s0 + S]) for do in range(8)], xnT_b, w=[xnT_b])
                yield

            tasks = []
            s0 = 0
            base = 0
            for S in seqs:
                tasks.append(lambda sidx, S=S, s0=s0: load_task(S, s0, sidx))
                for gi, g in enumerate(GROUPS):
                    D, HK = g["D"], g["HK"]
                    nt = S // D // 128
                    isA = HK == 1
                    gq = gains[:, 0 if isA else 2, :]
                    gk = gains[:, 1 if isA else 3, :]
                    mk = masks[:, 0 if isA else 1, :]
                    for r in range(D):
                        slots = [(base + a) % NTMAX for a in range(nt)]
                        base += nt
                        for a in range(nt):
                            tasks.append(lambda sidx, g=g, r=r, a=a, sl=slots[a], gk=gk: kv_task(g, r, a, sl, gk, sidx))
                        for a in range(nt):
                            tasks.append(lambda sidx, g=g, gi=gi, s0=s0, r=r, a=a, nt=nt, slots=slots, gq=gq, mk=mk:
                                         q_task(g, gi, s0, r, a, nt, slots, gq, mk, sidx))
                s0 += S
            active = {}
            it = iter(tasks)
            done = False
            while True:
                while not done and len(active) < 2:
                    t = next(it, None)
                    if t is None:
                        done = True
                        break
                    sidx = 0 if 0 not in active else 1
                    active[sidx] = t(sidx)
                if not active:
                    break
                for sidx in list(active.keys()):
                    try:
                        next(active[sidx])
                    except StopIteration:
                        del active[sidx]

        def phase3():
            def wload(name, d_ap, nchunk, c0, ncol):
                t = A.alloc(BF16, [128, nchunk, ncol]); b = Buf(name, const=True)
                v = d_ap.rearrange("(o i) c -> i o c", i=128)
                P.dma("pool", [(t[:, o, :], v[:, o, c0:c0 + ncol]) for o in range(nchunk)], b, w=[b])
                return t, b
            woa, woa_b = wload("woa", woa_d, 4, 0, 1024)
            wob, wob_b = wload("wob", wob_d, 2, 0, 1024)
            wg, wg_b = wload("wg", w_in_d, 8, 3072, 2048)
            wout, wout_b = wload("wout", wout_d, 8, 0, 1024)
            wq, wq_b = wload("wq", wq_d, 8, 0, 2048)
            g2 = A.alloc(F32, [128, 1024]); g2_b = Buf("g2", const=True)
            P.dma("sp", [(g2, bcast_rows(norm2_d, 1024))], g2_b, w=[g2_b])
            esink = A.alloc(F32, [128, 8]); esink_b = Buf("esink", const=True)
            P.dma("sp", [(esink, bcast_rows(sink_d, 8))], esink_b, w=[esink_b])
            P.add("act", I("activation", out=esink, in_=esink, func=AF.Exp), r=[esink_b], w=[esink_b])
            n5r = Ring(A, "n5", 2, F32, [128, 5, 260])
            tBr = Ring(A, "tB", 2, F32, [128, 260])
            rAr = Ring(A, "rA", 2, F32, [128, 16])
            Or = Ring(A, "O", 2, BF16, [128, 768])
            oTr = Ring(A, "oT", 1, BF16, [128, 6, 512])
            xTr = Ring(A, "xT3", 1, BF16, [128, 8, 512])
            sgr = Ring(A, "sg", 2, F32, [128, 512])
            mr = Ring(A, "m12", 2, F32, [128, 512])
            mTr = Ring(A, "mT", 1, BF16, [128, 8, 512])
            xr = Ring(A, "x3", 1, F32, [128, 1024])
            x2r = Ring(A, "x23", 2, F32, [128, 1024])
            xn2r = Ring(A, "xn23", 1, BF16, [128, 1024])
            skT = A.alloc(F32, [128, 16, 128]); skT_b = Buf("skT", const=True)
            qTs = A.alloc(F32, [128, 16, 128]); qTs_b = Buf("qTs")
            scs = A.alloc(F32, [128, 16, 128]); scs_b = Buf("scs")
            P.dma("sp", [(qTs[:, 2 * h, :], sk1_d[h]) for h in range(8)] + [(qTs[:, 2 * h + 1, :], sk2_d[h]) for h in range(8)],
                  qTs_b, w=[qTs_b])
            for rnd in range(4):
                bank = 4 + rnd % 2
                for j in range(4):
                    g_ = rnd * 4 + j
                    P.add("pe", I("transpose", out=ps_f32(bank)[:, j * 128:(j + 1) * 128], in_=qTs[:, g_, :], identity=identf),
                          r=[qTs_b, identf_b], w=[PSB[bank]])
                P.add("act", I("copy", out=skT[:, rnd * 4:(rnd + 1) * 4, :],
                                in_=ps_f32(bank).rearrange("p (a b) -> p a b", b=128)), r=[PSB[bank]], w=[skT_b])
            ev3 = [0]

            def evac3(out, in_, r, w):
                ev3[0] += 1
                if ev3[0] % 2:
                    P.add("act", I("copy", out=out, in_=in_), r=r, w=w)
                else:
                    P.add("dve", I("tensor_copy", out=out, in_=in_), r=r, w=w)
            jr = Ring(A, "j3", 1, BF16, [128, 1024])
            ssr = Ring(A, "ss3", 4, F32, [128, 4])
            gr = Ring(A, "g3", 2, BF16, [128, 8, 512])
            for T0 in range(0, TOK, 512):
                oT, oT_b = oTr.next()
                for j in range(4):
                    tok = T0 + 128 * j
                    n5, n5_b = n5r.next()
                    P.dma("sp", [(n5[:, g_, :], NUM[g_, tok:tok + 128, :]) for g_ in range(5)], n5_b, w=[n5_b])
                    tB, tB_b = tBr.next()
                    rA, rA_b = rAr.next()
                    O, O_b = Or.next()
                    P.add("dve", I("tensor_tensor", out=tB, in0=n5[:, 2, :], in1=n5[:, 3, :], op=ALU.add), r=[n5_b], w=[tB_b])
                    P.add("dve", I("tensor_tensor", out=tB, in0=tB, in1=n5[:, 4, :], op=ALU.add), r=[n5_b, tB_b], w=[tB_b])
                    A8 = n5[:, 0:2, :].rearrange("p g (h e) -> p (g h) e", e=65)
                    B4 = tB.rearrange("p (h e) -> p h e", e=65)
                    P.add("dve", I("tensor_tensor", out=rA[:, 0:8], in0=A8[:, :, 64], in1=esink, op=ALU.add),
                          r=[n5_b, esink_b], w=[rA_b])
                    P.add("dve", I("tensor_copy", out=rA[:, 8:12], in_=B4[:, :, 64]), r=[tB_b], w=[rA_b])
                    P.add("dve", I("reciprocal", out=rA[:, 0:12], in_=rA[:, 0:12]), r=[rA_b], w=[rA_b])
                    P.add("dve", I("tensor_tensor", out=O[:, 0:512].rearrange("p (h d) -> p h d", d=64), in0=A8[:, :, 0:64],
                                   in1=rA[:, 0:8].unsqueeze(2).to_broadcast([128, 8, 64]), op=ALU.mult), r=[n5_b, rA_b], w=[O_b])
                    P.add("dve", I("tensor_tensor", out=O[:, 512:768].rearrange("p (h d) -> p h d", d=64), in0=B4[:, :, 0:64],
                                   in1=rA[:, 8:12].unsqueeze(2).to_broadcast([128, 4, 64]), op=ALU.mult), r=[tB_b, rA_b, O_b], w=[O_b])
                    if debug and T0 == 0 and j == 0:
                        dbg("O", O, O_b)
                    bank = 0
                    pv = ps_bf16(bank)
                    for fc in range(6):
                        P.add("pe", I("transpose", out=pv[:, fc * 128:(fc + 1) * 128], in_=O[:, fc * 128:(fc + 1) * 128],
                                       identity=identb), r=[O_b, identb_b], w=[PSB[bank]])
                    P.add("act", I("copy", out=oT[:, :, j * 128:(j + 1) * 128],
                                    in_=pv[:, 0:768].rearrange("p (a b) -> p a b", b=128)), r=[PSB[bank]], w=[oT_b])
                xT, xT_b = xTr.next()
                P.dma("sp", [(xT[:, do, :], XNT[do, :, T0:T0 + 512]) for do in range(8)], xT_b, w=[xT_b])
                mT, mT_b = mTr.next()
                for dc in range(8):
                    bs = (dc % 2) * 4
                    dcs = slice(dc * 128, (dc + 1) * 128)
                    ya, yb, ga, gb = ps_f32(bs), ps_f32(bs + 1), ps_f32(bs + 2), ps_f32(bs + 3)
                    for fc in range(4):
                        P.add("pe", I("matmul", ya, woa[:, fc, dcs], oT[:, fc, :], start=(fc == 0), stop=(fc == 3)),
                              r=[woa_b, oT_b], w=[PSB[bs]])
                    for fc in range(2):
                        P.add("pe", I("matmul", yb, wob[:, fc, dcs], oT[:, 4 + fc, :], start=(fc == 0), stop=(fc == 1)),
                              r=[wob_b, oT_b], w=[PSB[bs + 1]])
                    for do in range(8):
                        P.add("pe", I("matmul", ga, wg[:, do, dcs], xT[:, do, :], start=(do == 0), stop=(do == 7)),
                              r=[wg_b, xT_b], w=[PSB[bs + 2]])
                    for do in range(8):
                        P.add("pe", I("matmul", gb, wg[:, do, 1024 + dc * 128:1024 + (dc + 1) * 128], xT[:, do, :],
                                       start=(do == 0), stop=(do == 7)), r=[wg_b, xT_b], w=[PSB[bs + 3]])
                    sga, sga_b = sgr.next()
                    sgb, sgb_b = sgr.next()
                    P.add("act", I("activation", out=sga, in_=ga, func=AF.Sigmoid), r=[PSB[bs + 2]], w=[sga_b])
                    P.add("act", I("activation", out=sgb, in_=gb, func=AF.Sigmoid), r=[PSB[bs + 3]], w=[sgb_b])
                    m1, m1_b = mr.next()
                    m2, m2_b = mr.next()
                    P.add("dve", I("tensor_tensor", out=m1, in0=ya, in1=sga, op=ALU.mult), r=[PSB[bs], sga_b], w=[m1_b])
                    P.add("dve", I("tensor_tensor", out=m2, in0=yb, in1=sgb, op=ALU.mult), r=[PSB[bs + 1], sgb_b], w=[m2_b])
                    P.add("pool", I("tensor_tensor", out=mT[:, dc, :], in0=m1, in1=m2, op=ALU.add), r=[m1_b, m2_b], w=[mT_b])
                grp = gr.next()
                for j in range(4):
                    tok = T0 + 128 * j
                    xt, xt_b = xr.next()
                    P.dma("sp", [(xt, x_d[tok:tok + 128, :])], xt_b, w=[xt_b])
                    x2t, x2t_b = x2r.next()
                    for hd in range(2):
                        bank = 1 + hd
                        ops_ = ps_f32(bank)
                        for dc in range(8):
                            P.add("pe", I("matmul", ops_, mT[:, dc, j * 128:(j + 1) * 128], wout[:, dc, hd * 512:(hd + 1) * 512],
                                           start=(dc == 0), stop=(dc == 7)), r=[mT_b, wout_b], w=[PSB[bank]])
                        P.add("dve", I("tensor_tensor", out=x2t[:, hd * 512:(hd + 1) * 512], in0=ops_,
                                       in1=xt[:, hd * 512:(hd + 1) * 512], op=ALU.add), r=[PSB[bank], xt_b], w=[x2t_b])
                    store("sp", [(X2[tok:tok + 128, :], x2t)], r=[x2t_b])
                    xn2, xn2_b = xn2r.next()
                    junk, junk_b = jr.next()
                    ss, ss_b = ssr.next()
                    rms_tile(x2t, x2t_b, g2, g2_b, xn2, xn2_b, junk, junk_b, ss, ss_b)
                    transpose8(xn2, xn2_b, grp[0][:, :, j * 128:(j + 1) * 128], grp[1], 3, "act")
                    for qc in range(16):
                        bank = 4 + qc % 2
                        qps = ps_f32(bank)[:, 0:128]
                        for do in range(8):
                            P.add("pe", I("matmul", qps, wq[:, do, qc * 128:(qc + 1) * 128], grp[0][:, do, j * 128:(j + 1) * 128],
                                           start=(do == 0), stop=(do == 7)), r=[wq_b, grp[1]], w=[PSB[bank]])
                        evac3(qTs[:, qc, :], qps, [PSB[bank]], [qTs_b])
                    for rnd in range(4):
                        bank = 6 + rnd % 2
                        sps = ps_f32(bank)
                        for jj in range(4):
                            g_ = rnd * 4 + jj
                            P.add("pe", I("matmul", sps[:, jj * 128:(jj + 1) * 128], qTs[:, g_, :], skT[:, g_, :], start=True, stop=True),
                                  r=[qTs_b, skT_b], w=[PSB[bank]])
                        evac3(scs[:, rnd * 4:(rnd + 1) * 4, :], sps.rearrange("p (a b) -> p a b", b=128), [PSB[bank]], [scs_b])
                    store("sp", [(SC[tok:tok + 128, :], scs.rearrange("p a b -> p (a b)"))], r=[scs_b])
                store("sp", [(XN2T[:, :, T0:T0 + 512].rearrange("o i t -> i o t"), grp[0])], r=[grp[1]])


        def phase4():
            PSBH = [Buf(f"psbh{i}") for i in range(4)]
            iota = A.alloc(F32, [128, 128]); iota_b = Buf("iota", const=True)
            P.dma("sp", [(iota, iota_d)], iota_b, w=[iota_b])
            Gr = Ring(A, "G", 2, BF16, [128, PT, 128])
            uvr = Ring(A, "uv", UVD, BF16, [128, 2048])
            xqr = Ring(A, "xq", 2, BF16, [128, 8, PT])
            sc = A.alloc(F32, [128, 16, 128]); sc_b = Buf("sc")
            sc2 = A.alloc(F32, [128, 16, 128]); sc2_b = Buf("sc2")
            cand = sc.rearrange("p a b -> p (a b)").rearrange("p (h c) -> p h c", c=256)
            cand2 = sc2.rearrange("p a b -> p (a b)").rearrange("p (h c) -> p h c", c=256)
            oh = sc.rearrange("p a b -> p (a b)").rearrange("p (h s i) -> p h s i", s=16, i=16)
            cand4 = oh
            vt = A.alloc(F32, [128, 16, 16]); vt_b = Buf("vt")
            ix = A.alloc(U32, [128, 16, 16]); ix_b = Buf("ix")
            ixf = A.alloc(F32, [128, 16, 16]); ixf_b = Buf("ixf")
            top = A.alloc(F32, [128, 8, 16]); top_b = Buf("top")
            ci = A.alloc(U32, [128, 8, 16]); ci_b = Buf("ci")
            hl = A.alloc(U32, [128, 2, 128]); hl_b = Buf("hl")
            hlf = A.alloc(F32, [128, 2, 128]); hlf_b = Buf("hlf")
            ex = A.alloc(F32, [128, 8, 16]); ex_b = Buf("ex")
            sm = A.alloc(F32, [128, 8]); sm_b = Buf("sm")
            e12w = A.alloc(F32, [128, 3, 128]); e12w_b = Buf("e12w")
            eTr = Ring(A, "eT", 2, F32, [128, 3, PT])
            A1r = Ring(A, "A1", 3, BF16, [128, GB, 128])
            B1r = Ring(A, "B1", 3, BF16, [128, GB, 128])
            gelr = Ring(A, "gel", 3, BF16, [128, PT])
            atr = Ring(A, "at", 3, BF16, [128, PT])
            x2r = Ring(A, "x24", 1, F32, [128, 1024])
            ev_i = [0]

            def evac(out, in_, r, w):
                ev_i[0] += 1
                if ev_i[0] % 2:
                    P.add("act", I("copy", out=out, in_=in_), r=r, w=w)
                else:
                    P.add("dve", I("tensor_copy", out=out, in_=in_), r=r, w=w)

            def prep(T0, u, xq, xq_b, eT, eT_b):
                us = slice(u * 128, (u + 1) * 128)
                tok = T0 + u * 128
                P.dma("pool", [(sc.rearrange("p a b -> p (a b)"), SC[tok:tok + 128, :])], sc_b, w=[sc_b])
                yield
                for g_ in range(16):
                    P.add("dve", I("max", out=vt[:, g_, 0:8], in_=sc[:, g_, :]), r=[sc_b], w=[vt_b])
                    P.add("dve", I("max_index", out=ix[:, g_, 0:8], in_max=vt[:, g_, 0:8], in_values=sc[:, g_, :]),
                          r=[sc_b, vt_b], w=[ix_b])
                    P.add("dve", I("match_replace", out=sc2[:, g_, :], in_to_replace=vt[:, g_, 0:8], in_values=sc[:, g_, :],
                                   imm_value=-1e30), r=[sc_b, vt_b], w=[sc2_b])
                    P.add("dve", I("max", out=vt[:, g_, 8:16], in_=sc2[:, g_, :]), r=[sc2_b], w=[vt_b])
                    P.add("dve", I("max_index", out=ix[:, g_, 8:16], in_max=vt[:, g_, 8:16], in_values=sc2[:, g_, :]),
                          r=[sc2_b, vt_b], w=[ix_b])
                    yield
                v4 = vt.rearrange("p (h two) s -> p h two s", two=2)
                P.add("dve", I("tensor_tensor", out=cand4, in0=v4[:, :, 0, :].unsqueeze(3).to_broadcast([128, 8, 16, 16]),
                               in1=v4[:, :, 1, :].unsqueeze(2).to_broadcast([128, 8, 16, 16]), op=ALU.add), r=[vt_b], w=[sc_b])
                for h in range(8):
                    P.add("dve", I("max", out=top[:, h, 0:8], in_=cand[:, h, :]), r=[sc_b], w=[top_b])
                    P.add("dve", I("max_index", out=ci[:, h, 0:8], in_max=top[:, h, 0:8], in_values=cand[:, h, :]),
                          r=[sc_b, top_b], w=[ci_b])
                    P.add("dve", I("match_replace", out=cand2[:, h, :], in_to_replace=top[:, h, 0:8], in_values=cand[:, h, :],
                                   imm_value=-1e30), r=[sc_b, top_b], w=[sc2_b])
                    P.add("dve", I("max", out=top[:, h, 8:16], in_=cand2[:, h, :]), r=[sc2_b], w=[top_b])
                    P.add("dve", I("max_index", out=ci[:, h, 8:16], in_max=top[:, h, 8:16], in_values=cand2[:, h, :]),
                          r=[sc2_b, top_b], w=[ci_b])
                    yield
                P.add("dve", I("tensor_tensor", out=ex, in0=top, in1=top[:, :, 0:1].to_broadcast([128, 8, 16]), op=ALU.subtract),
                      r=[top_b], w=[ex_b])
                P.add("act", I("activation", out=ex, in_=ex, func=AF.Exp), r=[ex_b], w=[ex_b])
                P.add("dve", I("tensor_reduce", out=sm, in_=ex, axis=AX.X, op=ALU.add), r=[ex_b], w=[sm_b])
                P.add("dve", I("reciprocal", out=sm, in_=sm), r=[sm_b], w=[sm_b])
                P.add("dve", I("tensor_tensor", out=e12w[:, 2, :].rearrange("p (h s) -> p h s", s=16), in0=ex,
                               in1=sm.unsqueeze(2).to_broadcast([128, 8, 16]), op=ALU.mult), r=[ex_b, sm_b], w=[e12w_b])
                cif = ci.rearrange("p h s -> p (h s)")
                P.add("dve", I("tensor_single_scalar", out=hl[:, 0, :], in_=cif, scalar=4, op=ALU.logical_shift_right),
                      r=[ci_b], w=[hl_b])
                P.add("dve", I("tensor_single_scalar", out=hl[:, 1, :], in_=cif, scalar=15, op=ALU.bitwise_and),
                      r=[ci_b], w=[hl_b])
                P.add("dve", I("tensor_copy", out=hlf, in_=hl), r=[hl_b], w=[hlf_b])
                P.add("dve", I("tensor_copy", out=ixf, in_=ix), r=[ix_b], w=[ixf_b])
                yield
                ixf4 = ixf.rearrange("p (h two) s -> p h two s", two=2)
                io16 = iota[:, 0:16].unsqueeze(1).unsqueeze(1).to_broadcast([128, 8, 16, 16])
                for k in range(2):
                    sel = hlf[:, k, :].rearrange("p (h s) -> p h s", s=16).unsqueeze(3).to_broadcast([128, 8, 16, 16])
                    P.add("dve", I("tensor_tensor", out=oh, in0=sel, in1=io16, op=ALU.is_equal), r=[hlf_b, iota_b], w=[sc_b])
                    P.add("dve", I("tensor_tensor", out=oh, in0=oh, in1=ixf4[:, :, k, :].unsqueeze(2).to_broadcast([128, 8, 16, 16]),
                                   op=ALU.mult), r=[sc_b, ixf_b], w=[sc_b])
                    P.add("dve", I("tensor_reduce", out=e12w[:, k, :].rearrange("p (h s) -> p h s", s=16), in_=oh, axis=AX.X,
                                   op=ALU.add), r=[sc_b], w=[e12w_b])
                    yield
                if debug and T0 == 0 and u == 0:
                    dbg("e12w", e12w, e12w_b)
                    dbg("top", top, top_b)
                bank = 7
                for k in range(3):
                    P.add("pe", I("transpose", out=ps_f32(bank)[:, k * 128:(k + 1) * 128], in_=e12w[:, k, :], identity=identf),
                          r=[e12w_b, identf_b], w=[PSB[bank]])
                P.add("act", I("copy", out=eT[:, :, us], in_=ps_f32(bank)[:, 0:384].rearrange("p (a b) -> p a b", b=128)),
                      r=[PSB[bank]], w=[eT_b])
                yield

            gb_i = [0]
            iota_bf = A.alloc(BF16, [128, 128]); iota_bf_b = Buf("iota_bf", const=True)
            P.add("dve", I("tensor_copy", out=iota_bf, in_=iota), r=[iota_b], w=[iota_bf_b])

            def gbuild(eT, eT_b, G_all, G_b):
                nb_ = PT // GB
                stageA = {}

                def sA(bt):
                    t0 = bt * GB
                    A1, A1_b = A1r.next()
                    B1, B1_b = B1r.next()
                    for t in range(GB):
                        tt = t0 + t
                        P.add("dve", I("tensor_scalar", out=A1[:, t, :], in0=iota_bf, scalar1=eT[:, 0, tt:tt + 1], scalar2=None,
                                       op0=ALU.is_equal), r=[iota_bf_b, eT_b], w=[A1_b])
                        P.add("dve", I("tensor_scalar", out=B1[:, t, :], in0=iota_bf, scalar1=eT[:, 1, tt:tt + 1],
                                       scalar2=eT[:, 2, tt:tt + 1], op0=ALU.is_equal, op1=ALU.mult), r=[iota_bf_b, eT_b], w=[B1_b])
                        yield
                    stageA[bt] = (A1, A1_b, B1, B1_b)

                def sB(bt):
                    t0 = bt * GB
                    A1, A1_b, B1, B1_b = stageA.pop(bt)
                    for q4 in range(GB // 4):
                        bank = 7
                        for tt in range(4):
                            t = q4 * 4 + tt
                            P.add("pe", I("matmul", ps_f32(bank)[:, tt * 128:(tt + 1) * 128], A1[:, t, :], B1[:, t, :],
                                           start=True, stop=True), r=[A1_b, B1_b], w=[PSB[bank]])
                        tb = t0 + q4 * 4
                        P.add("act", I("copy", out=G_all[:, tb:tb + 4, :], in_=ps_f32(bank).rearrange("p (t k) -> p t k", k=128)),
                              r=[PSB[bank]], w=[G_b])
                        yield

                yield from sA(0)
                yield from sA(1)
                for bt in range(nb_):
                    if bt + 2 < nb_:
                        yield from sA(bt + 2)
                    yield from sB(bt)

            def dense(T0, xq, xq_b, G_all, G_b, filler):
                nfill = [0]
                uvs = {}

                def load(c):
                    uv, uv_b = uvr.next()
                    P.dma("sp", [(uv, UV[c])], uv_b, w=[uv_b])
                    uvs[c] = (uv, uv_b)

                ats = {}

                def H(c):
                    uv, uv_b = uvs[c]
                    bank = 4 + c % 3
                    hps = ps_f32(bank)[:, 0:PT]
                    for do in range(8):
                        P.add("pe", I("matmul", hps, uv[:, do * 128:(do + 1) * 128], xq[:, do, :], start=(do == 0), stop=(do == 7)),
                              r=[uv_b, xq_b], w=[PSB[bank]])
                    gel, gel_b = gelr.next()
                    at, at_b = atr.next()
                    P.add("act", I("activation", out=gel, in_=hps, func=AF.Gelu), r=[PSB[bank]], w=[gel_b])
                    P.add("dve", I("tensor_tensor", out=at, in0=gel, in1=G_all[:, :, c], op=ALU.mult),
                          r=[gel_b, G_b], w=[at_b])
                    ats[c] = (at, at_b)

                def V(c):
                    uv, uv_b = uvs.pop(c)
                    at, at_b = ats.pop(c)
                    for u in range(2):
                        for hd in range(2):
                            bk = u * 2 + hd
                            P.add("pe", I("matmul", ps_f32(bk), at[:, u * 128:(u + 1) * 128], uv[:, 1024 + hd * 512:1024 + (hd + 1) * 512],
                                           start=(c == 0), stop=(c == NCH - 1)), r=[at_b, uv_b], w=[PSB[bk]])

                for c in range(UVD):
                    load(c)
                H(0); H(1)
                for c in range(NCH):
                    if c + 2 < NCH:
                        H(c + 2)
                    V(c)
                    if c + UVD < NCH:
                        load(c + UVD)
                    if filler is not None and c >= 2:
                        for _ in range(FILL_PER_CHUNK):
                            if next(filler, "end") == "end":
                                filler = None
                                break
                if filler is not None:
                    for _ in filler:
                        pass
                for u in range(2):
                    tok = T0 + u * 128
                    x2t, x2t_b = x2r.next()
                    P.dma("pool", [(x2t, X2[tok:tok + 128, :])], x2t_b, w=[x2t_b])
                    for hd in range(2):
                        bk = u * 2 + hd
                        P.add("dve", I("tensor_tensor", out=x2t[:, hd * 512:(hd + 1) * 512], in0=ps_f32(bk),
                                       in1=x2t[:, hd * 512:(hd + 1) * 512], op=ALU.add), r=[PSB[bk], x2t_b], w=[x2t_b])
                    store("sp", [(y_d[tok:tok + 128, :], x2t)], r=[x2t_b])

            def prep_tile(T0):
                xq, xq_b = xqr.next()
                eT, eT_b = eTr.next()
                P.dma("sp", [(xq[:, do, :], XN2T[do, :, T0:T0 + PT]) for do in range(8)], xq_b, w=[xq_b])
                st = dict(xq=xq, xq_b=xq_b, eT=eT, eT_b=eT_b)

                def gen():
                    for u in range(2):
                        yield from prep(T0, u, xq, xq_b, eT, eT_b)
                return st, gen()

            tiles = list(range(0, TOK, PT))

            def tile_gen(T0):
                st, g = prep_tile(T0)
                G_all, G_b = Gr.next()
                st["G"] = G_all
                st["G_b"] = G_b

                def gen():
                    yield from g
                    yield from gbuild(st["eT"], st["eT_b"], G_all, G_b)
                return st, gen()

            st, g0 = tile_gen(tiles[0])
            for _ in g0:
                pass
            for i, T0 in enumerate(tiles):
                if debug and i == 0:
                    dbg("eT", st["eT"], st["eT_b"])
                    dbg("G", st["G"][:, 0:8, :], st["G_b"])
                if i + 1 < len(tiles):
                    st2, g2 = tile_gen(tiles[i + 1])
                else:
                    st2, g2 = None, None
                dense(T0, st["xq"], st["xq_b"], st["G"], st["G_b"], g2)
                st = st2

        phase1()
        if stop_after >= 2:
            fence()
            phase2()
        if stop_after >= 3:
            fence()
            phase3()
        if stop_after >= 4:
            fence()
            phase4()
        P.emit()
    global LAST_PROG
    LAST_PROG = P
    return nc


def host_consts():
    half = 8
    inv = 500000.0 ** (-np.arange(0, 16, 2, dtype=np.float32) / 16)
    ang = np.arange(4096, dtype=np.float32)[:, None] * inv[None, :].astype(np.float32)
    rope = np.concatenate([np.cos(ang), np.sin(ang)], axis=1).astype(np.float32)
    j = np.arange(128)[:, None]
    i = np.arange(128)[None, :]
    m128 = np.concatenate([(j >= i), np.ones((128, 128), bool), (j <= i)], axis=1)
    m64 = np.concatenate([(j - i >= 64), (np.abs(i - j) <= 64), (i - j >= 64)], axis=1)
    masks = np.stack([m128, m64]).astype(np.float32).astype(ml_dtypes.bfloat16)
    return dict(
        c_rope=rope, c_mask=masks,
        c_identb=np.eye(128, dtype=np.float32).astype(ml_dtypes.bfloat16),
        c_identf=np.eye(128, dtype=np.float32),
        c_iota=np.tile(np.arange(128, dtype=np.float32)[None, :], (128, 1)),
    )


_PROG_CACHE = {}


def kernel(x_prompt, x_sample, norm1, w_in, q_norm_a, k_norm_a, sink_a, q_norm_b, k_norm_b,
           w_o_a, w_o_b, w_out, norm2, w_query, sub_keys_1, sub_keys_2, expert_u, expert_v):
    f = lambda a: np.ascontiguousarray(np.asarray(a, dtype=np.float32))
    x_prompt = f(x_prompt)
    x_sample = f(x_sample)
    shared = dict(
        norm1=f(norm1)[0:1], w_in=f(w_in)[0], q_norm_a=f(q_norm_a)[0:1], k_norm_a=f(k_norm_a)[0:1],
        sink_a=f(sink_a)[0:1], q_norm_b=f(q_norm_b)[0:1], k_norm_b=f(k_norm_b)[0:1],
        w_o_a=f(w_o_a)[0], w_o_b=f(w_o_b)[0], w_out=f(w_out)[0], norm2=f(norm2)[0:1],
        w_query=f(w_query)[0], sub_keys_1=f(sub_keys_1)[0], sub_keys_2=f(sub_keys_2)[0],
        expert_u=f(expert_u)[0], expert_v=f(expert_v)[0],
    )
    shared.update(host_consts())
    nc = build_program(FULL_SEQS)
    in_maps = []
    for c in range(NCORES):
        xp = x_prompt[4 * c:4 * c + 4].reshape(8192, 1024)
        xs = x_sample[2 * c:2 * c + 2].reshape(8192, 1024)
        m = dict(shared)
        m["x"] = np.concatenate([xp, xs], axis=0)
        in_maps.append(m)
    res = run_bass_kernel_spmd(nc, in_maps, core_ids=list(range(NCORES)))
    yp = np.empty((32, 2048, 1024), np.float32)
    ys = np.empty((16, 4096, 1024), np.float32)
    for c in range(NCORES):
        y = np.asarray(res.results[c]["y"], dtype=np.float32)
        yp[4 * c:4 * c + 4] = y[0:8192].reshape(4, 2048, 1024)
        ys[2 * c:2 * c + 2] = y[8192:16384].reshape(2, 4096, 1024)
    return (yp, ys)
```

```python
import numpy as np
import ml_dtypes
from contextlib import ExitStack
import concourse.bass as bass
import concourse.mybir as mybir
from concourse.bass_utils import run_bass_kernel_spmd

F32 = mybir.dt.float32
BF16 = mybir.dt.bfloat16
U32 = mybir.dt.uint32
U8 = mybir.dt.uint8
AF = mybir.ActivationFunctionType
ALU = mybir.AluOpType
AX = mybir.AxisListType
ESZ = {F32: 4, BF16: 2, U32: 4, U8: 1}

D_MODEL = 1024
EPS = 1e-6
NCORES = 8
FULL_SEQS = [2048] * 4 + [4096] * 2
GROUPS = [
    dict(name="A0", qcol=0, kcol=512, vcol=640, HK=1, w=128, D=1),
    dict(name="A1", qcol=256, kcol=576, vcol=704, HK=1, w=128, D=1),
    dict(name="B0", qcol=768, kcol=1536, vcol=2304, HK=4, w=64, D=1),
    dict(name="B1", qcol=1024, kcol=1792, vcol=2560, HK=4, w=64, D=4),
    dict(name="B2", qcol=1280, kcol=2048, vcol=2816, HK=4, w=64, D=16),
]
DBG_G = 2
NCH = 128
PT = 256
FILL_PER_CHUNK = 3
GB = 8
UVD = 4


class Buf:
    __slots__ = ("name", "lw", "rd", "rd_dma", "sem", "cum", "const")

    def __init__(self, name, const=False):
        self.name = name
        self.lw = None
        self.rd = {}
        self.rd_dma = []
        self.sem = None
        self.cum = 0
        self.const = const


class Op:
    __slots__ = ("eng", "fn", "deps", "sig", "sem", "ticket", "dma")

    def __init__(self, eng, fn, dma=None):
        self.eng = eng
        self.fn = fn
        self.deps = []
        self.sig = False
        self.sem = None
        self.ticket = 0
        self.dma = dma


class Prog:
    ENGS = ("pe", "act", "dve", "pool", "sp")

    def __init__(self, nc, stack):
        self.nc = nc
        self.stack = stack
        self.ops = {e: [] for e in self.ENGS}
        self.esem = {e: stack.enter_context(nc.semaphore("sem_" + e)) for e in self.ENGS}
        self.tokens = []
        self.nsem = len(self.ENGS)
        self.pending = {}
        self.fence_tok = Buf("fence")

    def _dep(self, op, prod, kind):
        if prod is None or prod is op:
            return
        if prod.dma is None and op.dma is None and prod.eng == op.eng and kind != "raw" and op.eng != "pool":
            return
        op.deps.append(prod)
        prod.sig = True

    def _track(self, op, reads, writes):
        for b in reads:
            self._dep(op, b.lw, "raw")
        for b in writes:
            self._dep(op, b.lw, "waw")
            for r in b.rd.values():
                self._dep(op, r, "war")
            for r in b.rd_dma:
                self._dep(op, r, "war")
        for b in writes:
            b.lw = op
            b.rd = {}
            b.rd_dma = []
        for b in reads:
            if b.const or b in writes:
                continue
            if op.dma is not None:
                b.rd_dma.append(op)
            else:
                b.rd[op.eng] = op

    def add(self, eng, fn, r=(), w=()):
        op = Op(eng, fn)
        p = self.pending.pop(eng, None)
        if p is not None:
            op.deps.append(p)
        self._track(op, r, w)
        self.ops[eng].append(op)
        return op

    def fence(self, pairs):
        lasts = [self.ops[e][-1] for e in self.ENGS if self.ops[e]]
        toks = [t.lw for t in self.tokens if t.lw is not None]
        op = self.dma("sp", pairs, self.fence_tok)
        for l in lasts + toks:
            if l is op:
                continue
            if l.dma is None:
                l.sig = True
            op.deps.append(l)
        self.pending = {e: op for e in self.ENGS}
        return op

    def dma(self, eng, pairs, token, r=(), w=()):
        op = Op(eng, None, dma=pairs)
        p = self.pending.pop(eng, None)
        if p is not None:
            op.deps.append(p)
        if token.sem is None:
            token.sem = self.stack.enter_context(self.nc.semaphore("dsem_" + token.name))
            self.nsem += 1
            self.tokens.append(token)
        if token not in w:
            w = tuple(w) + (token,)
        self._track(op, r, w)
        token.cum += 16 * len(pairs)
        op.sem = token.sem
        op.ticket = token.cum
        self.ops[eng].append(op)
        return op

    def emit(self):
        nc = self.nc
        for e in self.ENGS:
            n = 0
            for op in self.ops[e]:
                if op.dma is None and op.sig:
                    n += 1
                    op.sem = self.esem[e]
                    op.ticket = n
        handles = {"pe": nc.tensor, "act": nc.scalar, "dve": nc.vector, "pool": nc.gpsimd, "sp": nc.sync}

        def run(e, eng):
            waited = {}
            for op in self.ops[e]:
                for p in op.deps:
                    key = id(p.sem)
                    if waited.get(key, 0) >= p.ticket:
                        continue
                    waited[key] = p.ticket
                    eng.wait_ge(p.sem, p.ticket)
                if op.dma is not None:
                    for (o, i) in op.dma:
                        eng.dma_start(out=o, in_=i).then_inc(op.sem, 16)
                else:
                    ins = op.fn(eng)
                    if op.sig:
                        ins.then_inc(op.sem, 1)
            if e == "sp":
                for t in self.tokens:
                    if waited.get(id(t.sem), 0) < t.cum:
                        eng.wait_ge(t.sem, t.cum)

        with nc.Block() as block:
            @block.tensor
            def _(t):
                run("pe", t)

            @block.scalar
            def _(s):
                run("act", s)

            @block.vector
            def _(v):
                run("dve", v)

            @block.gpsimd
            def _(g):
                run("pool", g)

            @block.sync
            def _(sy):
                run("sp", sy)


def I(method, *args, **kw):
    return lambda e: getattr(e, method)(*args, **kw)


class Arena:
    def __init__(self, t, nbytes):
        self.t = t
        self.n = nbytes
        self.top = 0

    def alloc(self, dtype, shape):
        nfree = int(np.prod(shape[1:]))
        nb = nfree * ESZ[dtype]
        off = self.top
        self.top += (nb + 63) // 64 * 64
        assert self.top <= self.n, f"SBUF arena overflow {self.top} > {self.n}"
        ap = self.t[0:shape[0], off:off + nb].bitcast(dtype)
        if len(shape) == 3:
            ap = ap.rearrange("p (a b) -> p a b", b=shape[2])
        elif len(shape) == 4:
            ap = ap.rearrange("p (a b c) -> p a b c", b=shape[2], c=shape[3])
        return ap


class Ring:
    def __init__(self, arena, name, n, dtype, shape):
        self.n = n
        self.v = [arena.alloc(dtype, shape) for _ in range(n)]
        self.b = [Buf(f"{name}{i}") for i in range(n)]
        self.i = 0

    def next(self):
        k = self.i % self.n
        self.i += 1
        return self.v[k], self.b[k]


def bcast_rows(dram_ap_2d, ncols, nparts=128, col0=0):
    return bass.AP(dram_ap_2d.tensor, dram_ap_2d.offset + col0, [[0, nparts], [1, ncols]])


def build_program(seqs, stop_after=4, debug=False):
    TOK = sum(seqs)
    assert TOK % 512 == 0
    nc = bass.Bass("TRN2", target_bir_lowering=False)
    stack = ExitStack()
    with stack:
        def din(name, shape, dt=F32):
            return nc.dram_tensor(name, list(shape), dt, kind="ExternalInput").ap()

        okind = "ExternalOutput" if debug else "Internal"

        def dscr(name, shape, dt):
            return nc.dram_tensor(name, list(shape), dt, kind=okind).ap()

        x_d = din("x", [TOK, 1024])
        norm1_d = din("norm1", [1, 1024])
        w_in_d = din("w_in", [1024, 5120])
        qna_d = din("q_norm_a", [1, 64])
        kna_d = din("k_norm_a", [1, 64])
        sink_d = din("sink_a", [1, 8])
        qnb_d = din("q_norm_b", [1, 64])
        knb_d = din("k_norm_b", [1, 64])
        woa_d = din("w_o_a", [512, 1024])
        wob_d = din("w_o_b", [256, 1024])
        wout_d = din("w_out", [1024, 1024])
        norm2_d = din("norm2", [1, 1024])
        wq_d = din("w_query", [1024, 2048])
        sk1_d = din("sub_keys_1", [8, 128, 128])
        sk2_d = din("sub_keys_2", [8, 128, 128])
        eu_d = din("expert_u", [16384, 1024])
        ev_d = din("expert_v", [16384, 1024])
        rope_d = din("c_rope", [4096, 16])
        mask_d = din("c_mask", [2, 128, 384], BF16)
        identb_d = din("c_identb", [128, 128], BF16)
        identf_d = din("c_identf", [128, 128])
        iota_d = din("c_iota", [128, 128])
        y_d = nc.dram_tensor("y", [TOK, 1024], F32, kind="ExternalOutput").ap()

        XNT = dscr("s_xnt", [8, 128, TOK], BF16)
        NUM = dscr("s_num", [5, TOK, 260], F32)
        X2 = dscr("s_x2", [TOK, 1024], F32)
        XN2T = dscr("s_xn2t", [8, 128, TOK], BF16)
        UV = dscr("s_uv", [NCH, 128, 2048], BF16)
        SC = dscr("s_sc", [TOK, 2048], F32)

        SB_BYTES = 205 * 1024
        arena_t = stack.enter_context(nc.sbuf_tensor("arena", [128, SB_BYTES], U8))
        psb = [stack.enter_context(nc.psum_tensor(f"psb{i}", [128, 512], F32)) for i in range(8)]
        PSB = [Buf(f"psb{i}") for i in range(8)]
        P = Prog(nc, stack)
        A = Arena(arena_t, SB_BYTES)

        def ps_f32(i):
            return psb[i][:, :]

        def ps_bf16(i):
            return psb[i][:, :].bitcast(BF16)

        st_tok = [Buf(f"st{i}") for i in range(8)]
        st_i = [0]

        def store(eng, pairs, r):
            t = st_tok[st_i[0] % len(st_tok)]
            st_i[0] += 1
            return P.dma(eng, pairs, t, r=r)

        dbg_n = [0]

        def dbg(name, ap, b):
            if not debug:
                return
            shp = list(ap.shape)
            dt_ = ap.dtype
            d = nc.dram_tensor("dbg_" + name, shp, dt_, kind="ExternalOutput").ap()
            store("sp", [(d, ap)], r=[b])

        identb = A.alloc(BF16, [128, 128]); identb_b = Buf("identb", const=True)
        identf = A.alloc(F32, [128, 128]); identf_b = Buf("identf", const=True)
        P.dma("sp", [(identb, identb_d)], identb_b, w=[identb_b])
        P.dma("sp", [(identf, identf_d)], identf_b, w=[identf_b])
        epsc = A.alloc(F32, [128, 4]); epsc_b = Buf("epsc", const=True)
        P.add("pool", I("memset", epsc, EPS), w=[epsc_b])
        pers_top = A.top

        FZ = dscr("s_fence", [2, 16], F32)

        def fence():
            P.fence([(FZ[1:2, :], identf_d[0:1, 0:16])])
            A.top = pers_top

        def rms_tile(xt, xt_b, gb, gb_b, xn, xn_b, junk, junk_b, ss, ss_b):
            P.add("act", I("activation", out=junk, in_=xt, func=AF.Square, accum_out=ss[:, 0:1]),
                  r=[xt_b], w=[junk_b, ss_b])
            P.add("act", I("activation", out=ss[:, 1:2], in_=ss[:, 0:1], func=AF.Ln, scale=1.0 / 1024, bias=epsc[:, 0:1]),
                  r=[ss_b, epsc_b], w=[ss_b])
            P.add("act", I("activation", out=ss[:, 2:3], in_=ss[:, 1:2], func=AF.Exp, scale=-0.5), r=[ss_b], w=[ss_b])
            P.add("dve", I("scalar_tensor_tensor", out=xn, in0=xt, scalar=ss[:, 2:3], in1=gb,
                                                          op0=ALU.mult, op1=ALU.mult), r=[xt_b, ss_b, gb_b], w=[xn_b])

        def transpose8(src, src_b, dst_fn, dst_b, bank, evac_eng):
            pv = ps_bf16(bank)
            for do in range(8):
                P.add("pe", I("transpose", out=pv[:, do * 128:(do + 1) * 128],
                                                         in_=src[:, do * 128:(do + 1) * 128], identity=identb),
                      r=[src_b, identb_b], w=[PSB[bank]])
            pv3 = pv[:, 0:1024].rearrange("p (a b) -> p a b", b=128)
            if evac_eng == "act":
                P.add("act", I("copy", out=dst_fn, in_=pv3), r=[PSB[bank]], w=[dst_b])
            else:
                P.add("dve", I("tensor_copy", out=dst_fn, in_=pv3), r=[PSB[bank]], w=[dst_b])

        def uv_prepass_chunk(c, uvr, ubr):
            eu3 = eu_d.rearrange("(a b) d -> a b d", b=128)
            ev3 = ev_d.rearrange("(a b) d -> a b d", b=128)
            ub, ub_b = ubr.next()
            P.dma("pool", [(ub, eu3[:, c, :])], ub_b, w=[ub_b])
            uv, uv_b = uvr.next()
            P.dma("pool", [(uv[:, 1024:2048], ev3[:, c, :])], uv_b, w=[uv_b])
            bank = 6 + c % 2
            pv = ps_bf16(bank)
            for do in range(8):
                P.add("pe", I("transpose", out=pv[:, do * 128:(do + 1) * 128], in_=ub[:, do * 128:(do + 1) * 128],
                               identity=identb), r=[ub_b, identb_b], w=[PSB[bank]])
            P.add("dve", I("tensor_copy", out=uv[:, 0:1024], in_=pv[:, 0:1024]), r=[PSB[bank], uv_b], w=[uv_b])
            store("sp", [(UV[c], uv)], r=[uv_b])

        def phase1():
            uvr1 = Ring(A, "uv1", 3, BF16, [128, 2048])
            ubr1 = Ring(A, "ub1", 2, BF16, [128, 1024])
            uvc = [0]
            g1 = A.alloc(F32, [128, 1024]); g1_b = Buf("g1", const=True)
            P.dma("sp", [(g1, bcast_rows(norm1_d, 1024))], g1_b, w=[g1_b])
            xr = Ring(A, "p1x", 3, F32, [128, 1024])
            xnr = Ring(A, "p1xn", 2, BF16, [128, 1024])
            jr = Ring(A, "p1j", 1, BF16, [128, 1024])
            ssr = Ring(A, "p1ss", 4, F32, [128, 4])
            gr = Ring(A, "p1g", 2, BF16, [128, 8, 512])
            nt = TOK // 128
            loads = {}

            def load(i):
                xt, xt_b = xr.next()
                P.dma("sp", [(xt, x_d[i * 128:(i + 1) * 128, :])], xt_b, w=[xt_b])
                loads[i] = (xt, xt_b)

            load(0)
            if nt > 1:
                load(1)
            grp = None
            for i in range(nt):
                if i + 2 < nt:
                    load(i + 2)
                xt, xt_b = loads.pop(i)
                xn, xn_b = xnr.next()
                junk, junk_b = jr.next()
                ss, ss_b = ssr.next()
                rms_tile(xt, xt_b, g1, g1_b, xn, xn_b, junk, junk_b, ss, ss_b)
                j = i % 4
                if j == 0:
                    grp = gr.next()
                transpose8(xn, xn_b, grp[0][:, :, j * 128:(j + 1) * 128], grp[1], i % 2, "act")
                if j == 3:
                    t0 = (i - 3) * 128
                    store("sp", [(XNT[:, :, t0:t0 + 512].rearrange("o i t -> i o t"), grp[0])], r=[grp[1]])
                while uvc[0] < NCH and uvc[0] * nt < (i + 1) * NCH:
                    uv_prepass_chunk(uvc[0], uvr1, ubr1)
                    uvc[0] += 1

        def qk_norm_rope(ps_ap, ps_b, H, g_ap, g_b, tab, tab_b, out_bf, out_b, work):
            sq, sq_b, st, st_b, xn, xn_b, tmp, tmp_b = work
            H64 = H * 64
            P.add("act", I("activation", out=sq[:, 0:H64], in_=ps_ap, func=AF.Square), r=[ps_b], w=[sq_b])
            yield
            sq3 = sq[:, 0:H64].rearrange("p (h d) -> p h d", d=64)
            P.add("dve", I("tensor_reduce", out=st[:, 0:H], in_=sq3, axis=AX.X, op=ALU.add), r=[sq_b], w=[st_b])
            yield
            P.add("act", I("activation", out=st[:, 4:4 + H], in_=st[:, 0:H], func=AF.Ln, scale=1.0 / 64, bias=epsc[:, 0:1]),
                  r=[st_b, epsc_b], w=[st_b])
            yield
            P.add("act", I("activation", out=st[:, 8:8 + H], in_=st[:, 4:4 + H], func=AF.Exp, scale=-0.5), r=[st_b], w=[st_b])
            yield
            ps3 = ps_ap.rearrange("p (h d) -> p h d", d=64)
            xn3 = xn[:, 0:H64].rearrange("p (h d) -> p h d", d=64)
            rs_bc = st[:, 8:8 + H].unsqueeze(2).to_broadcast([128, H, 64])
            P.add("dve", I("tensor_tensor", out=xn3, in0=ps3, in1=rs_bc, op=ALU.mult), r=[ps_b, st_b], w=[xn_b])
            yield
            g_bc = g_ap.unsqueeze(1).to_broadcast([128, H, 64])
            P.add("dve", I("tensor_tensor", out=xn3, in0=xn3, in1=g_bc, op=ALU.mult), r=[xn_b, g_b], w=[xn_b])
            yield
            P.add("act", I("copy", out=out_bf, in_=xn3), r=[xn_b], w=[out_b])
            yield
            cos_bc = tab[:, 0:8].unsqueeze(1).to_broadcast([128, H, 8])
            sin_bc = tab[:, 8:16].unsqueeze(1).to_broadcast([128, H, 8])
            x1 = xn3[:, :, 0:8]
            x2 = xn3[:, :, 8:16]
            t3 = tmp[:, 0:4 * H * 8].rearrange("p (k h d) -> p k h d", k=4, d=8)
            P.add("dve", I("tensor_tensor", out=t3[:, 0], in0=x1, in1=cos_bc, op=ALU.mult), r=[xn_b, tab_b], w=[tmp_b])
            yield
            P.add("dve", I("tensor_tensor", out=t3[:, 1], in0=x2, in1=sin_bc, op=ALU.mult), r=[xn_b, tab_b], w=[tmp_b])
            yield
            P.add("pool", I("tensor_tensor", out=t3[:, 2], in0=x2, in1=cos_bc, op=ALU.mult), r=[xn_b, tab_b], w=[tmp_b])
            yield
            P.add("pool", I("tensor_tensor", out=t3[:, 3], in0=x1, in1=sin_bc, op=ALU.mult), r=[xn_b, tab_b], w=[tmp_b])
            yield
            P.add("dve", I("tensor_tensor", out=out_bf[:, :, 0:8], in0=t3[:, 0], in1=t3[:, 1], op=ALU.subtract),
                  r=[tmp_b, out_b], w=[out_b])
            yield
            P.add("dve", I("tensor_tensor", out=out_bf[:, :, 8:16], in0=t3[:, 2], in1=t3[:, 3], op=ALU.add),
                  r=[tmp_b, out_b], w=[out_b])
            yield

        def phase2():
            wqkv = A.alloc(BF16, [128, 8, 3072]); wqkv_b = Buf("wqkv", const=True)
            w3 = w_in_d.rearrange("(o i) c -> i o c", i=128)
            P.dma("pool", [(wqkv[:, do, :], w3[:, do, 0:3072]) for do in range(8)], wqkv_b, w=[wqkv_b])
            gains = A.alloc(F32, [128, 4, 64]); gains_b = Buf("gains", const=True)
            P.dma("sp", [(gains[:, 0, :], bcast_rows(qna_d, 64)), (gains[:, 1, :], bcast_rows(kna_d, 64)),
                         (gains[:, 2, :], bcast_rows(qnb_d, 64)), (gains[:, 3, :], bcast_rows(knb_d, 64))],
                  gains_b, w=[gains_b])
            P.add("act", I("mul", out=gains[:, 0, :], in_=gains[:, 0, :], mul=0.125), r=[gains_b], w=[gains_b])
            P.add("act", I("mul", out=gains[:, 2, :], in_=gains[:, 2, :], mul=0.125), r=[gains_b], w=[gains_b])
            masks = A.alloc(BF16, [128, 2, 384]); masks_b = Buf("masks", const=True)
            P.dma("sp", [(masks[:, 0, :], mask_d[0]), (masks[:, 1, :], mask_d[1])], masks_b, w=[masks_b])
            SMAX = max(seqs)
            xnT = A.alloc(BF16, [128, 8, SMAX]); xnT_b = Buf("xnT")
            NTMAX = SMAX // 128
            KT = A.alloc(BF16, [128, 2, SMAX])
            KT_b = [Buf(f"KT{a}") for a in range(NTMAX)]
            VA = A.alloc(BF16, [128, NTMAX, 4, 65])
            VA_b = [Buf(f"VA{a}") for a in range(NTMAX)]
            VA1_b = Buf("VAones")
            P.add("pool", I("memset", VA[:, :, :, 64:65], 1.0), w=[VA1_b] + VA_b)
            tabr = Ring(A, "tab", 6, F32, [128, 16])
            sqr = Ring(A, "sq", 4, F32, [128, 256])
            str_ = Ring(A, "st", 4, F32, [128, 12])
            xnr = Ring(A, "xnq", 4, F32, [128, 256])
            tmpr = Ring(A, "tmpq", 4, F32, [128, 128])
            kbr = Ring(A, "kb", 4, BF16, [128, 4, 64])
            qbr = Ring(A, "qb", 4, BF16, [128, 4, 64])
            QTr = Ring(A, "QT", 4, BF16, [128, 2, 128])
            ptr = Ring(A, "pt", 6, BF16, [128, 384])
            numr = Ring(A, "num", 4, F32, [128, 260])
            SBANK = [dict(proj=0, tr=2, sc=3, num=5), dict(proj=1, tr=7, sc=4, num=6)]

            def work():
                sq, sq_b = sqr.next(); st, st_b = str_.next(); xn, xn_b = xnr.next(); tmp, tmp_b = tmpr.next()
                return (sq, sq_b, st, st_b, xn, xn_b, tmp, tmp_b)

            def yield_each(n0):
                return sum(len(v) for v in P.ops.values())

            def load_tab(r, D, a):
                tab, tab_b = tabr.next()
                st0 = r + D * 128 * a
                P.dma("sp", [(tab, rope_d[st0:st0 + D * 127 + 1:D, :])], tab_b, w=[tab_b])
                return tab, tab_b

            def kv_task(g, r, a, slot, gk, sidx):
                D, HK = g["D"], g["HK"]
                isA = HK == 1
                bk = SBANK[sidx]
                tab, tab_b = load_tab(r, D, a)
                bank = bk["proj"]
                kv = ps_f32(bank)
                nk = HK * 64
                st0 = r + D * 128 * a
                ts = slice(st0, st0 + D * 127 + 1, D)
                for do in range(8):
                    P.add("pe", I("matmul", kv[:, 0:nk], xnT[:, do, ts], wqkv[:, do, g["kcol"]:g["kcol"] + nk],
                                   start=(do == 0), stop=(do == 7)), r=[xnT_b, wqkv_b], w=[PSB[bank]])
                yield
                for do in range(8):
                    P.add("pe", I("matmul", kv[:, 256:256 + nk], xnT[:, do, ts], wqkv[:, do, g["vcol"]:g["vcol"] + nk],
                                   start=(do == 0), stop=(do == 7)), r=[xnT_b, wqkv_b], w=[PSB[bank]])
                yield
                kb, kb_b = kbr.next()
                yield from qk_norm_rope(kv[:, 0:nk], PSB[bank], HK, gk, gains_b, tab, tab_b, kb[:, 0:HK, :], kb_b, work())
                if isA:
                    P.add("dve", I("tensor_copy", out=kb[:, 1, :], in_=kb[:, 0, :]), r=[kb_b], w=[kb_b])
                P.add("act", I("copy", out=VA[:, slot, 0:HK, 0:64], in_=kv[:, 256:256 + nk].rearrange("p (h d) -> p h d", d=64)),
                      r=[PSB[bank]], w=[VA_b[slot]])
                yield
                npair = 1 if isA else 2
                tb_ = bk["tr"]
                pv = ps_bf16(tb_)
                for p_ in range(npair):
                    P.add("pe", I("transpose", out=pv[:, p_ * 128:(p_ + 1) * 128],
                                   in_=kb[:, 2 * p_:2 * p_ + 2, :].rearrange("p h d -> p (h d)"), identity=identb),
                          r=[kb_b, identb_b], w=[PSB[tb_]])
                yield
                P.add("dve", I("tensor_copy", out=KT[:, 0:npair, slot * 128:(slot + 1) * 128],
                               in_=pv[:, 0:npair * 128].rearrange("p (a b) -> p a b", b=128)), r=[PSB[tb_]], w=[KT_b[slot]])
                yield

            def q_task(g, gi, s0, r, a, nt, slots, gq, mk, sidx):
                D, HK = g["D"], g["HK"]
                isA = HK == 1
                bk = SBANK[sidx]
                tab, tab_b = load_tab(r, D, a)
                bank = bk["proj"]
                qp = ps_f32(bank)
                st0 = r + D * 128 * a
                ts = slice(st0, st0 + D * 127 + 1, D)
                for do in range(8):
                    P.add("pe", I("matmul", qp[:, 0:256], xnT[:, do, ts], wqkv[:, do, g["qcol"]:g["qcol"] + 256],
                                   start=(do == 0), stop=(do == 7)), r=[xnT_b, wqkv_b], w=[PSB[bank]])
                yield
                qb, qb_b = qbr.next()
                yield from qk_norm_rope(qp[:, 0:256], PSB[bank], 4, gq, gains_b, tab, tab_b, qb, qb_b, work())
                tb_ = bk["tr"]
                pv = ps_bf16(tb_)
                for p_ in range(2):
                    P.add("pe", I("transpose", out=pv[:, p_ * 128:(p_ + 1) * 128],
                                   in_=qb[:, 2 * p_:2 * p_ + 2, :].rearrange("p h d -> p (h d)"), identity=identb),
                          r=[qb_b, identb_b], w=[PSB[tb_]])
                yield
                QT, QT_b = QTr.next()
                P.add("dve", I("tensor_copy", out=QT, in_=pv[:, 0:256].rearrange("p (a b) -> p a b", b=128)), r=[PSB[tb_]], w=[QT_b])
                yield
                blocks = [b for b in (a - 1, a, a + 1) if 0 <= b < nt]
                nb = len(blocks)
                moff = (blocks[0] - (a - 1)) * 128
                nbank = bk["num"]
                nps = ps_f32(nbank)
                sbank = bk["sc"]
                sps = ps_f32(sbank)
                for h in range(4):
                    kh = 0 if isA else h
                    bp = (h % 2) * 64
                    pair = h // 2
                    kpair = 0 if isA else kh // 2
                    for bi, b in enumerate(blocks):
                        sl_ = slots[b]
                        P.add("pe", I("matmul", sps[:, bi * 128:(bi + 1) * 128], KT[bp:bp + 64, kpair, sl_ * 128:(sl_ + 1) * 128],
                                       QT[bp:bp + 64, pair, :], start=True, stop=True), r=[KT_b[sl_], QT_b], w=[PSB[sbank]])
                    yield
                    pt, pt_b = ptr.next()
                    P.add("act", I("activation", out=pt[:, 0:nb * 128], in_=sps[:, 0:nb * 128], func=AF.Exp), r=[PSB[sbank]], w=[pt_b])
                    yield
                    P.add("dve", I("tensor_tensor", out=pt[:, 0:nb * 128], in0=pt[:, 0:nb * 128], in1=mk[:, moff:moff + nb * 128],
                                   op=ALU.mult), r=[pt_b, masks_b], w=[pt_b])
                    yield
                    for bi, b in enumerate(blocks):
                        sl_ = slots[b]
                        P.add("pe", I("matmul", nps[:, h * 65:(h + 1) * 65], pt[:, bi * 128:(bi + 1) * 128], VA[:, sl_, kh, :],
                                       start=(bi == 0), stop=(bi == nb - 1)), r=[pt_b, VA_b[sl_], VA1_b], w=[PSB[nbank]])
                    yield
                num, num_b = numr.next()
                P.add("act", I("copy", out=num, in_=nps[:, 0:260]), r=[PSB[nbank]], w=[num_b])
                yield
                st1 = s0 + r + D * 128 * a
                store("sp", [(NUM[gi, st1:st1 + D * 127 + 1:D, :], num)], r=[num_b])
                yield

            def load_task(S, s0, sidx):
                P.dma("sp", [(xnT[:, do, 0:S], XNT[do, :, s0:s0 + S]) for do in range(8)], xnT_b, w=[xnT_b])
                yield

            tasks = []
            s0 = 0
            base = 0
            for S in seqs:
                tasks.append(lambda sidx, S=S, s0=s0: load_task(S, s0, sidx))
                for gi, g in enumerate(GROUPS):
                    D, HK = g["D"], g["HK"]
                    nt = S // D // 128
                    isA = HK == 1
                    gq = gains[:, 0 if isA else 2, :]
                    gk = gains[:, 1 if isA else 3, :]
                    mk = masks[:, 0 if isA else 1, :]
                    for r in range(D):
                        slots = [(base + a) % NTMAX for a in range(nt)]
                        base += nt
                        for a in range(nt):
                            tasks.append(lambda sidx, g=g, r=r, a=a, sl=slots[a], gk=gk: kv_task(g, r, a, sl, gk, sidx))
                        for a in range(nt):
                            tasks.append(lambda sidx, g=g, gi=gi, s0=s0, r=r, a=a, nt=nt, slots=slots, gq=gq, mk=mk:
                                         q_task(g, gi, s0, r, a, nt, slots, gq, mk, sidx))
                s0 += S
            active = {}
            it = iter(tasks)
            done = False
            while True:
                while not done and len(active) < 2:
                    t = next(it, None)
                    if t is None:
                        done = True
                        break
                    sidx = 0 if 0 not in active else 1
                    active[sidx] = t(sidx)
                if not active:
                    break
                for sidx in list(active.keys()):
                    try:
                        next(active[sidx])
                    except StopIteration:
                        del active[sidx]

        def phase3():
            def wload(name, d_ap, nchunk, c0, ncol):
                t = A.alloc(BF16, [128, nchunk, ncol]); b = Buf(name, const=True)
                v = d_ap.rearrange("(o i) c -> i o c", i=128)
                P.dma("pool", [(t[:, o, :], v[:, o, c0:c0 + ncol]) for o in range(nchunk)], b, w=[b])
                return t, b
            woa, woa_b = wload("woa", woa_d, 4, 0, 1024)
            wob, wob_b = wload("wob", wob_d, 2, 0, 1024)
            wg, wg_b = wload("wg", w_in_d, 8, 3072, 2048)
            wout, wout_b = wload("wout", wout_d, 8, 0, 1024)
            wq, wq_b = wload("wq", wq_d, 8, 0, 2048)
            g2 = A.alloc(F32, [128, 1024]); g2_b = Buf("g2", const=True)
            P.dma("sp", [(g2, bcast_rows(norm2_d, 1024))], g2_b, w=[g2_b])
            esink = A.alloc(F32, [128, 8]); esink_b = Buf("esink", const=True)
            P.dma("sp", [(esink, bcast_rows(sink_d, 8))], esink_b, w=[esink_b])
            P.add("act", I("activation", out=esink, in_=esink, func=AF.Exp), r=[esink_b], w=[esink_b])
            n5r = Ring(A, "n5", 2, F32, [128, 5, 260])
            tBr = Ring(A, "tB", 2, F32, [128, 260])
            rAr = Ring(A, "rA", 2, F32, [128, 16])
            Or = Ring(A, "O", 2, BF16, [128, 768])
            oTr = Ring(A, "oT", 1, BF16, [128, 6, 512])
            xTr = Ring(A, "xT3", 1, BF16, [128, 8, 512])
            sgr = Ring(A, "sg", 2, F32, [128, 512])
            mr = Ring(A, "m12", 2, F32, [128, 512])
            mTr = Ring(A, "mT", 1, BF16, [128, 8, 512])
            xr = Ring(A, "x3", 1, F32, [128, 1024])
            x2r = Ring(A, "x23", 2, F32, [128, 1024])
            xn2r = Ring(A, "xn23", 1, BF16, [128, 1024])
            skT = A.alloc(F32, [128, 16, 128]); skT_b = Buf("skT", const=True)
            qTs = A.alloc(F32, [128, 16, 128]); qTs_b = Buf("qTs")
            scs = A.alloc(F32, [128, 16, 128]); scs_b = Buf("scs")
            P.dma("sp", [(qTs[:, 2 * h, :], sk1_d[h]) for h in range(8)] + [(qTs[:, 2 * h + 1, :], sk2_d[h]) for h in range(8)],
                  qTs_b, w=[qTs_b])
            for rnd in range(4):
                bank = 4 + rnd % 2
                for j in range(4):
                    g_ = rnd * 4 + j
                    P.add("pe", I("transpose", out=ps_f32(bank)[:, j * 128:(j + 1) * 128], in_=qTs[:, g_, :], identity=identf),
                          r=[qTs_b, identf_b], w=[PSB[bank]])
                P.add("act", I("copy", out=skT[:, rnd * 4:(rnd + 1) * 4, :],
                                in_=ps_f32(bank).rearrange("p (a b) -> p a b", b=128)), r=[PSB[bank]], w=[skT_b])
            ev3 = [0]

            def evac3(out, in_, r, w):
                ev3[0] += 1
                if ev3[0] % 2:
                    P.add("act", I("copy", out=out, in_=in_), r=r, w=w)
                else:
                    P.add("dve", I("tensor_copy", out=out, in_=in_), r=r, w=w)
            jr = Ring(A, "j3", 1, BF16, [128, 1024])
            ssr = Ring(A, "ss3", 4, F32, [128, 4])
            gr = Ring(A, "g3", 2, BF16, [128, 8, 512])
            for T0 in range(0, TOK, 512):
                oT, oT_b = oTr.next()
                for j in range(4):
                    tok = T0 + 128 * j
                    n5, n5_b = n5r.next()
                    P.dma("sp", [(n5[:, g_, :], NUM[g_, tok:tok + 128, :]) for g_ in range(5)], n5_b, w=[n5_b])
                    tB, tB_b = tBr.next()
                    rA, rA_b = rAr.next()
                    O, O_b = Or.next()
                    P.add("dve", I("tensor_tensor", out=tB, in0=n5[:, 2, :], in1=n5[:, 3, :], op=ALU.add), r=[n5_b], w=[tB_b])
                    P.add("dve", I("tensor_tensor", out=tB, in0=tB, in1=n5[:, 4, :], op=ALU.add), r=[n5_b, tB_b], w=[tB_b])
                    A8 = n5[:, 0:2, :].rearrange("p g (h e) -> p (g h) e", e=65)
                    B4 = tB.rearrange("p (h e) -> p h e", e=65)
                    P.add("dve", I("tensor_tensor", out=rA[:, 0:8], in0=A8[:, :, 64], in1=esink, op=ALU.add),
                          r=[n5_b, esink_b], w=[rA_b])
                    P.add("dve", I("tensor_copy", out=rA[:, 8:12], in_=B4[:, :, 64]), r=[tB_b], w=[rA_b])
                    P.add("dve", I("reciprocal", out=rA[:, 0:12], in_=rA[:, 0:12]), r=[rA_b], w=[rA_b])
                    P.add("dve", I("tensor_tensor", out=O[:, 0:512].rearrange("p (h d) -> p h d", d=64), in0=A8[:, :, 0:64],
                                   in1=rA[:, 0:8].unsqueeze(2).to_broadcast([128, 8, 64]), op=ALU.mult), r=[n5_b, rA_b], w=[O_b])
                    P.add("dve", I("tensor_tensor", out=O[:, 512:768].rearrange("p (h d) -> p h d", d=64), in0=B4[:, :, 0:64],
                                   in1=rA[:, 8:12].unsqueeze(2).to_broadcast([128, 4, 64]), op=ALU.mult), r=[tB_b, rA_b, O_b], w=[O_b])
                    if debug and T0 == 0 and j == 0:
                        dbg("O", O, O_b)
                    bank = 0
                    pv = ps_bf16(bank)
                    for fc in range(6):
                        P.add("pe", I("transpose", out=pv[:, fc * 128:(fc + 1) * 128], in_=O[:, fc * 128:(fc + 1) * 128],
                                       identity=identb), r=[O_b, identb_b], w=[PSB[bank]])
                    P.add("act", I("copy", out=oT[:, :, j * 128:(j + 1) * 128],
                                    in_=pv[:, 0:768].rearrange("p (a b) -> p a b", b=128)), r=[PSB[bank]], w=[oT_b])
                xT, xT_b = xTr.next()
                P.dma("sp", [(xT[:, do, :], XNT[do, :, T0:T0 + 512]) for do in range(8)], xT_b, w=[xT_b])
                mT, mT_b = mTr.next()
                for dc in range(8):
                    bs = (dc % 2) * 4
                    dcs = slice(dc * 128, (dc + 1) * 128)
                    ya, yb, ga, gb = ps_f32(bs), ps_f32(bs + 1), ps_f32(bs + 2), ps_f32(bs + 3)
                    for fc in range(4):
                        P.add("pe", I("matmul", ya, woa[:, fc, dcs], oT[:, fc, :], start=(fc == 0), stop=(fc == 3)),
                              r=[woa_b, oT_b], w=[PSB[bs]])
                    for fc in range(2):
                        P.add("pe", I("matmul", yb, wob[:, fc, dcs], oT[:, 4 + fc, :], start=(fc == 0), stop=(fc == 1)),
                              r=[wob_b, oT_b], w=[PSB[bs + 1]])
                    for do in range(8):
                        P.add("pe", I("matmul", ga, wg[:, do, dcs], xT[:, do, :], start=(do == 0), stop=(do == 7)),
                              r=[wg_b, xT_b], w=[PSB[bs + 2]])
                    for do in range(8):
                        P.add("pe", I("matmul", gb, wg[:, do, 1024 + dc * 128:1024 + (dc + 1) * 128], xT[:, do, :],
                                       start=(do == 0), stop=(do == 7)), r=[wg_b, xT_b], w=[PSB[bs + 3]])
                    sga, sga_b = sgr.next()
                    sgb, sgb_b = sgr.next()
                    P.add("act", I("activation", out=sga, in_=ga, func=AF.Sigmoid), r=[PSB[bs + 2]], w=[sga_b])
                    P.add("act", I("activation", out=sgb, in_=gb, func=AF.Sigmoid), r=[PSB[bs + 3]], w=[sgb_b])
                    m1, m1_b = mr.next()
                    m2, m2_b = mr.next()
                    P.add("dve", I("tensor_tensor", out=m1, in0=ya, in1=sga, op=ALU.mult), r=[PSB[bs], sga_b], w=[m1_b])
                    P.add("dve", I("tensor_tensor", out=m2, in0=yb, in1=sgb, op=ALU.mult), r=[PSB[bs + 1], sgb_b], w=[m2_b])
                    P.add("pool", I("tensor_tensor", out=mT[:, dc, :], in0=m1, in1=m2, op=ALU.add), r=[m1_b, m2_b], w=[mT_b])
                grp = gr.next()
                for j in range(4):
                    tok = T0 + 128 * j
                    xt, xt_b = xr.next()
                    P.dma("sp", [(xt, x_d[tok:tok + 128, :])], xt_b, w=[xt_b])
                    x2t, x2t_b = x2r.next()
                    for hd in range(2):
                        bank = 1 + hd
                        ops_ = ps_f32(bank)
                        for dc in range(8):
                            P.add("pe", I("matmul", ops_, mT[:, dc, j * 128:(j + 1) * 128], wout[:, dc, hd * 512:(hd + 1) * 512],
                                           start=(dc == 0), stop=(dc == 7)), r=[mT_b, wout_b], w=[PSB[bank]])
                        P.add("dve", I("tensor_tensor", out=x2t[:, hd * 512:(hd + 1) * 512], in0=ops_,
                                       in1=xt[:, hd * 512:(hd + 1) * 512], op=ALU.add), r=[PSB[bank], xt_b], w=[x2t_b])
                    store("sp", [(X2[tok:tok + 128, :], x2t)], r=[x2t_b])
                    xn2, xn2_b = xn2r.next()
                    junk, junk_b = jr.next()
                    ss, ss_b = ssr.next()
                    rms_tile(x2t, x2t_b, g2, g2_b, xn2, xn2_b, junk, junk_b, ss, ss_b)
                    transpose8(xn2, xn2_b, grp[0][:, :, j * 128:(j + 1) * 128], grp[1], 3, "act")
                    for qc in range(16):
                        bank = 4 + qc % 2
                        qps = ps_f32(bank)[:, 0:128]
                        for do in range(8):
                            P.add("pe", I("matmul", qps, wq[:, do, qc * 128:(qc + 1) * 128], grp[0][:, do, j * 128:(j + 1) * 128],
                                           start=(do == 0), stop=(do == 7)), r=[wq_b, grp[1]], w=[PSB[bank]])
                        evac3(qTs[:, qc, :], qps, [PSB[bank]], [qTs_b])
                    for rnd in range(4):
                        bank = 6 + rnd % 2
                        sps = ps_f32(bank)
                        for jj in range(4):
                            g_ = rnd * 4 + jj
                            P.add("pe", I("matmul", sps[:, jj * 128:(jj + 1) * 128], qTs[:, g_, :], skT[:, g_, :], start=True, stop=True),
                                  r=[qTs_b, skT_b], w=[PSB[bank]])
                        evac3(scs[:, rnd * 4:(rnd + 1) * 4, :], sps.rearrange("p (a b) -> p a b", b=128), [PSB[bank]], [scs_b])
                    store("sp", [(SC[tok:tok + 128, :], scs.rearrange("p a b -> p (a b)"))], r=[scs_b])
                store("sp", [(XN2T[:, :, T0:T0 + 512].rearrange("o i t -> i o t"), grp[0])], r=[grp[1]])


        def phase4():
            PSBH = [Buf(f"psbh{i}") for i in range(4)]
            iota = A.alloc(F32, [128, 128]); iota_b = Buf("iota", const=True)
            P.dma("sp", [(iota, iota_d)], iota_b, w=[iota_b])
            Gr = Ring(A, "G", 2, BF16, [128, PT, 128])
            uvr = Ring(A, "uv", UVD, BF16, [128, 2048])
            xqr = Ring(A, "xq", 2, BF16, [128, 8, PT])
            sc = A.alloc(F32, [128, 16, 128]); sc_b = Buf("sc")
            sc2 = A.alloc(F32, [128, 16, 128]); sc2_b = Buf("sc2")
            cand = sc.rearrange("p a b -> p (a b)").rearrange("p (h c) -> p h c", c=256)
            cand2 = sc2.rearrange("p a b -> p (a b)").rearrange("p (h c) -> p h c", c=256)
            oh = sc.rearrange("p a b -> p (a b)").rearrange("p (h s i) -> p h s i", s=16, i=16)
            cand4 = oh
            vt = A.alloc(F32, [128, 16, 16]); vt_b = Buf("vt")
            ix = A.alloc(U32, [128, 16, 16]); ix_b = Buf("ix")
            ixf = A.alloc(F32, [128, 16, 16]); ixf_b = Buf("ixf")
            top = A.alloc(F32, [128, 8, 16]); top_b = Buf("top")
            ci = A.alloc(U32, [128, 8, 16]); ci_b = Buf("ci")
            hl = A.alloc(U32, [128, 2, 128]); hl_b = Buf("hl")
            hlf = A.alloc(F32, [128, 2, 128]); hlf_b = Buf("hlf")
            ex = A.alloc(F32, [128, 8, 16]); ex_b = Buf("ex")
            sm = A.alloc(F32, [128, 8]); sm_b = Buf("sm")
            e12w = A.alloc(F32, [128, 3, 128]); e12w_b = Buf("e12w")
            eTr = Ring(A, "eT", 1, F32, [128, 3, PT])
            scl = A.alloc(F32, [128, 16, 128]); scl_b = Buf("scl")
            A1r = Ring(A, "A1", 2, BF16, [128, GB, 128])
            B1r = Ring(A, "B1", 2, BF16, [128, GB, 128])
            gelr = Ring(A, "gel", 3, BF16, [128, PT])
            atr = Ring(A, "at", 3, BF16, [128, PT])
            x2r = Ring(A, "x24", 1, F32, [128, 1024])
            ev_i = [0]

            def evac(out, in_, r, w):
                ev_i[0] += 1
                if ev_i[0] % 2:
                    P.add("act", I("copy", out=out, in_=in_), r=r, w=w)
                else:
                    P.add("dve", I("tensor_copy", out=out, in_=in_), r=r, w=w)

            def prep(T0, u, xq, xq_b, eT, eT_b):
                us = slice(u * 128, (u + 1) * 128)
                if u == 0:
                    P.dma("pool", [(scl.rearrange("p a b -> p (a b)"), SC[T0:T0 + 128, :])], scl_b, w=[scl_b])
                    yield
                for g_ in range(16):
                    P.add("dve", I("max", out=vt[:, g_, 0:8], in_=scl[:, g_, :]), r=[scl_b], w=[vt_b])
                    P.add("dve", I("max_index", out=ix[:, g_, 0:8], in_max=vt[:, g_, 0:8], in_values=scl[:, g_, :]),
                          r=[scl_b, vt_b], w=[ix_b])
                    P.add("dve", I("match_replace", out=sc2[:, g_, :], in_to_replace=vt[:, g_, 0:8], in_values=scl[:, g_, :],
                                   imm_value=-1e30), r=[scl_b, vt_b], w=[sc2_b])
                    P.add("dve", I("max", out=vt[:, g_, 8:16], in_=sc2[:, g_, :]), r=[sc2_b], w=[vt_b])
                    P.add("dve", I("max_index", out=ix[:, g_, 8:16], in_max=vt[:, g_, 8:16], in_values=sc2[:, g_, :]),
                          r=[sc2_b, vt_b], w=[ix_b])
                    yield
                if u == 0:
                    P.dma("pool", [(scl.rearrange("p a b -> p (a b)"), SC[T0 + 128:T0 + 256, :])], scl_b, w=[scl_b])
                v4 = vt.rearrange("p (h two) s -> p h two s", two=2)
                P.add("dve", I("tensor_tensor", out=cand4, in0=v4[:, :, 0, :].unsqueeze(3).to_broadcast([128, 8, 16, 16]),
                               in1=v4[:, :, 1, :].unsqueeze(2).to_broadcast([128, 8, 16, 16]), op=ALU.add), r=[vt_b], w=[sc_b])
                for h in range(8):
                    P.add("dve", I("max", out=top[:, h, 0:8], in_=cand[:, h, :]), r=[sc_b], w=[top_b])
                    P.add("dve", I("max_index", out=ci[:, h, 0:8], in_max=top[:, h, 0:8], in_values=cand[:, h, :]),
                          r=[sc_b, top_b], w=[ci_b])
                    P.add("dve", I("match_replace", out=cand2[:, h, :], in_to_replace=top[:, h, 0:8], in_values=cand[:, h, :],
                                   imm_value=-1e30), r=[sc_b, top_b], w=[sc2_b])
                    P.add("dve", I("max", out=top[:, h, 8:16], in_=cand2[:, h, :]), r=[sc2_b], w=[top_b])
                    P.add("dve", I("max_index", out=ci[:, h, 8:16], in_max=top[:, h, 8:16], in_values=cand2[:, h, :]),
                          r=[sc2_b, top_b], w=[ci_b])
                    yield
                P.add("dve", I("tensor_tensor", out=ex, in0=top, in1=top[:, :, 0:1].to_broadcast([128, 8, 16]), op=ALU.subtract),
                      r=[top_b], w=[ex_b])
                P.add("act", I("activation", out=ex, in_=ex, func=AF.Exp), r=[ex_b], w=[ex_b])
                P.add("dve", I("tensor_reduce", out=sm, in_=ex, axis=AX.X, op=ALU.add), r=[ex_b], w=[sm_b])
                P.add("dve", I("reciprocal", out=sm, in_=sm), r=[sm_b], w=[sm_b])
                P.add("dve", I("tensor_tensor", out=e12w[:, 2, :].rearrange("p (h s) -> p h s", s=16), in0=ex,
                               in1=sm.unsqueeze(2).to_broadcast([128, 8, 16]), op=ALU.mult), r=[ex_b, sm_b], w=[e12w_b])
                cif = ci.rearrange("p h s -> p (h s)")
                P.add("dve", I("tensor_single_scalar", out=hl[:, 0, :], in_=cif, scalar=4, op=ALU.logical_shift_right),
                      r=[ci_b], w=[hl_b])
                P.add("dve", I("tensor_single_scalar", out=hl[:, 1, :], in_=cif, scalar=15, op=ALU.bitwise_and),
                      r=[ci_b], w=[hl_b])
                P.add("dve", I("tensor_copy", out=hlf, in_=hl), r=[hl_b], w=[hlf_b])
                P.add("dve", I("tensor_copy", out=ixf, in_=ix), r=[ix_b], w=[ixf_b])
                yield
                ixf4 = ixf.rearrange("p (h two) s -> p h two s", two=2)
                io16 = iota[:, 0:16].unsqueeze(1).unsqueeze(1).to_broadcast([128, 8, 16, 16])
                for k in range(2):
                    sel = hlf[:, k, :].rearrange("p (h s) -> p h s", s=16).unsqueeze(3).to_broadcast([128, 8, 16, 16])
                    P.add("dve", I("tensor_tensor", out=oh, in0=sel, in1=io16, op=ALU.is_equal), r=[hlf_b, iota_b], w=[sc_b])
                    P.add("dve", I("tensor_tensor", out=oh, in0=oh, in1=ixf4[:, :, k, :].unsqueeze(2).to_broadcast([128, 8, 16, 16]),
                                   op=ALU.mult), r=[sc_b, ixf_b], w=[sc_b])
                    P.add("dve", I("tensor_reduce", out=e12w[:, k, :].rearrange("p (h s) -> p h s", s=16), in_=oh, axis=AX.X,
                                   op=ALU.add), r=[sc_b], w=[e12w_b])
                    yield
                if debug and T0 == 0 and u == 0:
                    dbg("e12w", e12w, e12w_b)
                    dbg("top", top, top_b)
                bank = 7
                for k in range(3):
                    P.add("pe", I("transpose", out=ps_f32(bank)[:, k * 128:(k + 1) * 128], in_=e12w[:, k, :], identity=identf),
                          r=[e12w_b, identf_b], w=[PSB[bank]])
                P.add("act", I("copy", out=eT[:, :, us], in_=ps_f32(bank)[:, 0:384].rearrange("p (a b) -> p a b", b=128)),
                      r=[PSB[bank]], w=[eT_b])
                yield

            gb_i = [0]
            iota_bf = A.alloc(BF16, [128, 128]); iota_bf_b = Buf("iota_bf", const=True)
            P.add("dve", I("tensor_copy", out=iota_bf, in_=iota), r=[iota_b], w=[iota_bf_b])

            def gbuild(eT, eT_b, G_all, G_b):
                nb_ = PT // GB
                stageA = {}

                def sA(bt):
                    t0 = bt * GB
                    A1, A1_b = A1r.next()
                    B1, B1_b = B1r.next()
                    for t in range(GB):
                        tt = t0 + t
                        P.add("dve", I("tensor_scalar", out=A1[:, t, :], in0=iota_bf, scalar1=eT[:, 0, tt:tt + 1], scalar2=None,
                                       op0=ALU.is_equal), r=[iota_bf_b, eT_b], w=[A1_b])
                        P.add("dve", I("tensor_scalar", out=B1[:, t, :], in0=iota_bf, scalar1=eT[:, 1, tt:tt + 1],
                                       scalar2=eT[:, 2, tt:tt + 1], op0=ALU.is_equal, op1=ALU.mult), r=[iota_bf_b, eT_b], w=[B1_b])
                        yield
                    stageA[bt] = (A1, A1_b, B1, B1_b)

                def sB(bt):
                    t0 = bt * GB
                    A1, A1_b, B1, B1_b = stageA.pop(bt)
                    for q4 in range(GB // 4):
                        bank = 7
                        for tt in range(4):
                            t = q4 * 4 + tt
                            P.add("pe", I("matmul", ps_f32(bank)[:, tt * 128:(tt + 1) * 128], A1[:, t, :], B1[:, t, :],
                                           start=True, stop=True), r=[A1_b, B1_b], w=[PSB[bank]])
                        tb = t0 + q4 * 4
                        P.add("act", I("copy", out=G_all[:, tb:tb + 4, :], in_=ps_f32(bank).rearrange("p (t k) -> p t k", k=128)),
                              r=[PSB[bank]], w=[G_b])
                        yield

                yield from sA(0)
                for bt in range(nb_):
                    if bt + 1 < nb_:
                        yield from sA(bt + 1)
                    yield from sB(bt)

            uvq = []

            def dense(T0, xq, xq_b, G_all, G_b, filler, last_tile):
                nfill = [0]
                uvs = {}
                for c in range(NCH):
                    if uvq:
                        uvs[c] = uvq.pop(0)
                    else:
                        break

                def load(c):
                    uv, uv_b = uvr.next()
                    P.dma("sp", [(uv, UV[c % NCH])], uv_b, w=[uv_b])
                    if c < NCH:
                        uvs[c] = (uv, uv_b)
                    elif not last_tile:
                        uvq.append((uv, uv_b))

                ats = {}

                def H(c):
                    uv, uv_b = uvs[c]
                    bank = 4 + c % 3
                    hps = ps_f32(bank)[:, 0:PT]
                    for do in range(8):
                        P.add("pe", I("matmul", hps, uv[:, do * 128:(do + 1) * 128], xq[:, do, :], start=(do == 0), stop=(do == 7)),
                              r=[uv_b, xq_b], w=[PSB[bank]])
                    gel, gel_b = gelr.next()
                    at, at_b = atr.next()
                    P.add("act", I("activation", out=gel, in_=hps, func=AF.Gelu), r=[PSB[bank]], w=[gel_b])
                    P.add("dve", I("tensor_tensor", out=at, in0=gel, in1=G_all[:, :, c], op=ALU.mult),
                          r=[gel_b, G_b], w=[at_b])
                    ats[c] = (at, at_b)

                def V(c):
                    uv, uv_b = uvs.pop(c)
                    at, at_b = ats.pop(c)
                    for u in range(2):
                        for hd in range(2):
                            bk = u * 2 + hd
                            P.add("pe", I("matmul", ps_f32(bk), at[:, u * 128:(u + 1) * 128], uv[:, 1024 + hd * 512:1024 + (hd + 1) * 512],
                                           start=(c == 0), stop=(c == NCH - 1)), r=[at_b, uv_b], w=[PSB[bk]])

                for c in range(len(uvs), UVD):
                    load(c)
                H(0); H(1)
                for c in range(NCH):
                    if c + 2 < NCH:
                        H(c + 2)
                    V(c)
                    if c + UVD < NCH or not last_tile:
                        load(c + UVD)
                    if filler is not None and c >= 2:
                        for _ in range(FILL_PER_CHUNK):
                            if next(filler, "end") == "end":
                                filler = None
                                break
                if filler is not None:
                    for _ in filler:
                        pass
                for u in range(2):
                    tok = T0 + u * 128
                    x2t, x2t_b = x2r.next()
                    P.dma("pool", [(x2t, X2[tok:tok + 128, :])], x2t_b, w=[x2t_b])
                    for hd in range(2):
                        bk = u * 2 + hd
                        P.add("dve", I("tensor_tensor", out=x2t[:, hd * 512:(hd + 1) * 512], in0=ps_f32(bk),
                                       in1=x2t[:, hd * 512:(hd + 1) * 512], op=ALU.add), r=[PSB[bk], x2t_b], w=[x2t_b])
                    store("sp", [(y_d[tok:tok + 128, :], x2t)], r=[x2t_b])

            def prep_tile(T0):
                xq, xq_b = xqr.next()
                eT, eT_b = eTr.next()
                P.dma("sp", [(xq[:, do, :], XN2T[do, :, T0:T0 + PT]) for do in range(8)], xq_b, w=[xq_b])
                st = dict(xq=xq, xq_b=xq_b, eT=eT, eT_b=eT_b)

                def gen():
                    for u in range(2):
                        yield from prep(T0, u, xq, xq_b, eT, eT_b)
                return st, gen()

            tiles = list(range(0, TOK, PT))

            def tile_gen(T0):
                st, g = prep_tile(T0)
                G_all, G_b = Gr.next()
                st["G"] = G_all
                st["G_b"] = G_b

                def gen():
                    yield from g
                    yield from gbuild(st["eT"], st["eT_b"], G_all, G_b)
                return st, gen()

            st, g0 = tile_gen(tiles[0])
            for _ in g0:
                pass
            for i, T0 in enumerate(tiles):
                if debug and i == 0:
                    dbg("eT", st["eT"], st["eT_b"])
                    dbg("G", st["G"][:, 0:8, :], st["G_b"])
                if i + 1 < len(tiles):
                    st2, g2 = tile_gen(tiles[i + 1])
                else:
                    st2, g2 = None, None
                dense(T0, st["xq"], st["xq_b"], st["G"], st["G_b"], g2, i + 1 == len(tiles))
                st = st2

        phase1()
        if stop_after >= 2:
            fence()
            phase2()
        if stop_after >= 3:
            fence()
            phase3()
        if stop_after >= 4:
            fence()
            phase4()
        P.emit()
    global LAST_PROG
    LAST_PROG = P
    return nc


def host_consts():
    half = 8
    inv = 500000.0 ** (-np.arange(0, 16, 2, dtype=np.float32) / 16)
    ang = np.arange(4096, dtype=np.float32)[:, None] * inv[None, :].astype(np.float32)
    rope = np.concatenate([np.cos(ang), np.sin(ang)], axis=1).astype(np.float32)
    j = np.arange(128)[:, None]
    i = np.arange(128)[None, :]
    m128 = np.concatenate([(j >= i), np.ones((128, 128), bool), (j <= i)], axis=1)
    m64 = np.concatenate([(j - i >= 64), (np.abs(i - j) <= 64), (i - j >= 64)], axis=1)
    masks = np.stack([m128, m64]).astype(np.float32).astype(ml_dtypes.bfloat16)
    return dict(
        c_rope=rope, c_mask=masks,
        c_identb=np.eye(128, dtype=np.float32).astype(ml_dtypes.bfloat16),
        c_identf=np.eye(128, dtype=np.float32),
        c_iota=np.tile(np.arange(128, dtype=np.float32)[None, :], (128, 1)),
    )


_PROG_CACHE = {}


def kernel(x_prompt, x_sample, norm1, w_in, q_norm_a, k_norm_a, sink_a, q_norm_b, k_norm_b,
           w_o_a, w_o_b, w_out, norm2, w_query, sub_keys_1, sub_keys_2, expert_u, expert_v):
    f = lambda a: np.ascontiguousarray(np.asarray(a, dtype=np.float32))
    x_prompt = f(x_prompt)
    x_sample = f(x_sample)
    shared = dict(
        norm1=f(norm1)[0:1], w_in=f(w_in)[0], q_norm_a=f(q_norm_a)[0:1], k_norm_a=f(k_norm_a)[0:1],
        sink_a=f(sink_a)[0:1], q_norm_b=f(q_norm_b)[0:1], k_norm_b=f(k_norm_b)[0:1],
        w_o_a=f(w_o_a)[0], w_o_b=f(w_o_b)[0], w_out=f(w_out)[0], norm2=f(norm2)[0:1],
        w_query=f(w_query)[0], sub_keys_1=f(sub_keys_1)[0], sub_keys_2=f(sub_keys_2)[0],
        expert_u=f(expert_u)[0], expert_v=f(expert_v)[0],
    )
    shared.update(host_consts())
    nc = build_program(FULL_SEQS)
    in_maps = []
    for c in range(NCORES):
        xp = x_prompt[4 * c:4 * c + 4].reshape(8192, 1024)
        xs = x_sample[2 * c:2 * c + 2].reshape(8192, 1024)
        m = dict(shared)
        m["x"] = np.concatenate([xp, xs], axis=0)
        in_maps.append(m)
    res = run_bass_kernel_spmd(nc, in_maps, core_ids=list(range(NCORES)))
    yp = np.empty((32, 2048, 1024), np.float32)
    ys = np.empty((16, 4096, 1024), np.float32)
    for c in range(NCORES):
        y = np.asarray(res.results[c]["y"], dtype=np.float32)
        yp[4 * c:4 * c + 4] = y[0:8192].reshape(4, 2048, 1024)
        ys[2 * c:2 * c + 2] = y[8192:16384].reshape(2, 4096, 1024)
    return (yp, ys)
```

```python
import numpy as np
import ml_dtypes
from contextlib import ExitStack
import concourse.bass as bass
import concourse.mybir as mybir
from concourse.bass_utils import run_bass_kernel_spmd

F32 = mybir.dt.float32
BF16 = mybir.dt.bfloat16
U32 = mybir.dt.uint32
U8 = mybir.dt.uint8
AF = mybir.ActivationFunctionType
ALU = mybir.AluOpType
AX = mybir.AxisListType
ESZ = {F32: 4, BF16: 2, U32: 4, U8: 1}

D_MODEL = 1024
EPS = 1e-6
NCORES = 8
FULL_SEQS = [2048] * 4 + [4096] * 2
GROUPS = [
    dict(name="A0", qcol=0, kcol=512, vcol=640, HK=1, w=128, D=1),
    dict(name="A1", qcol=256, kcol=576, vcol=704, HK=1, w=128, D=1),
    dict(name="B0", qcol=768, kcol=1536, vcol=2304, HK=4, w=64, D=1),
    dict(name="B1", qcol=1024, kcol=1792, vcol=2560, HK=4, w=64, D=4),
    dict(name="B2", qcol=1280, kcol=2048, vcol=2816, HK=4, w=64, D=16),
]
DBG_G = 2
NCH = 128
PT = 256
FILL_PER_CHUNK = 3
GB = 8
UVD = 4


class Buf:
    __slots__ = ("name", "lw", "rd", "rd_dma", "sem", "cum", "const")

    def __init__(self, name, const=False):
        self.name = name
        self.lw = None
        self.rd = {}
        self.rd_dma = []
        self.sem = None
        self.cum = 0
        self.const = const


class Op:
    __slots__ = ("eng", "fn", "deps", "sig", "sem", "ticket", "dma")

    def __init__(self, eng, fn, dma=None):
        self.eng = eng
        self.fn = fn
        self.deps = []
        self.sig = False
        self.sem = None
        self.ticket = 0
        self.dma = dma


class Prog:
    ENGS = ("pe", "act", "dve", "pool", "sp")

    def __init__(self, nc, stack):
        self.nc = nc
        self.stack = stack
        self.ops = {e: [] for e in self.ENGS}
        self.esem = {e: stack.enter_context(nc.semaphore("sem_" + e)) for e in self.ENGS}
        self.tokens = []
        self.nsem = len(self.ENGS)
        self.pending = {}
        self.fence_tok = Buf("fence")

    def _dep(self, op, prod, kind):
        if prod is None or prod is op:
            return
        if prod.dma is None and op.dma is None and prod.eng == op.eng and kind != "raw" and op.eng != "pool":
            return
        op.deps.append(prod)
        prod.sig = True

    def _track(self, op, reads, writes):
        for b in reads:
            self._dep(op, b.lw, "raw")
        for b in writes:
            self._dep(op, b.lw, "waw")
            for r in b.rd.values():
                self._dep(op, r, "war")
            for r in b.rd_dma:
                self._dep(op, r, "war")
        for b in writes:
            b.lw = op
            b.rd = {}
            b.rd_dma = []
        for b in reads:
            if b.const or b in writes:
                continue
            if op.dma is not None:
                b.rd_dma.append(op)
            else:
                b.rd[op.eng] = op

    def add(self, eng, fn, r=(), w=()):
        op = Op(eng, fn)
        p = self.pending.pop(eng, None)
        if p is not None:
            op.deps.append(p)
        self._track(op, r, w)
        self.ops[eng].append(op)
        return op

    def fence(self, pairs):
        lasts = [self.ops[e][-1] for e in self.ENGS if self.ops[e]]
        toks = [t.lw for t in self.tokens if t.lw is not None]
        op = self.dma("sp", pairs, self.fence_tok)
        for l in lasts + toks:
            if l is op:
                continue
            if l.dma is None:
                l.sig = True
            op.deps.append(l)
        self.pending = {e: op for e in self.ENGS}
        return op

    def dma(self, eng, pairs, token, r=(), w=()):
        op = Op(eng, None, dma=pairs)
        p = self.pending.pop(eng, None)
        if p is not None:
            op.deps.append(p)
        if token.sem is None:
            token.sem = self.stack.enter_context(self.nc.semaphore("dsem_" + token.name))
            self.nsem += 1
            self.tokens.append(token)
        if token not in w:
            w = tuple(w) + (token,)
        self._track(op, r, w)
        token.cum += 16 * len(pairs)
        op.sem = token.sem
        op.ticket = token.cum
        self.ops[eng].append(op)
        return op

    def emit(self):
        nc = self.nc
        for e in self.ENGS:
            n = 0
            for op in self.ops[e]:
                if op.dma is None and op.sig:
                    n += 1
                    op.sem = self.esem[e]
                    op.ticket = n
        handles = {"pe": nc.tensor, "act": nc.scalar, "dve": nc.vector, "pool": nc.gpsimd, "sp": nc.sync}

        def run(e, eng):
            waited = {}
            for op in self.ops[e]:
                for p in op.deps:
                    key = id(p.sem)
                    if waited.get(key, 0) >= p.ticket:
                        continue
                    waited[key] = p.ticket
                    eng.wait_ge(p.sem, p.ticket)
                if op.dma is not None:
                    for (o, i) in op.dma:
                        eng.dma_start(out=o, in_=i).then_inc(op.sem, 16)
                else:
                    ins = op.fn(eng)
                    if op.sig:
                        ins.then_inc(op.sem, 1)
            if e == "sp":
                for t in self.tokens:
                    if waited.get(id(t.sem), 0) < t.cum:
                        eng.wait_ge(t.sem, t.cum)

        with nc.Block() as block:
            @block.tensor
            def _(t):
                run("pe", t)

            @block.scalar
            def _(s):
                run("act", s)

            @block.vector
            def _(v):
                run("dve", v)

            @block.gpsimd
            def _(g):
                run("pool", g)

            @block.sync
            def _(sy):
                run("sp", sy)


def I(method, *args, **kw):
    return lambda e: getattr(e, method)(*args, **kw)


class Arena:
    def __init__(self, t, nbytes):
        self.t = t
        self.n = nbytes
        self.top = 0

    def alloc(self, dtype, shape):
        nfree = int(np.prod(shape[1:]))
        nb = nfree * ESZ[dtype]
        off = self.top
        self.top += (nb + 63) // 64 * 64
        assert self.top <= self.n, f"SBUF arena overflow {self.top} > {self.n}"
        ap = self.t[0:shape[0], off:off + nb].bitcast(dtype)
        if len(shape) == 3:
            ap = ap.rearrange("p (a b) -> p a b", b=shape[2])
        elif len(shape) == 4:
            ap = ap.rearrange("p (a b c) -> p a b c", b=shape[2], c=shape[3])
        return ap


class Ring:
    def __init__(self, arena, name, n, dtype, shape):
        self.n = n
        self.v = [arena.alloc(dtype, shape) for _ in range(n)]
        self.b = [Buf(f"{name}{i}") for i in range(n)]
        self.i = 0

    def next(self):
        k = self.i % self.n
        self.i += 1
        return self.v[k], self.b[k]


def bcast_rows(dram_ap_2d, ncols, nparts=128, col0=0):
    return bass.AP(dram_ap_2d.tensor, dram_ap_2d.offset + col0, [[0, nparts], [1, ncols]])


def build_program(seqs, stop_after=4, debug=False):
    TOK = sum(seqs)
    assert TOK % 512 == 0
    nc = bass.Bass("TRN2", target_bir_lowering=False)
    stack = ExitStack()
    with stack:
        def din(name, shape, dt=F32):
            return nc.dram_tensor(name, list(shape), dt, kind="ExternalInput").ap()

        okind = "ExternalOutput" if debug else "Internal"

        def dscr(name, shape, dt):
            return nc.dram_tensor(name, list(shape), dt, kind=okind).ap()

        x_d = din("x", [TOK, 1024])
        norm1_d = din("norm1", [1, 1024])
        w_in_d = din("w_in", [1024, 5120])
        qna_d = din("q_norm_a", [1, 64])
        kna_d = din("k_norm_a", [1, 64])
        sink_d = din("sink_a", [1, 8])
        qnb_d = din("q_norm_b", [1, 64])
        knb_d = din("k_norm_b", [1, 64])
        woa_d = din("w_o_a", [512, 1024])
        wob_d = din("w_o_b", [256, 1024])
        wout_d = din("w_out", [1024, 1024])
        norm2_d = din("norm2", [1, 1024])
        wq_d = din("w_query", [1024, 2048])
        sk1_d = din("sub_keys_1", [8, 128, 128])
        sk2_d = din("sub_keys_2", [8, 128, 128])
        eu_d = din("expert_u", [16384, 1024])
        ev_d = din("expert_v", [16384, 1024])
        rope_d = din("c_rope", [4096, 16])
        mask_d = din("c_mask", [2, 128, 384], BF16)
        identb_d = din("c_identb", [128, 128], BF16)
        identf_d = din("c_identf", [128, 128])
        iota_d = din("c_iota", [128, 128])
        y_d = nc.dram_tensor("y", [TOK, 1024], F32, kind="ExternalOutput").ap()

        XNT = dscr("s_xnt", [8, 128, TOK], BF16)
        NUM = dscr("s_num", [5, TOK, 260], F32)
        X2 = dscr("s_x2", [TOK, 1024], F32)
        XN2T = dscr("s_xn2t", [8, 128, TOK], BF16)
        UV = dscr("s_uv", [NCH, 128, 2048], BF16)
        SC = dscr("s_sc", [TOK, 2048], F32)

        SB_BYTES = 205 * 1024
        arena_t = stack.enter_context(nc.sbuf_tensor("arena", [128, SB_BYTES], U8))
        psb = [stack.enter_context(nc.psum_tensor(f"psb{i}", [128, 512], F32)) for i in range(8)]
        PSB = [Buf(f"psb{i}") for i in range(8)]
        P = Prog(nc, stack)
        A = Arena(arena_t, SB_BYTES)

        def ps_f32(i):
            return psb[i][:, :]

        def ps_bf16(i):
            return psb[i][:, :].bitcast(BF16)

        st_tok = [Buf(f"st{i}") for i in range(8)]
        st_i = [0]

        def store(eng, pairs, r):
            t = st_tok[st_i[0] % len(st_tok)]
            st_i[0] += 1
            return P.dma(eng, pairs, t, r=r)

        dbg_n = [0]

        def dbg(name, ap, b):
            if not debug:
                return
            shp = list(ap.shape)
            dt_ = ap.dtype
            d = nc.dram_tensor("dbg_" + name, shp, dt_, kind="ExternalOutput").ap()
            store("sp", [(d, ap)], r=[b])

        identb = A.alloc(BF16, [128, 128]); identb_b = Buf("identb", const=True)
        identf = A.alloc(F32, [128, 128]); identf_b = Buf("identf", const=True)
        P.dma("sp", [(identb, identb_d)], identb_b, w=[identb_b])
        P.dma("sp", [(identf, identf_d)], identf_b, w=[identf_b])
        epsc = A.alloc(F32, [128, 4]); epsc_b = Buf("epsc", const=True)
        P.add("pool", I("memset", epsc, EPS), w=[epsc_b])
        pers_top = A.top

        FZ = dscr("s_fence", [2, 16], F32)

        def fence():
            P.fence([(FZ[1:2, :], identf_d[0:1, 0:16])])
            A.top = pers_top

        def rms_tile(xt, xt_b, gb, gb_b, xn, xn_b, junk, junk_b, ss, ss_b):
            P.add("act", I("activation", out=junk, in_=xt, func=AF.Square, accum_out=ss[:, 0:1]),
                  r=[xt_b], w=[junk_b, ss_b])
            P.add("act", I("activation", out=ss[:, 1:2], in_=ss[:, 0:1], func=AF.Ln, scale=1.0 / 1024, bias=epsc[:, 0:1]),
                  r=[ss_b, epsc_b], w=[ss_b])
            P.add("act", I("activation", out=ss[:, 2:3], in_=ss[:, 1:2], func=AF.Exp, scale=-0.5), r=[ss_b], w=[ss_b])
            P.add("dve", I("scalar_tensor_tensor", out=xn, in0=xt, scalar=ss[:, 2:3], in1=gb,
                                                          op0=ALU.mult, op1=ALU.mult), r=[xt_b, ss_b, gb_b], w=[xn_b])

        def transpose8(src, src_b, dst_fn, dst_b, bank, evac_eng):
            pv = ps_bf16(bank)
            for do in range(8):
                P.add("pe", I("transpose", out=pv[:, do * 128:(do + 1) * 128],
                                                         in_=src[:, do * 128:(do + 1) * 128], identity=identb),
                      r=[src_b, identb_b], w=[PSB[bank]])
            pv3 = pv[:, 0:1024].rearrange("p (a b) -> p a b", b=128)
            if evac_eng == "act":
                P.add("act", I("copy", out=dst_fn, in_=pv3), r=[PSB[bank]], w=[dst_b])
            else:
                P.add("dve", I("tensor_copy", out=dst_fn, in_=pv3), r=[PSB[bank]], w=[dst_b])

        def uv_prepass_chunk(c, uvr, ubr):
            eu3 = eu_d.rearrange("(a b) d -> a b d", b=128)
            ev3 = ev_d.rearrange("(a b) d -> a b d", b=128)
            ub, ub_b = ubr.next()
            P.dma("pool", [(ub, eu3[:, c, :])], ub_b, w=[ub_b])
            uv, uv_b = uvr.next()
            P.dma("pool", [(uv[:, 1024:2048], ev3[:, c, :])], uv_b, w=[uv_b])
            bank = 6 + c % 2
            pv = ps_bf16(bank)
            for do in range(8):
                P.add("pe", I("transpose", out=pv[:, do * 128:(do + 1) * 128], in_=ub[:, do * 128:(do + 1) * 128],
                               identity=identb), r=[ub_b, identb_b], w=[PSB[bank]])
            P.add("dve", I("tensor_copy", out=uv[:, 0:1024], in_=pv[:, 0:1024]), r=[PSB[bank], uv_b], w=[uv_b])
            store("sp", [(UV[c], uv)], r=[uv_b])

        def phase1():
            uvr1 = Ring(A, "uv1", 3, BF16, [128, 2048])
            ubr1 = Ring(A, "ub1", 2, BF16, [128, 1024])
            uvc = [0]
            g1 = A.alloc(F32, [128, 1024]); g1_b = Buf("g1", const=True)
            P.dma("sp", [(g1, bcast_rows(norm1_d, 1024))], g1_b, w=[g1_b])
            xr = Ring(A, "p1x", 3, F32, [128, 1024])
            xnr = Ring(A, "p1xn", 2, BF16, [128, 1024])
            jr = Ring(A, "p1j", 1, BF16, [128, 1024])
            ssr = Ring(A, "p1ss", 4, F32, [128, 4])
            gr = Ring(A, "p1g", 2, BF16, [128, 8, 512])
            nt = TOK // 128
            loads = {}

            def load(i):
                xt, xt_b = xr.next()
                P.dma("sp", [(xt, x_d[i * 128:(i + 1) * 128, :])], xt_b, w=[xt_b])
                loads[i] = (xt, xt_b)

            load(0)
            if nt > 1:
                load(1)
            grp = None
            for i in range(nt):
                if i + 2 < nt:
                    load(i + 2)
                xt, xt_b = loads.pop(i)
                xn, xn_b = xnr.next()
                junk, junk_b = jr.next()
                ss, ss_b = ssr.next()
                rms_tile(xt, xt_b, g1, g1_b, xn, xn_b, junk, junk_b, ss, ss_b)
                j = i % 4
                if j == 0:
                    grp = gr.next()
                transpose8(xn, xn_b, grp[0][:, :, j * 128:(j + 1) * 128], grp[1], i % 2, "act")
                if j == 3:
                    t0 = (i - 3) * 128
                    store("sp", [(XNT[:, :, t0:t0 + 512].rearrange("o i t -> i o t"), grp[0])], r=[grp[1]])
                while uvc[0] < NCH and uvc[0] * nt < (i + 1) * NCH:
                    uv_prepass_chunk(uvc[0], uvr1, ubr1)
                    uvc[0] += 1

        def qk_norm_rope(ps_ap, ps_b, H, g_ap, g_b, tab, tab_b, out_bf, out_b, work):
            sq, sq_b, st, st_b, xn, xn_b, tmp, tmp_b = work
            H64 = H * 64
            P.add("act", I("activation", out=sq[:, 0:H64], in_=ps_ap, func=AF.Square), r=[ps_b], w=[sq_b])
            yield
            sq3 = sq[:, 0:H64].rearrange("p (h d) -> p h d", d=64)
            P.add("dve", I("tensor_reduce", out=st[:, 0:H], in_=sq3, axis=AX.X, op=ALU.add), r=[sq_b], w=[st_b])
            yield
            P.add("act", I("activation", out=st[:, 4:4 + H], in_=st[:, 0:H], func=AF.Ln, scale=1.0 / 64, bias=epsc[:, 0:1]),
                  r=[st_b, epsc_b], w=[st_b])
            yield
            P.add("act", I("activation", out=st[:, 8:8 + H], in_=st[:, 4:4 + H], func=AF.Exp, scale=-0.5), r=[st_b], w=[st_b])
            yield
            ps3 = ps_ap.rearrange("p (h d) -> p h d", d=64)
            xn3 = xn[:, 0:H64].rearrange("p (h d) -> p h d", d=64)
            rs_bc = st[:, 8:8 + H].unsqueeze(2).to_broadcast([128, H, 64])
            P.add("dve", I("tensor_tensor", out=xn3, in0=ps3, in1=rs_bc, op=ALU.mult), r=[ps_b, st_b], w=[xn_b])
            yield
            g_bc = g_ap.unsqueeze(1).to_broadcast([128, H, 64])
            P.add("dve", I("tensor_tensor", out=xn3, in0=xn3, in1=g_bc, op=ALU.mult), r=[xn_b, g_b], w=[xn_b])
            yield
            P.add("act", I("copy", out=out_bf, in_=xn3), r=[xn_b], w=[out_b])
            yield
            cos_bc = tab[:, 0:8].unsqueeze(1).to_broadcast([128, H, 8])
            sin_bc = tab[:, 8:16].unsqueeze(1).to_broadcast([128, H, 8])
            x1 = xn3[:, :, 0:8]
            x2 = xn3[:, :, 8:16]
            t3 = tmp[:, 0:4 * H * 8].rearrange("p (k h d) -> p k h d", k=4, d=8)
            P.add("dve", I("tensor_tensor", out=t3[:, 0], in0=x1, in1=cos_bc, op=ALU.mult), r=[xn_b, tab_b], w=[tmp_b])
            yield
            P.add("dve", I("tensor_tensor", out=t3[:, 1], in0=x2, in1=sin_bc, op=ALU.mult), r=[xn_b, tab_b], w=[tmp_b])
            yield
            P.add("pool", I("tensor_tensor", out=t3[:, 2], in0=x2, in1=cos_bc, op=ALU.mult), r=[xn_b, tab_b], w=[tmp_b])
            yield
            P.add("pool", I("tensor_tensor", out=t3[:, 3], in0=x1, in1=sin_bc, op=ALU.mult), r=[xn_b, tab_b], w=[tmp_b])
            yield
            P.add("dve", I("tensor_tensor", out=out_bf[:, :, 0:8], in0=t3[:, 0], in1=t3[:, 1], op=ALU.subtract),
                  r=[tmp_b, out_b], w=[out_b])
            yield
            P.add("dve", I("tensor_tensor", out=out_bf[:, :, 8:16], in0=t3[:, 2], in1=t3[:, 3], op=ALU.add),
                  r=[tmp_b, out_b], w=[out_b])
            yield

        def phase2():
            wqkv = A.alloc(BF16, [128, 8, 3072]); wqkv_b = Buf("wqkv", const=True)
            w3 = w_in_d.rearrange("(o i) c -> i o c", i=128)
            P.dma("pool", [(wqkv[:, do, :], w3[:, do, 0:3072]) for do in range(8)], wqkv_b, w=[wqkv_b])
            gains = A.alloc(F32, [128, 4, 64]); gains_b = Buf("gains", const=True)
            P.dma("sp", [(gains[:, 0, :], bcast_rows(qna_d, 64)), (gains[:, 1, :], bcast_rows(kna_d, 64)),
                         (gains[:, 2, :], bcast_rows(qnb_d, 64)), (gains[:, 3, :], bcast_rows(knb_d, 64))],
                  gains_b, w=[gains_b])
            P.add("act", I("mul", out=gains[:, 0, :], in_=gains[:, 0, :], mul=0.125), r=[gains_b], w=[gains_b])
            P.add("act", I("mul", out=gains[:, 2, :], in_=gains[:, 2, :], mul=0.125), r=[gains_b], w=[gains_b])
            masks = A.alloc(BF16, [128, 2, 384]); masks_b = Buf("masks", const=True)
            P.dma("sp", [(masks[:, 0, :], mask_d[0]), (masks[:, 1, :], mask_d[1])], masks_b, w=[masks_b])
            SMAX = max(seqs)
            xnT = A.alloc(BF16, [128, 8, SMAX]); xnT_b = Buf("xnT")
            NTMAX = SMAX // 128
            KT = A.alloc(BF16, [128, 2, SMAX])
            KT_b = [Buf(f"KT{a}") for a in range(NTMAX)]
            VA = A.alloc(BF16, [128, NTMAX, 4, 65])
            VA_b = [Buf(f"VA{a}") for a in range(NTMAX)]
            VA1_b = Buf("VAones")
            P.add("pool", I("memset", VA[:, :, :, 64:65], 1.0), w=[VA1_b] + VA_b)
            tabr = Ring(A, "tab", 6, F32, [128, 16])
            sqr = Ring(A, "sq", 4, F32, [128, 256])
            str_ = Ring(A, "st", 4, F32, [128, 12])
            xnr = Ring(A, "xnq", 4, F32, [128, 256])
            tmpr = Ring(A, "tmpq", 4, F32, [128, 128])
            kbr = Ring(A, "kb", 4, BF16, [128, 4, 64])
            qbr = Ring(A, "qb", 4, BF16, [128, 4, 64])
            QTr = Ring(A, "QT", 4, BF16, [128, 2, 128])
            ptr = Ring(A, "pt", 6, BF16, [128, 384])
            numr = Ring(A, "num", 4, F32, [128, 260])
            SBANK = [dict(proj=0, tr=2, sc=3, num=5), dict(proj=1, tr=7, sc=4, num=6)]

            def work():
                sq, sq_b = sqr.next(); st, st_b = str_.next(); xn, xn_b = xnr.next(); tmp, tmp_b = tmpr.next()
                return (sq, sq_b, st, st_b, xn, xn_b, tmp, tmp_b)

            def yield_each(n0):
                return sum(len(v) for v in P.ops.values())

            def load_tab(r, D, a):
                tab, tab_b = tabr.next()
                st0 = r + D * 128 * a
                P.dma("sp", [(tab, rope_d[st0:st0 + D * 127 + 1:D, :])], tab_b, w=[tab_b])
                return tab, tab_b

            def kv_task(g, r, a, slot, gk, sidx):
                D, HK = g["D"], g["HK"]
                isA = HK == 1
                bk = SBANK[sidx]
                tab, tab_b = load_tab(r, D, a)
                bank = bk["proj"]
                kv = ps_f32(bank)
                nk = HK * 64
                st0 = r + D * 128 * a
                ts = slice(st0, st0 + D * 127 + 1, D)
                for do in range(8):
                    P.add("pe", I("matmul", kv[:, 0:nk], xnT[:, do, ts], wqkv[:, do, g["kcol"]:g["kcol"] + nk],
                                   start=(do == 0), stop=(do == 7)), r=[xnT_b, wqkv_b], w=[PSB[bank]])
                yield
                for do in range(8):
                    P.add("pe", I("matmul", kv[:, 256:256 + nk], xnT[:, do, ts], wqkv[:, do, g["vcol"]:g["vcol"] + nk],
                                   start=(do == 0), stop=(do == 7)), r=[xnT_b, wqkv_b], w=[PSB[bank]])
                yield
                kb, kb_b = kbr.next()
                yield from qk_norm_rope(kv[:, 0:nk], PSB[bank], HK, gk, gains_b, tab, tab_b, kb[:, 0:HK, :], kb_b, work())
                if isA:
                    P.add("dve", I("tensor_copy", out=kb[:, 1, :], in_=kb[:, 0, :]), r=[kb_b], w=[kb_b])
                P.add("act", I("copy", out=VA[:, slot, 0:HK, 0:64], in_=kv[:, 256:256 + nk].rearrange("p (h d) -> p h d", d=64)),
                      r=[PSB[bank]], w=[VA_b[slot]])
                yield
                npair = 1 if isA else 2
                tb_ = bk["tr"]
                pv = ps_bf16(tb_)
                for p_ in range(npair):
                    P.add("pe", I("transpose", out=pv[:, p_ * 128:(p_ + 1) * 128],
                                   in_=kb[:, 2 * p_:2 * p_ + 2, :].rearrange("p h d -> p (h d)"), identity=identb),
                          r=[kb_b, identb_b], w=[PSB[tb_]])
                yield
                P.add("dve", I("tensor_copy", out=KT[:, 0:npair, slot * 128:(slot + 1) * 128],
                               in_=pv[:, 0:npair * 128].rearrange("p (a b) -> p a b", b=128)), r=[PSB[tb_]], w=[KT_b[slot]])
                yield

            def q_task(g, gi, s0, r, a, nt, slots, gq, mk, sidx):
                D, HK = g["D"], g["HK"]
                isA = HK == 1
                bk = SBANK[sidx]
                tab, tab_b = load_tab(r, D, a)
                bank = bk["proj"]
                qp = ps_f32(bank)
                st0 = r + D * 128 * a
                ts = slice(st0, st0 + D * 127 + 1, D)
                for do in range(8):
                    P.add("pe", I("matmul", qp[:, 0:256], xnT[:, do, ts], wqkv[:, do, g["qcol"]:g["qcol"] + 256],
                                   start=(do == 0), stop=(do == 7)), r=[xnT_b, wqkv_b], w=[PSB[bank]])
                yield
                qb, qb_b = qbr.next()
                yield from qk_norm_rope(qp[:, 0:256], PSB[bank], 4, gq, gains_b, tab, tab_b, qb, qb_b, work())
                tb_ = bk["tr"]
                pv = ps_bf16(tb_)
                for p_ in range(2):
                    P.add("pe", I("transpose", out=pv[:, p_ * 128:(p_ + 1) * 128],
                                   in_=qb[:, 2 * p_:2 * p_ + 2, :].rearrange("p h d -> p (h d)"), identity=identb),
                          r=[qb_b, identb_b], w=[PSB[tb_]])
                yield
                QT, QT_b = QTr.next()
                P.add("dve", I("tensor_copy", out=QT, in_=pv[:, 0:256].rearrange("p (a b) -> p a b", b=128)), r=[PSB[tb_]], w=[QT_b])
                yield
                blocks = [b for b in (a - 1, a, a + 1) if 0 <= b < nt]
                nb = len(blocks)
                moff = (blocks[0] - (a - 1)) * 128
                nbank = bk["num"]
                nps = ps_f32(nbank)
                sbank = bk["sc"]
                sps = ps_f32(sbank)
                for h in range(4):
                    kh = 0 if isA else h
                    bp = (h % 2) * 64
                    pair = h // 2
                    kpair = 0 if isA else kh // 2
                    for bi, b in enumerate(blocks):
                        sl_ = slots[b]
                        P.add("pe", I("matmul", sps[:, bi * 128:(bi + 1) * 128], KT[bp:bp + 64, kpair, sl_ * 128:(sl_ + 1) * 128],
                                       QT[bp:bp + 64, pair, :], start=True, stop=True), r=[KT_b[sl_], QT_b], w=[PSB[sbank]])
                    yield
                    pt, pt_b = ptr.next()
                    P.add("act", I("activation", out=pt[:, 0:nb * 128], in_=sps[:, 0:nb * 128], func=AF.Exp), r=[PSB[sbank]], w=[pt_b])
                    yield
                    P.add("dve", I("tensor_tensor", out=pt[:, 0:nb * 128], in0=pt[:, 0:nb * 128], in1=mk[:, moff:moff + nb * 128],
                                   op=ALU.mult), r=[pt_b, masks_b], w=[pt_b])
                    yield
                    for bi, b in enumerate(blocks):
                        sl_ = slots[b]
                        P.add("pe", I("matmul", nps[:, h * 65:(h + 1) * 65], pt[:, bi * 128:(bi + 1) * 128], VA[:, sl_, kh, :],
                                       start=(bi == 0), stop=(bi == nb - 1)), r=[pt_b, VA_b[sl_], VA1_b], w=[PSB[nbank]])
                    yield
                num, num_b = numr.next()
                P.add("act", I("copy", out=num, in_=nps[:, 0:260]), r=[PSB[nbank]], w=[num_b])
                yield
                st1 = s0 + r + D * 128 * a
                store("sp", [(NUM[gi, st1:st1 + D * 127 + 1:D, :], num)], r=[num_b])
                yield

            def load_task(S, s0, sidx):
                P.dma("sp", [(xnT[:, do, 0:S], XNT[do, :, s0:s0 + S]) for do in range(8)], xnT_b, w=[xnT_b])
                yield

            tasks = []
            s0 = 0
            base = 0
            for S in seqs:
                tasks.append(lambda sidx, S=S, s0=s0: load_task(S, s0, sidx))
                for gi, g in enumerate(GROUPS):
                    D, HK = g["D"], g["HK"]
                    nt = S // D // 128
                    isA = HK == 1
                    gq = gains[:, 0 if isA else 2, :]
                    gk = gains[:, 1 if isA else 3, :]
                    mk = masks[:, 0 if isA else 1, :]
                    for r in range(D):
                        slots = [(base + a) % NTMAX for a in range(nt)]
                        base += nt
                        for a in range(nt):
                            tasks.append(lambda sidx, g=g, r=r, a=a, sl=slots[a], gk=gk: kv_task(g, r, a, sl, gk, sidx))
                        for a in range(nt):
                            tasks.append(lambda sidx, g=g, gi=gi, s0=s0, r=r, a=a, nt=nt, slots=slots, gq=gq, mk=mk:
                                         q_task(g, gi, s0, r, a, nt, slots, gq, mk, sidx))
                s0 += S
            active = {}
            it = iter(tasks)
            done = False
            while True:
                while not done and len(active) < 2:
                    t = next(it, None)
                    if t is None:
                        done = True
                        break
                    sidx = 0 if 0 not in active else 1
                    active[sidx] = t(sidx)
                if not active:
                    break
                for sidx in list(active.keys()):
                    try:
                        next(active[sidx])
                    except StopIteration:
                        del active[sidx]

        def phase3():
            def wload(name, d_ap, nchunk, c0, ncol):
                t = A.alloc(BF16, [128, nchunk, ncol]); b = Buf(name, const=True)
                v = d_ap.rearrange("(o i) c -> i o c", i=128)
                P.dma("pool", [(t[:, o, :], v[:, o, c0:c0 + ncol]) for o in range(nchunk)], b, w=[b])
                return t, b
            woa, woa_b = wload("woa", woa_d, 4, 0, 1024)
            wob, wob_b = wload("wob", wob_d, 2, 0, 1024)
            wg, wg_b = wload("wg", w_in_d, 8, 3072, 2048)
            wout, wout_b = wload("wout", wout_d, 8, 0, 1024)
            wq, wq_b = wload("wq", wq_d, 8, 0, 2048)
            g2 = A.alloc(F32, [128, 1024]); g2_b = Buf("g2", const=True)
            P.dma("sp", [(g2, bcast_rows(norm2_d, 1024))], g2_b, w=[g2_b])
            esink = A.alloc(F32, [128, 8]); esink_b = Buf("esink", const=True)
            P.dma("sp", [(esink, bcast_rows(sink_d, 8))], esink_b, w=[esink_b])
            P.add("act", I("activation", out=esink, in_=esink, func=AF.Exp), r=[esink_b], w=[esink_b])
            n5r = Ring(A, "n5", 2, F32, [128, 5, 260])
            tBr = Ring(A, "tB", 2, F32, [128, 260])
            rAr = Ring(A, "rA", 2, F32, [128, 16])
            Or = Ring(A, "O", 2, BF16, [128, 768])
            oTr = Ring(A, "oT", 1, BF16, [128, 6, 512])
            xTr = Ring(A, "xT3", 1, BF16, [128, 8, 512])
            sgr = Ring(A, "sg", 2, F32, [128, 512])
            mr = Ring(A, "m12", 2, F32, [128, 512])
            mTr = Ring(A, "mT", 1, BF16, [128, 8, 512])
            xr = Ring(A, "x3", 1, F32, [128, 1024])
            x2r = Ring(A, "x23", 2, F32, [128, 1024])
            xn2r = Ring(A, "xn23", 1, BF16, [128, 1024])
            skT = A.alloc(F32, [128, 16, 128]); skT_b = Buf("skT", const=True)
            qTs = A.alloc(F32, [128, 16, 128]); qTs_b = Buf("qTs")
            scs = A.alloc(F32, [128, 16, 128]); scs_b = Buf("scs")
            P.dma("sp", [(qTs[:, 2 * h, :], sk1_d[h]) for h in range(8)] + [(qTs[:, 2 * h + 1, :], sk2_d[h]) for h in range(8)],
                  qTs_b, w=[qTs_b])
            for rnd in range(4):
                bank = 4 + rnd % 2
                for j in range(4):
                    g_ = rnd * 4 + j
                    P.add("pe", I("transpose", out=ps_f32(bank)[:, j * 128:(j + 1) * 128], in_=qTs[:, g_, :], identity=identf),
                          r=[qTs_b, identf_b], w=[PSB[bank]])
                P.add("act", I("copy", out=skT[:, rnd * 4:(rnd + 1) * 4, :],
                                in_=ps_f32(bank).rearrange("p (a b) -> p a b", b=128)), r=[PSB[bank]], w=[skT_b])
            ev3 = [0]

            def evac3(out, in_, r, w):
                ev3[0] += 1
                if ev3[0] % 2:
                    P.add("act", I("copy", out=out, in_=in_), r=r, w=w)
                else:
                    P.add("dve", I("tensor_copy", out=out, in_=in_), r=r, w=w)
            jr = Ring(A, "j3", 1, BF16, [128, 1024])
            ssr = Ring(A, "ss3", 4, F32, [128, 4])
            gr = Ring(A, "g3", 2, BF16, [128, 8, 512])
            for T0 in range(0, TOK, 512):
                oT, oT_b = oTr.next()
                for j in range(4):
                    tok = T0 + 128 * j
                    n5, n5_b = n5r.next()
                    P.dma("sp", [(n5[:, g_, :], NUM[g_, tok:tok + 128, :]) for g_ in range(5)], n5_b, w=[n5_b])
                    tB, tB_b = tBr.next()
                    rA, rA_b = rAr.next()
                    O, O_b = Or.next()
                    P.add("dve", I("tensor_tensor", out=tB, in0=n5[:, 2, :], in1=n5[:, 3, :], op=ALU.add), r=[n5_b], w=[tB_b])
                    P.add("dve", I("tensor_tensor", out=tB, in0=tB, in1=n5[:, 4, :], op=ALU.add), r=[n5_b, tB_b], w=[tB_b])
                    A8 = n5[:, 0:2, :].rearrange("p g (h e) -> p (g h) e", e=65)
                    B4 = tB.rearrange("p (h e) -> p h e", e=65)
                    P.add("dve", I("tensor_tensor", out=rA[:, 0:8], in0=A8[:, :, 64], in1=esink, op=ALU.add),
                          r=[n5_b, esink_b], w=[rA_b])
                    P.add("dve", I("tensor_copy", out=rA[:, 8:12], in_=B4[:, :, 64]), r=[tB_b], w=[rA_b])
                    P.add("dve", I("reciprocal", out=rA[:, 0:12], in_=rA[:, 0:12]), r=[rA_b], w=[rA_b])
                    P.add("dve", I("tensor_tensor", out=O[:, 0:512].rearrange("p (h d) -> p h d", d=64), in0=A8[:, :, 0:64],
                                   in1=rA[:, 0:8].unsqueeze(2).to_broadcast([128, 8, 64]), op=ALU.mult), r=[n5_b, rA_b], w=[O_b])
                    P.add("dve", I("tensor_tensor", out=O[:, 512:768].rearrange("p (h d) -> p h d", d=64), in0=B4[:, :, 0:64],
                                   in1=rA[:, 8:12].unsqueeze(2).to_broadcast([128, 4, 64]), op=ALU.mult), r=[tB_b, rA_b, O_b], w=[O_b])
                    if debug and T0 == 0 and j == 0:
                        dbg("O", O, O_b)
                    bank = 0
                    pv = ps_bf16(bank)
                    for fc in range(6):
                        P.add("pe", I("transpose", out=pv[:, fc * 128:(fc + 1) * 128], in_=O[:, fc * 128:(fc + 1) * 128],
                                       identity=identb), r=[O_b, identb_b], w=[PSB[bank]])
                    P.add("act", I("copy", out=oT[:, :, j * 128:(j + 1) * 128],
                                    in_=pv[:, 0:768].rearrange("p (a b) -> p a b", b=128)), r=[PSB[bank]], w=[oT_b])
                xT, xT_b = xTr.next()
                P.dma("sp", [(xT[:, do, :], XNT[do, :, T0:T0 + 512]) for do in range(8)], xT_b, w=[xT_b])
                mT, mT_b = mTr.next()
                for dc in range(8):
                    bs = (dc % 2) * 4
                    dcs = slice(dc * 128, (dc + 1) * 128)
                    ya, yb, ga, gb = ps_f32(bs), ps_f32(bs + 1), ps_f32(bs + 2), ps_f32(bs + 3)
                    for fc in range(4):
                        P.add("pe", I("matmul", ya, woa[:, fc, dcs], oT[:, fc, :], start=(fc == 0), stop=(fc == 3)),
                              r=[woa_b, oT_b], w=[PSB[bs]])
                    for fc in range(2):
                        P.add("pe", I("matmul", yb, wob[:, fc, dcs], oT[:, 4 + fc, :], start=(fc == 0), stop=(fc == 1)),
                              r=[wob_b, oT_b], w=[PSB[bs + 1]])
                    for do in range(8):
                        P.add("pe", I("matmul", ga, wg[:, do, dcs], xT[:, do, :], start=(do == 0), stop=(do == 7)),
                              r=[wg_b, xT_b], w=[PSB[bs + 2]])
                    for do in range(8):
                        P.add("pe", I("matmul", gb, wg[:, do, 1024 + dc * 128:1024 + (dc + 1) * 128], xT[:, do, :],
                                       start=(do == 0), stop=(do == 7)), r=[wg_b, xT_b], w=[PSB[bs + 3]])
                    sga, sga_b = sgr.next()
                    sgb, sgb_b = sgr.next()
                    P.add("act", I("activation", out=sga, in_=ga, func=AF.Sigmoid), r=[PSB[bs + 2]], w=[sga_b])
                    P.add("act", I("activation", out=sgb, in_=gb, func=AF.Sigmoid), r=[PSB[bs + 3]], w=[sgb_b])
                    m1, m1_b = mr.next()
                    m2, m2_b = mr.next()
                    P.add("dve", I("tensor_tensor", out=m1, in0=ya, in1=sga, op=ALU.mult), r=[PSB[bs], sga_b], w=[m1_b])
                    P.add("dve", I("tensor_tensor", out=m2, in0=yb, in1=sgb, op=ALU.mult), r=[PSB[bs + 1], sgb_b], w=[m2_b])
                    P.add("pool", I("tensor_tensor", out=mT[:, dc, :], in0=m1, in1=m2, op=ALU.add), r=[m1_b, m2_b], w=[mT_b])
                grp = gr.next()
                for j in range(4):
                    tok = T0 + 128 * j
                    xt, xt_b = xr.next()
                    P.dma("sp", [(xt, x_d[tok:tok + 128, :])], xt_b, w=[xt_b])
                    x2t, x2t_b = x2r.next()
                    for hd in range(2):
                        bank = 1 + hd
                        ops_ = ps_f32(bank)
                        for dc in range(8):
                            P.add("pe", I("matmul", ops_, mT[:, dc, j * 128:(j + 1) * 128], wout[:, dc, hd * 512:(hd + 1) * 512],
                                           start=(dc == 0), stop=(dc == 7)), r=[mT_b, wout_b], w=[PSB[bank]])
                        P.add("dve", I("tensor_tensor", out=x2t[:, hd * 512:(hd + 1) * 512], in0=ops_,
                                       in1=xt[:, hd * 512:(hd + 1) * 512], op=ALU.add), r=[PSB[bank], xt_b], w=[x2t_b])
                    store("sp", [(X2[tok:tok + 128, :], x2t)], r=[x2t_b])
                    xn2, xn2_b = xn2r.next()
                    junk, junk_b = jr.next()
                    ss, ss_b = ssr.next()
                    rms_tile(x2t, x2t_b, g2, g2_b, xn2, xn2_b, junk, junk_b, ss, ss_b)
                    transpose8(xn2, xn2_b, grp[0][:, :, j * 128:(j + 1) * 128], grp[1], 3, "act")
                    for qc in range(16):
                        bank = 4 + qc % 2
                        qps = ps_f32(bank)[:, 0:128]
                        for do in range(8):
                            P.add("pe", I("matmul", qps, wq[:, do, qc * 128:(qc + 1) * 128], grp[0][:, do, j * 128:(j + 1) * 128],
                                           start=(do == 0), stop=(do == 7)), r=[wq_b, grp[1]], w=[PSB[bank]])
                        evac3(qTs[:, qc, :], qps, [PSB[bank]], [qTs_b])
                    for rnd in range(4):
                        bank = 6 + rnd % 2
                        sps = ps_f32(bank)
                        for jj in range(4):
                            g_ = rnd * 4 + jj
                            P.add("pe", I("matmul", sps[:, jj * 128:(jj + 1) * 128], qTs[:, g_, :], skT[:, g_, :], start=True, stop=True),
                                  r=[qTs_b, skT_b], w=[PSB[bank]])
                        evac3(scs[:, rnd * 4:(rnd + 1) * 4, :], sps.rearrange("p (a b) -> p a b", b=128), [PSB[bank]], [scs_b])
                    store("sp", [(SC[tok:tok + 128, :], scs.rearrange("p a b -> p (a b)"))], r=[scs_b])
                store("sp", [(XN2T[:, :, T0:T0 + 512].rearrange("o i t -> i o t"), grp[0])], r=[grp[1]])


        def phase4():
            PSBH = [Buf(f"psbh{i}") for i in range(4)]
            iota = A.alloc(F32, [128, 128]); iota_b = Buf("iota", const=True)
            P.dma("sp", [(iota, iota_d)], iota_b, w=[iota_b])
            Gr = Ring(A, "G", 2, BF16, [128, PT, 128])
            uvr = Ring(A, "uv", UVD, BF16, [128, 2048])
            xqr = Ring(A, "xq", 2, BF16, [128, 8, PT])
            sc = A.alloc(F32, [128, 16, 128]); sc_b = Buf("sc")
            sc2 = A.alloc(F32, [128, 16, 128]); sc2_b = Buf("sc2")
            cand = sc.rearrange("p a b -> p (a b)").rearrange("p (h c) -> p h c", c=256)
            cand2 = sc2.rearrange("p a b -> p (a b)").rearrange("p (h c) -> p h c", c=256)
            oh = sc.rearrange("p a b -> p (a b)").rearrange("p (h s i) -> p h s i", s=16, i=16)
            cand4 = oh
            vt = A.alloc(F32, [128, 16, 16]); vt_b = Buf("vt")
            ix = A.alloc(U32, [128, 16, 16]); ix_b = Buf("ix")
            ixf = A.alloc(F32, [128, 16, 16]); ixf_b = Buf("ixf")
            top = A.alloc(F32, [128, 8, 16]); top_b = Buf("top")
            ci = A.alloc(U32, [128, 8, 16]); ci_b = Buf("ci")
            hl = A.alloc(U32, [128, 2, 128]); hl_b = Buf("hl")
            hlf = A.alloc(F32, [128, 2, 128]); hlf_b = Buf("hlf")
            ex = A.alloc(F32, [128, 8, 16]); ex_b = Buf("ex")
            sm = A.alloc(F32, [128, 8]); sm_b = Buf("sm")
            e12w = A.alloc(F32, [128, 3, 128]); e12w_b = Buf("e12w")
            eTr = Ring(A, "eT", 1, F32, [128, 3, PT])
            scl = A.alloc(F32, [128, 16, 128]); scl_b = Buf("scl")
            A1r = Ring(A, "A1", 2, BF16, [128, GB, 128])
            B1r = Ring(A, "B1", 2, BF16, [128, GB, 128])
            gelr = Ring(A, "gel", 3, BF16, [128, PT])
            atr = Ring(A, "at", 3, BF16, [128, PT])
            x2r = Ring(A, "x24", 1, F32, [128, 1024])
            ev_i = [0]

            def evac(out, in_, r, w):
                ev_i[0] += 1
                if ev_i[0] % 2:
                    P.add("act", I("copy", out=out, in_=in_), r=r, w=w)
                else:
                    P.add("dve", I("tensor_copy", out=out, in_=in_), r=r, w=w)

            def prep(T0, u, xq, xq_b, eT, eT_b):
                us = slice(u * 128, (u + 1) * 128)
                for g_ in range(16):
                    P.add("dve", I("max", out=vt[:, g_, 0:8], in_=scl[:, g_, :]), r=[scl_b], w=[vt_b])
                    P.add("dve", I("max_index", out=ix[:, g_, 0:8], in_max=vt[:, g_, 0:8], in_values=scl[:, g_, :]),
                          r=[scl_b, vt_b], w=[ix_b])
                    P.add("dve", I("match_replace", out=sc2[:, g_, :], in_to_replace=vt[:, g_, 0:8], in_values=scl[:, g_, :],
                                   imm_value=-1e30), r=[scl_b, vt_b], w=[sc2_b])
                    P.add("dve", I("max", out=vt[:, g_, 8:16], in_=sc2[:, g_, :]), r=[sc2_b], w=[vt_b])
                    P.add("dve", I("max_index", out=ix[:, g_, 8:16], in_max=vt[:, g_, 8:16], in_values=sc2[:, g_, :]),
                          r=[sc2_b, vt_b], w=[ix_b])
                    yield
                if u == 0:
                    P.dma("pool", [(scl.rearrange("p a b -> p (a b)"), SC[T0 + 128:T0 + 256, :])], scl_b, w=[scl_b])
                v4 = vt.rearrange("p (h two) s -> p h two s", two=2)
                P.add("dve", I("tensor_tensor", out=cand4, in0=v4[:, :, 0, :].unsqueeze(3).to_broadcast([128, 8, 16, 16]),
                               in1=v4[:, :, 1, :].unsqueeze(2).to_broadcast([128, 8, 16, 16]), op=ALU.add), r=[vt_b], w=[sc_b])
                for h in range(8):
                    P.add("dve", I("max", out=top[:, h, 0:8], in_=cand[:, h, :]), r=[sc_b], w=[top_b])
                    P.add("dve", I("max_index", out=ci[:, h, 0:8], in_max=top[:, h, 0:8], in_values=cand[:, h, :]),
                          r=[sc_b, top_b], w=[ci_b])
                    P.add("dve", I("match_replace", out=cand2[:, h, :], in_to_replace=top[:, h, 0:8], in_values=cand[:, h, :],
                                   imm_value=-1e30), r=[sc_b, top_b], w=[sc2_b])
                    P.add("dve", I("max", out=top[:, h, 8:16], in_=cand2[:, h, :]), r=[sc2_b], w=[top_b])
                    P.add("dve", I("max_index", out=ci[:, h, 8:16], in_max=top[:, h, 8:16], in_values=cand2[:, h, :]),
                          r=[sc2_b, top_b], w=[ci_b])
                    yield
                P.add("dve", I("tensor_tensor", out=ex, in0=top, in1=top[:, :, 0:1].to_broadcast([128, 8, 16]), op=ALU.subtract),
                      r=[top_b], w=[ex_b])
                P.add("act", I("activation", out=ex, in_=ex, func=AF.Exp), r=[ex_b], w=[ex_b])
                P.add("dve", I("tensor_reduce", out=sm, in_=ex, axis=AX.X, op=ALU.add), r=[ex_b], w=[sm_b])
                P.add("dve", I("reciprocal", out=sm, in_=sm), r=[sm_b], w=[sm_b])
                P.add("dve", I("tensor_tensor", out=e12w[:, 2, :].rearrange("p (h s) -> p h s", s=16), in0=ex,
                               in1=sm.unsqueeze(2).to_broadcast([128, 8, 16]), op=ALU.mult), r=[ex_b, sm_b], w=[e12w_b])
                cif = ci.rearrange("p h s -> p (h s)")
                P.add("dve", I("tensor_single_scalar", out=hl[:, 0, :], in_=cif, scalar=4, op=ALU.logical_shift_right),
                      r=[ci_b], w=[hl_b])
                P.add("dve", I("tensor_single_scalar", out=hl[:, 1, :], in_=cif, scalar=15, op=ALU.bitwise_and),
                      r=[ci_b], w=[hl_b])
                P.add("dve", I("tensor_copy", out=hlf, in_=hl), r=[hl_b], w=[hlf_b])
                P.add("dve", I("tensor_copy", out=ixf, in_=ix), r=[ix_b], w=[ixf_b])
                yield
                ixf4 = ixf.rearrange("p (h two) s -> p h two s", two=2)
                io16 = iota[:, 0:16].unsqueeze(1).unsqueeze(1).to_broadcast([128, 8, 16, 16])
                for k in range(2):
                    sel = hlf[:, k, :].rearrange("p (h s) -> p h s", s=16).unsqueeze(3).to_broadcast([128, 8, 16, 16])
                    P.add("dve", I("tensor_tensor", out=oh, in0=sel, in1=io16, op=ALU.is_equal), r=[hlf_b, iota_b], w=[sc_b])
                    P.add("dve", I("tensor_tensor", out=oh, in0=oh, in1=ixf4[:, :, k, :].unsqueeze(2).to_broadcast([128, 8, 16, 16]),
                                   op=ALU.mult), r=[sc_b, ixf_b], w=[sc_b])
                    P.add("dve", I("tensor_reduce", out=e12w[:, k, :].rearrange("p (h s) -> p h s", s=16), in_=oh, axis=AX.X,
                                   op=ALU.add), r=[sc_b], w=[e12w_b])
                    yield
                if debug and T0 == 0 and u == 0:
                    dbg("e12w", e12w, e12w_b)
                    dbg("top", top, top_b)
                bank = 7
                for k in range(3):
                    P.add("pe", I("transpose", out=ps_f32(bank)[:, k * 128:(k + 1) * 128], in_=e12w[:, k, :], identity=identf),
                          r=[e12w_b, identf_b], w=[PSB[bank]])
                P.add("act", I("copy", out=eT[:, :, us], in_=ps_f32(bank)[:, 0:384].rearrange("p (a b) -> p a b", b=128)),
                      r=[PSB[bank]], w=[eT_b])
                yield

            gb_i = [0]
            iota_bf = A.alloc(BF16, [128, 128]); iota_bf_b = Buf("iota_bf", const=True)
            P.add("dve", I("tensor_copy", out=iota_bf, in_=iota), r=[iota_b], w=[iota_bf_b])

            def gbuild(eT, eT_b, G_all, G_b):
                nb_ = PT // GB
                stageA = {}

                def sA(bt):
                    t0 = bt * GB
                    A1, A1_b = A1r.next()
                    B1, B1_b = B1r.next()
                    for t in range(GB):
                        tt = t0 + t
                        P.add("dve", I("tensor_scalar", out=A1[:, t, :], in0=iota_bf, scalar1=eT[:, 0, tt:tt + 1], scalar2=None,
                                       op0=ALU.is_equal), r=[iota_bf_b, eT_b], w=[A1_b])
                        P.add("dve", I("tensor_scalar", out=B1[:, t, :], in0=iota_bf, scalar1=eT[:, 1, tt:tt + 1],
                                       scalar2=eT[:, 2, tt:tt + 1], op0=ALU.is_equal, op1=ALU.mult), r=[iota_bf_b, eT_b], w=[B1_b])
                        yield
                    stageA[bt] = (A1, A1_b, B1, B1_b)

                def sB(bt):
                    t0 = bt * GB
                    A1, A1_b, B1, B1_b = stageA.pop(bt)
                    for q4 in range(GB // 4):
                        bank = 7
                        for tt in range(4):
                            t = q4 * 4 + tt
                            P.add("pe", I("matmul", ps_f32(bank)[:, tt * 128:(tt + 1) * 128], A1[:, t, :], B1[:, t, :],
                                           start=True, stop=True), r=[A1_b, B1_b], w=[PSB[bank]])
                        tb = t0 + q4 * 4
                        P.add("act", I("copy", out=G_all[:, tb:tb + 4, :], in_=ps_f32(bank).rearrange("p (t k) -> p t k", k=128)),
                              r=[PSB[bank]], w=[G_b])
                        yield

                yield from sA(0)
                for bt in range(nb_):
                    if bt + 1 < nb_:
                        yield from sA(bt + 1)
                    yield from sB(bt)

            uvq = []

            def dense(T0, xq, xq_b, G_all, G_b, filler, last_tile):
                nfill = [0]
                x2pre = []
                uvs = {}
                for c in range(NCH):
                    if uvq:
                        uvs[c] = uvq.pop(0)
                    else:
                        break

                def load(c):
                    uv, uv_b = uvr.next()
                    P.dma("sp", [(uv, UV[c % NCH])], uv_b, w=[uv_b])
                    if c < NCH:
                        uvs[c] = (uv, uv_b)
                    elif not last_tile:
                        uvq.append((uv, uv_b))

                ats = {}

                def H(c):
                    uv, uv_b = uvs[c]
                    bank = 4 + c % 3
                    hps = ps_f32(bank)[:, 0:PT]
                    for do in range(8):
                        P.add("pe", I("matmul", hps, uv[:, do * 128:(do + 1) * 128], xq[:, do, :], start=(do == 0), stop=(do == 7)),
                              r=[uv_b, xq_b], w=[PSB[bank]])
                    gel, gel_b = gelr.next()
                    at, at_b = atr.next()
                    P.add("act", I("activation", out=gel, in_=hps, func=AF.Gelu), r=[PSB[bank]], w=[gel_b])
                    P.add("dve", I("tensor_tensor", out=at, in0=gel, in1=G_all[:, :, c], op=ALU.mult),
                          r=[gel_b, G_b], w=[at_b])
                    ats[c] = (at, at_b)

                def V(c):
                    uv, uv_b = uvs.pop(c)
                    at, at_b = ats.pop(c)
                    for u in range(2):
                        for hd in range(2):
                            bk = u * 2 + hd
                            P.add("pe", I("matmul", ps_f32(bk), at[:, u * 128:(u + 1) * 128], uv[:, 1024 + hd * 512:1024 + (hd + 1) * 512],
                                           start=(c == 0), stop=(c == NCH - 1)), r=[at_b, uv_b], w=[PSB[bk]])

                for c in range(len(uvs), UVD):
                    load(c)
                H(0); H(1)
                for c in range(NCH):
                    if c + 2 < NCH:
                        H(c + 2)
                    V(c)
                    if c + UVD < NCH or not last_tile:
                        load(c + UVD)
                    if c == NCH // 2:
                        x2pre.append(x2r.next())
                        P.dma("sp", [(x2pre[0][0], X2[T0:T0 + 128, :])], x2pre[0][1], w=[x2pre[0][1]])
                    if filler is not None and c >= 2:
                        for _ in range(FILL_PER_CHUNK):
                            if next(filler, "end") == "end":
                                filler = None
                                break
                if filler is not None:
                    for _ in filler:
                        pass
                for u in range(2):
                    tok = T0 + u * 128
                    if u == 0:
                        x2t, x2t_b = x2pre.pop(0)
                    else:
                        x2t, x2t_b = x2r.next()
                        P.dma("sp", [(x2t, X2[tok:tok + 128, :])], x2t_b, w=[x2t_b])
                    for hd in range(2):
                        bk = u * 2 + hd
                        P.add("dve", I("tensor_tensor", out=x2t[:, hd * 512:(hd + 1) * 512], in0=ps_f32(bk),
                                       in1=x2t[:, hd * 512:(hd + 1) * 512], op=ALU.add), r=[PSB[bk], x2t_b], w=[x2t_b])
                    store("sp", [(y_d[tok:tok + 128, :], x2t)], r=[x2t_b])

            def prep_tile(T0):
                xq, xq_b = xqr.next()
                eT, eT_b = eTr.next()
                P.dma("sp", [(xq[:, do, :], XN2T[do, :, T0:T0 + PT]) for do in range(8)], xq_b, w=[xq_b])
                P.dma("pool", [(scl.rearrange("p a b -> p (a b)"), SC[T0:T0 + 128, :])], scl_b, w=[scl_b])
                st = dict(xq=xq, xq_b=xq_b, eT=eT, eT_b=eT_b)

                def gen():
                    for u in range(2):
                        yield from prep(T0, u, xq, xq_b, eT, eT_b)
                return st, gen()

            tiles = list(range(0, TOK, PT))

            def tile_gen(T0):
                st, g = prep_tile(T0)
                G_all, G_b = Gr.next()
                st["G"] = G_all
                st["G_b"] = G_b

                def gen():
                    yield from g
                    yield from gbuild(st["eT"], st["eT_b"], G_all, G_b)
                return st, gen()

            st, g0 = tile_gen(tiles[0])
            for _ in g0:
                pass
            for i, T0 in enumerate(tiles):
                if debug and i == 0:
                    dbg("eT", st["eT"], st["eT_b"])
                    dbg("G", st["G"][:, 0:8, :], st["G_b"])
                if i + 1 < len(tiles):
                    st2, g2 = tile_gen(tiles[i + 1])
                else:
                    st2, g2 = None, None
                dense(T0, st["xq"], st["xq_b"], st["G"], st["G_b"], g2, i + 1 == len(tiles))
                st = st2

        phase1()
        if stop_after >= 2:
            fence()
            phase2()
        if stop_after >= 3:
            fence()
            phase3()
        if stop_after >= 4:
            fence()
            phase4()
        P.emit()
    global LAST_PROG
    LAST_PROG = P
    return nc


def host_consts():
    half = 8
    inv = 500000.0 ** (-np.arange(0, 16, 2, dtype=np.float32) / 16)
    ang = np.arange(4096, dtype=np.float32)[:, None] * inv[None, :].astype(np.float32)
    rope = np.concatenate([np.cos(ang), np.sin(ang)], axis=1).astype(np.float32)
    j = np.arange(128)[:, None]
    i = np.arange(128)[None, :]
    m128 = np.concatenate([(j >= i), np.ones((128, 128), bool), (j <= i)], axis=1)
    m64 = np.concatenate([(j - i >= 64), (np.abs(i - j) <= 64), (i - j >= 64)], axis=1)
    masks = np.stack([m128, m64]).astype(np.float32).astype(ml_dtypes.bfloat16)
    return dict(
        c_rope=rope, c_mask=masks,
        c_identb=np.eye(128, dtype=np.float32).astype(ml_dtypes.bfloat16),
        c_identf=np.eye(128, dtype=np.float32),
        c_iota=np.tile(np.arange(128, dtype=np.float32)[None, :], (128, 1)),
    )


_PROG_CACHE = {}


def kernel(x_prompt, x_sample, norm1, w_in, q_norm_a, k_norm_a, sink_a, q_norm_b, k_norm_b,
           w_o_a, w_o_b, w_out, norm2, w_query, sub_keys_1, sub_keys_2, expert_u, expert_v):
    f = lambda a: np.ascontiguousarray(np.asarray(a, dtype=np.float32))
    x_prompt = f(x_prompt)
    x_sample = f(x_sample)
    shared = dict(
        norm1=f(norm1)[0:1], w_in=f(w_in)[0], q_norm_a=f(q_norm_a)[0:1], k_norm_a=f(k_norm_a)[0:1],
        sink_a=f(sink_a)[0:1], q_norm_b=f(q_norm_b)[0:1], k_norm_b=f(k_norm_b)[0:1],
        w_o_a=f(w_o_a)[0], w_o_b=f(w_o_b)[0], w_out=f(w_out)[0], norm2=f(norm2)[0:1],
        w_query=f(w_query)[0], sub_keys_1=f(sub_keys_1)[0], sub_keys_2=f(sub_keys_2)[0],
        expert_u=f(expert_u)[0], expert_v=f(expert_v)[0],
    )
    shared.update(host_consts())
    nc = build_program(FULL_SEQS)
    in_maps = []
    for c in range(NCORES):
        xp = x_prompt[4 * c:4 * c + 4].reshape(8192, 1024)
        xs = x_sample[2 * c:2 * c + 2].reshape(8192, 1024)
        m = dict(shared)
        m["x"] = np.concatenate([xp, xs], axis=0)
        in_maps.append(m)
    res = run_bass_kernel_spmd(nc, in_maps, core_ids=list(range(NCORES)))
    yp = np.empty((32, 2048, 1024), np.float32)
    ys = np.empty((16, 4096, 1024), np.float32)
    for c in range(NCORES):
        y = np.asarray(res.results[c]["y"], dtype=np.float32)
        yp[4 * c:4 * c + 4] = y[0:8192].reshape(4, 2048, 1024)
        ys[2 * c:2 * c + 2] = y[8192:16384].reshape(2, 4096, 1024)
    return (yp, ys)
```
